# Optimizing a Trainium2 kernel written in Bass

```python
import math
import jax, jax.numpy as jnp
from jax import lax
import numpy as np

D_MODEL = 2048
BATCH = 32
SEQ = 256
DEPTH = 4
DEC_BATCH = 8
DEC_SEQ = 4096
PAST_LEN = 256

GRID_W = 64
N_EVEN = (DEPTH + 1) // 2
N_ODD = DEPTH // 2
S5_GROUP_CH = 16
S5_WIDTH = D_MODEL // 4
S5_GROUPS = S5_WIDTH // S5_GROUP_CH
S5_STATE = 64
HEAD_DIM = 128
NA_HEADS = D_MODEL // (2 * HEAD_DIM)
NA_WIDTH = NA_HEADS * HEAD_DIM
NA_WIN_R = 8
NA_WIN_C = 16
NA_QC = 16
NA_KC = 2 * NA_QC
AB_IN = S5_WIDTH + 3 * NA_WIDTH
AB_OUT = S5_WIDTH + NA_WIDTH
D_FF = 4 * D_MODEL
Q_BLOCK = 128
NORM_EPS = 1e-6
NEG_INF = -1e30

kernel_name = "hybrid_s5_natten_shortconv_diffusion_step"


def rms_norm(x, g):
    x32 = x.astype(jnp.float32)
    y = x32 * lax.rsqrt(jnp.mean(x32 * x32, axis=-1, keepdims=True) + NORM_EPS)
    return (y * g.astype(jnp.float32)).astype(x.dtype)


def ada_params(cvec, w, b):
    m = jax.nn.silu(cvec) @ w + b
    m = m.reshape(m.shape[0], 1, 6, D_MODEL)
    return tuple(m[:, :, i] for i in range(6))


def modulate(x, g, shift, scale):
    return rms_norm(x, g) * (1 + scale) + shift


def _complex_affine_combine(e1, e2):
    a1r, a1i, b1r, b1i = e1
    a2r, a2i, b2r, b2i = e2
    return (a2r * a1r - a2i * a1i,
            a2r * a1i + a2i * a1r,
            a2r * b1r - a2i * b1i + b2r,
            a2r * b1i + a2i * b1r + b2i)


def s5_scan(u, h0, lam_re, lam_im, log_dt, b_re, b_im, c_re, c_im):
    seq_len = u.shape[1]
    dt = jnp.exp(log_dt)[:, None]
    ar, ai = lam_re * dt, lam_im * dt
    decay = jnp.exp(ar)
    abar_re, abar_im = decay * jnp.cos(ai), decay * jnp.sin(ai)
    den = lam_re * lam_re + lam_im * lam_im
    nr = abar_re - 1.0
    f_re = (nr * lam_re + abar_im * lam_im) / den
    f_im = (abar_im * lam_re - nr * lam_im) / den
    bb_re = f_re[..., None] * b_re - f_im[..., None] * b_im
    bb_im = f_re[..., None] * b_im + f_im[..., None] * b_re
    bu_re = jnp.einsum('blgn,gpn->blgp', u, bb_re)
    bu_im = jnp.einsum('blgn,gpn->blgp', u, bb_im)
    a_re = jnp.broadcast_to(abar_re, (1, seq_len) + abar_re.shape)
    a_im = jnp.broadcast_to(abar_im, (1, seq_len) + abar_im.shape)
    _, _, x_re, x_im = lax.associative_scan(_complex_affine_combine, (a_re, a_im, bu_re, bu_im), axis=1)
    if h0 is not None:
        steps = jnp.arange(1, seq_len + 1, dtype=jnp.float32)[:, None, None]
        pw_mag = jnp.exp(steps * ar)
        pw_re, pw_im = pw_mag * jnp.cos(steps * ai), pw_mag * jnp.sin(steps * ai)
        h_re, h_im = h0[0][:, None], h0[1][:, None]
        x_re = x_re + pw_re * h_re - pw_im * h_im
        x_im = x_im + pw_re * h_im + pw_im * h_re
    y = jnp.einsum('blgp,gnp->blgn', x_re, c_re) - jnp.einsum('blgp,gnp->blgn', x_im, c_im)
    return y, x_re, x_im


def s5_mixer(u, h0_re, h0_im, lam_re, lam_im, log_dt, b_re, b_im, c_re, c_im, d, glu_w, glu_b):
    f32 = jnp.float32
    bsz, seq_len, _ = u.shape
    u32 = u.astype(f32).reshape(bsz, seq_len, S5_GROUPS, S5_GROUP_CH)
    y = d.astype(f32).reshape(S5_GROUPS, S5_GROUP_CH) * u32
    last_re, last_im = [], []
    for direction in range(2):
        ud = u32 if direction == 0 else u32[:, ::-1]
        h0 = None if h0_re is None else (h0_re[:, direction].astype(f32), h0_im[:, direction].astype(f32))
        yd, x_re, x_im = s5_scan(ud, h0, lam_re[direction].astype(f32), lam_im[direction].astype(f32),
                                 log_dt[direction].astype(f32), b_re[direction].astype(f32),
                                 b_im[direction].astype(f32), c_re[direction].astype(f32),
                                 c_im[direction].astype(f32))
        y = y + (yd if direction == 0 else yd[:, ::-1])
        if h0_re is None:
            last_re.append(x_re[:, -1])
            last_im.append(x_im[:, -1])
    g = jax.nn.gelu(y.reshape(bsz, seq_len, S5_WIDTH))
    out = (g * jax.nn.sigmoid(g @ glu_w.astype(f32) + glu_b.astype(f32))).astype(u.dtype)
    if h0_re is None:
        return out, jnp.stack(last_re, axis=1), jnp.stack(last_im, axis=1)
    return out


def ab_projection(h, w_in, qn_g, kn_g):
    bsz, seq_len, _ = h.shape
    proj = h @ w_in
    u = proj[..., :S5_WIDTH]
    q, k, v = jnp.split(proj[..., S5_WIDTH:], 3, axis=-1)
    q = rms_norm(q.reshape(bsz, seq_len, NA_HEADS, HEAD_DIM), qn_g)
    k = rms_norm(k.reshape(bsz, seq_len, NA_HEADS, HEAD_DIM), kn_g)
    v = v.reshape(bsz, seq_len, NA_HEADS, HEAD_DIM)
    return u, q, k, v


def context_attention(q, k, v):
    bsz, ctx_len, n_heads, hd = q.shape
    scale = hd ** -0.5
    qb = q.reshape(bsz, ctx_len // Q_BLOCK, Q_BLOCK, n_heads, hd).transpose(1, 0, 2, 3, 4)

    def block(qi):
        s = jnp.einsum('bqhd,bkhd->bhqk', qi, k, preferred_element_type=jnp.float32) * scale
        p = jax.nn.softmax(s, axis=-1).astype(v.dtype)
        return jnp.einsum('bhqk,bkhd->bqhd', p, v)

    o = lax.map(block, qb)
    return o.transpose(1, 0, 2, 3, 4).reshape(bsz, ctx_len, n_heads * hd)


def neighbourhood_attention(q, k, v, k_ctx, v_ctx, rpb):
    f32 = jnp.float32
    bsz, n_tok, n_heads, hd = q.shape
    rows = n_tok // GRID_W
    wr = min(NA_WIN_R, rows)
    n_cb = GRID_W // NA_QC
    scale = hd ** -0.5
    qg = q.reshape(bsz, rows, GRID_W, n_heads, hd)
    kg = k.reshape(bsz, rows, GRID_W, n_heads, hd)
    vg = v.reshape(bsz, rows, GRID_W, n_heads, hd)
    q_cols = np.arange(GRID_W).reshape(n_cb, NA_QC)
    c0 = np.clip(q_cols - NA_WIN_C // 2, 0, GRID_W - NA_WIN_C)
    kc0 = np.clip(np.arange(n_cb) * NA_QC - NA_WIN_C // 2, 0, GRID_W - NA_KC)
    k_cols = kc0[:, None] + np.arange(NA_KC)[None, :]
    col_ok = (k_cols[:, None, :] >= c0[:, :, None]) & (k_cols[:, None, :] < c0[:, :, None] + NA_WIN_C)
    n_loc = wr * NA_KC
    mask = np.broadcast_to(col_ok[:, None, :, None, :], (n_cb, 1, NA_QC, wr, NA_KC)).reshape(n_cb, 1, NA_QC, n_loc)
    dc_idx = np.clip(k_cols[:, None, :] - q_cols[:, :, None] + NA_WIN_C - 1, 0, 2 * NA_WIN_C - 2)

    def row_block(r):
        r0 = jnp.clip(r - wr // 2, 0, rows - wr)
        k_blk = lax.dynamic_slice_in_dim(kg, r0, wr, axis=1)[:, :, k_cols]
        v_blk = lax.dynamic_slice_in_dim(vg, r0, wr, axis=1)[:, :, k_cols]
        k_blk = k_blk.transpose(0, 2, 1, 3, 4, 5).reshape(bsz, n_cb, n_loc, n_heads, hd)
        v_blk = v_blk.transpose(0, 2, 1, 3, 4, 5).reshape(bsz, n_cb, n_loc, n_heads, hd)
        q_row = lax.dynamic_index_in_dim(qg, r, axis=1, keepdims=False).reshape(bsz, n_cb, NA_QC, n_heads, hd)
        dr_idx = r0 + jnp.arange(wr) - r + NA_WIN_R - 1
        bias = rpb[:, dr_idx[:, None, None, None], dc_idx[None]]
        bias = bias.transpose(2, 0, 3, 1, 4).reshape(n_cb, n_heads, NA_QC, n_loc).astype(f32)
        s_loc = jnp.einsum('bjqhd,bjkhd->bjhqk', q_row, k_blk, preferred_element_type=f32) * scale + bias
        s_loc = jnp.where(mask, s_loc, NEG_INF)
        s_ctx = jnp.einsum('bjqhd,bchd->bjhqc', q_row, k_ctx, preferred_element_type=f32) * scale
        p = jax.nn.softmax(jnp.concatenate([s_loc, s_ctx], axis=-1), axis=-1).astype(v.dtype)
        o = (jnp.einsum('bjhqk,bjkhd->bjqhd', p[..., :n_loc], v_blk)
             + jnp.einsum('bjhqc,bchd->bjqhd', p[..., n_loc:], v_ctx))
        return o.reshape(bsz, GRID_W, n_heads, hd)

    out = lax.map(row_block, jnp.arange(rows))
    return out.transpose(1, 0, 2, 3, 4).reshape(bsz, n_tok, n_heads * hd)


def ab_mixer_context(h, w_in, w_out, qn_g, kn_g, s5p):
    u, q, k, v = ab_projection(h, w_in, qn_g, kn_g)
    s5_out, s_re, s_im = s5_mixer(u, None, None, *s5p)
    attn = context_attention(q, k, v)
    out = jnp.concatenate([s5_out, attn], axis=-1) @ w_out
    return out, k, v, s_re, s_im


def ab_mixer_latent(h, k_ctx, v_ctx, h0_re, h0_im, w_in, w_out, qn_g, kn_g, rpb, s5p):
    u, q, k, v = ab_projection(h, w_in, qn_g, kn_g)
    s5_out = s5_mixer(u, h0_re, h0_im, *s5p)
    attn = neighbourhood_attention(q, k, v, k_ctx, v_ctx, rpb)
    return jnp.concatenate([s5_out, attn], axis=-1) @ w_out


def short_conv_mixer(h, w_in, conv_w, conv_b, w_out):
    gate_b, gate_c, xt = jnp.split(h @ w_in, 3, axis=-1)
    z = jnp.pad(gate_c * xt, ((0, 0), (1, 1), (0, 0)))
    conv = z[:, :-2] * conv_w[0] + z[:, 1:-1] * conv_w[1] + z[:, 2:] * conv_w[2] + conv_b
    return (gate_b * conv) @ w_out


def sq_relu_mlp(h, w1, w2):
    a = jax.nn.relu(h @ w1)
    return (a * a) @ w2


def setup_inputs(seed: int = 0) -> dict:
    key = jax.random.key(seed)
    ks = iter(jax.random.split(key, 40))
    f32 = jnp.float32

    def nrm(shape, scale):
        return jax.random.normal(next(ks), shape, f32) * scale

    inp = {}
    inp['x_prompt'] = nrm((BATCH, SEQ, D_MODEL), 1.0)
    inp['x_sample'] = nrm((DEC_BATCH, DEC_SEQ, D_MODEL), 1.0)
    inp['c'] = nrm((DEC_BATCH, D_MODEL), 1.0)
    inp['cache_k'] = nrm((DEC_BATCH, N_EVEN, PAST_LEN, NA_HEADS, HEAD_DIM), 1.0)
    inp['cache_v'] = nrm((DEC_BATCH, N_EVEN, PAST_LEN, NA_HEADS, HEAD_DIM), 1.0)
    inp['state_ssm_re'] = nrm((DEC_BATCH, N_EVEN, 2, S5_GROUPS, S5_STATE), 0.3)
    inp['state_ssm_im'] = nrm((DEC_BATCH, N_EVEN, 2, S5_GROUPS, S5_STATE), 0.3)
    inp['c_ctx'] = nrm((D_MODEL,), 1.0)
    inp['ada_w'] = nrm((DEPTH, D_MODEL, 6 * D_MODEL), 0.5 * D_MODEL ** -0.5)
    inp['ada_b'] = nrm((DEPTH, 6 * D_MODEL), 0.01)
    inp['norm1_g'] = 1.0 + nrm((DEPTH, D_MODEL), 0.02)
    inp['norm2_g'] = 1.0 + nrm((DEPTH, D_MODEL), 0.02)
    inp['ab_w_in'] = nrm((N_EVEN, D_MODEL, AB_IN), D_MODEL ** -0.5)
    inp['ab_w_out'] = nrm((N_EVEN, AB_OUT, D_MODEL), AB_OUT ** -0.5)
    inp['s5_lam_re'] = -0.5 + nrm((N_EVEN, 2, S5_GROUPS, S5_STATE), 0.01)
    inp['s5_lam_im'] = jnp.pi * jnp.arange(S5_STATE, dtype=f32) + nrm((N_EVEN, 2, S5_GROUPS, S5_STATE), 0.01)
    inp['s5_log_dt'] = jax.random.uniform(next(ks), (N_EVEN, 2, S5_GROUPS), f32, math.log(1e-3), math.log(1e-1))
    inp['s5_b_re'] = nrm((N_EVEN, 2, S5_GROUPS, S5_STATE, S5_GROUP_CH), (2 * S5_GROUP_CH) ** -0.5)
    inp['s5_b_im'] = nrm((N_EVEN, 2, S5_GROUPS, S5_STATE, S5_GROUP_CH), (2 * S5_GROUP_CH) ** -0.5)
    inp['s5_c_re'] = nrm((N_EVEN, 2, S5_GROUPS, S5_GROUP_CH, S5_STATE), S5_STATE ** -0.5)
    inp['s5_c_im'] = nrm((N_EVEN, 2, S5_GROUPS, S5_GROUP_CH, S5_STATE), S5_STATE ** -0.5)
    inp['s5_d'] = nrm((N_EVEN, S5_WIDTH), 1.0)
    inp['s5_glu_w'] = nrm((N_EVEN, S5_WIDTH, S5_WIDTH), S5_WIDTH ** -0.5)
    inp['s5_glu_b'] = nrm((N_EVEN, S5_WIDTH), 0.01)
    inp['q_norm_g'] = 1.0 + nrm((N_EVEN, HEAD_DIM), 0.02)
    inp['k_norm_g'] = 1.0 + nrm((N_EVEN, HEAD_DIM), 0.02)
    inp['na_rpb'] = nrm((N_EVEN, NA_HEADS, 2 * NA_WIN_R - 1, 2 * NA_WIN_C - 1), 0.1)
    inp['conv_w_in'] = nrm((N_ODD, D_MODEL, 3 * D_MODEL), D_MODEL ** -0.5)
    inp['conv_w'] = nrm((N_ODD, 3, D_MODEL), 3 ** -0.5)
    inp['conv_b'] = nrm((N_ODD, D_MODEL), 0.01)
    inp['conv_w_out'] = nrm((N_ODD, D_MODEL, D_MODEL), D_MODEL ** -0.5)
    inp['mlp_w1'] = nrm((DEPTH, D_MODEL, D_FF), D_MODEL ** -0.5)
    inp['mlp_w2'] = nrm((DEPTH, D_FF, D_MODEL), D_FF ** -0.5)
    return inp


def reference(x_prompt, x_sample, c, cache_k, cache_v, state_ssm_re, state_ssm_im, c_ctx,
              ada_w, ada_b, norm1_g, norm2_g, ab_w_in, ab_w_out,
              s5_lam_re, s5_lam_im, s5_log_dt, s5_b_re, s5_b_im, s5_c_re, s5_c_im, s5_d,
              s5_glu_w, s5_glu_b, q_norm_g, k_norm_g, na_rpb,
              conv_w_in, conv_w, conv_b, conv_w_out, mlp_w1, mlp_w2):
    xp, xs = x_prompt, x_sample
    ks, vs, srs, sis = [], [], [], []
    for layer in range(DEPTH):
        sh1p, sc1p, g1p, sh2p, sc2p, g2p = ada_params(c_ctx[None], ada_w[layer], ada_b[layer])
        sh1s, sc1s, g1s, sh2s, sc2s, g2s = ada_params(c, ada_w[layer], ada_b[layer])
        hp = modulate(xp, norm1_g[layer], sh1p, sc1p)
        hs = modulate(xs, norm1_g[layer], sh1s, sc1s)
        if layer % 2 == 0:
            e = layer // 2
            s5p = (s5_lam_re[e], s5_lam_im[e], s5_log_dt[e], s5_b_re[e], s5_b_im[e],
                   s5_c_re[e], s5_c_im[e], s5_d[e], s5_glu_w[e], s5_glu_b[e])
            yp, kp, vp, srp, sip = ab_mixer_context(hp, ab_w_in[e], ab_w_out[e], q_norm_g[e], k_norm_g[e], s5p)
            ys = ab_mixer_latent(hs, cache_k[:, e], cache_v[:, e], state_ssm_re[:, e], state_ssm_im[:, e],
                                 ab_w_in[e], ab_w_out[e], q_norm_g[e], k_norm_g[e], na_rpb[e], s5p)
            ks.append(kp)
            vs.append(vp)
            srs.append(srp)
            sis.append(sip)
        else:
            o = layer // 2
            yp = short_conv_mixer(hp, conv_w_in[o], conv_w[o], conv_b[o], conv_w_out[o])
            ys = short_conv_mixer(hs, conv_w_in[o], conv_w[o], conv_b[o], conv_w_out[o])
        xp = xp + g1p * yp
        xs = xs + g1s * ys
        xp = xp + g2p * sq_relu_mlp(modulate(xp, norm2_g[layer], sh2p, sc2p), mlp_w1[layer], mlp_w2[layer])
        xs = xs + g2s * sq_relu_mlp(modulate(xs, norm2_g[layer], sh2s, sc2s), mlp_w1[layer], mlp_w2[layer])
    new_cache_k = jnp.stack(ks, axis=1)
    new_cache_v = jnp.stack(vs, axis=1)
    new_state_ssm_re = jnp.stack(srs, axis=1)
    new_state_ssm_im = jnp.stack(sis, axis=1)
    return (xp, xs, new_cache_k, new_cache_v, new_state_ssm_re, new_state_ssm_im)
```

```python
import contextlib
import numpy as np
import concourse.bass as bass
import concourse.mybir as mybir
from concourse.bass_utils import run_bass_kernel_spmd

F32 = mybir.dt.float32
BF16 = mybir.dt.bfloat16
U8 = mybir.dt.uint8
AF = mybir.ActivationFunctionType
ALU = mybir.AluOpType

D = 2048
KC = 16
DEPTH = 4
TT = 512
NTOK = 5120
NT = NTOK // TT
NST = 8
DFF = 8192
EPS = 1e-6
ABIN = 3584
ABOUT = 1536


class Buf:
    __slots__ = ("name", "w", "r")

    def __init__(self, name):
        self.name = name
        self.w = None
        self.r = {}


class DSem:
    def __init__(self, name):
        self.name = name
        self.count = 0
        self.h = None


class Op:
    __slots__ = ("eng", "fn", "waits", "inc", "val", "dsem", "dval")

    def __init__(self, eng, fn, waits, dsem=None):
        self.eng = eng
        self.fn = fn
        self.waits = waits
        self.inc = False
        self.val = 0
        self.dsem = dsem
        self.dval = 0


ENGS = ["pe", "act", "dve", "pool", "sp"]
DBG_SKIP = set()
_I, _O = "ExternalInput", "ExternalOutput"
MODE_IO = {
    "nab": {"BIAS": _O},
    "C": {"Qsc": _I, "Ksc": _I, "Vsc": _I, "BIAS": _I, "MIX": _O},
    "A": {"XS": _I, "awib0": _I, "Usc": _O, "Qsc": _O, "Ksc": _O, "Vsc": _O},
    "S5": {"Usc": _I, "glub0": _I, "MIX": _O},
}


class Prog:
    def __init__(self):
        self.ops = {e: [] for e in ENGS}
        self.dsems = []
        self.last = {e: None for e in ENGS}

    def dsem(self, name):
        key = name.split("_")[0]
        for d in self.dsems:
            if d.name == key:
                return d
        d = DSem(key)
        self.dsems.append(d)
        return d

    def _deps(self, eng, reads, writes):
        toks = []
        for b in reads:
            if b.w is not None:
                toks.append(b.w)
        for b in writes:
            if b.w is not None:
                toks.append(b.w)
            toks.extend(b.r.values())
        out = []
        for t in toks:
            if t[0] == "E" and t[1].eng == "pe" and eng == "pe":
                continue
            out.append(t)
        return out

    def op(self, eng, fn, reads=(), writes=()):
        o = Op(eng, fn, self._deps(eng, reads, writes))
        self.ops[eng].append(o)
        self.last[eng] = o
        tok = ("E", o)
        for b in reads:
            b.r[eng] = tok
        for b in writes:
            b.w = tok
            b.r = {}
        return o

    def dma(self, eng, fn, dsem, reads=(), writes=()):
        o = Op(eng, fn, self._deps(eng, reads, writes), dsem=dsem)
        dsem.count += 1
        o.dval = 16 * dsem.count
        self.ops[eng].append(o)
        tok = ("D", dsem, o.dval)
        for b in reads:
            b.r["dma_" + dsem.name] = tok
        for b in writes:
            b.w = tok
            b.r = {}
        return o

    def barrier(self):
        toks = []
        for e in ENGS:
            if self.last[e] is not None:
                toks.append(("E", self.last[e]))
        for d in self.dsems:
            if d.count:
                toks.append(("D", d, 16 * d.count))
        for e in ENGS:
            o = Op(e, None, list(toks))
            self.ops[e].append(o)

    def finalize(self):
        for e in ENGS:
            for o in self.ops[e]:
                for t in o.waits:
                    if t[0] == "E":
                        t[1].inc = True
        for e in ENGS:
            n = 0
            for o in self.ops[e]:
                if o.inc and o.dsem is None and o.fn is not None:
                    n += 1
                    o.val = n
                elif o.inc:
                    o.val = n

    def emit(self, ename, eng, sems):
        known = {}
        for o in self.ops[ename]:
            for t in o.waits:
                if t[0] == "E":
                    p = t[1]
                    if p.dsem is not None:
                        s, v = p.dsem.h, p.dval
                        key = ("d", p.dsem.name)
                    else:
                        s, v = sems[p.eng], p.val
                        key = ("e", p.eng)
                else:
                    s, v = t[1].h, t[2]
                    key = ("d", t[1].name)
                if v <= 0:
                    continue
                if known.get(key, 0) < v:
                    eng.wait_ge(s, v)
                    known[key] = v
            if o.fn is None:
                continue
            ins = o.fn(eng)
            if o.dsem is not None:
                ins.then_inc(o.dsem.h, 16)
            elif o.inc:
                ins.then_inc(sems[ename], 1)


def build_program(n_layers=DEPTH, with_ab=True, with_s5=True, mode="full"):
    nc = bass.Bass("TRN2", target_bir_lowering=False)
    P = Prog()

    big = (mode == "full")
    io = MODE_IO.get(mode, {})

    def din(name, shape, dt=F32, heavy=False):
        if heavy and not big:
            shape = [1, 1]
        return nc.dram_tensor(name, list(shape), dt, kind="ExternalInput").ap()

    def dout(name, shape, dt=F32):
        return nc.dram_tensor(name, list(shape), dt, kind="ExternalOutput").ap()

    def dscr(name, shape, dt):
        if name in io:
            return nc.dram_tensor(name, list(shape), dt, kind=io[name]).ap()
        return nc.dram_tensor(name, list(shape), dt).ap()

    x_in = din("x_in", [NTOK, D], heavy=True)
    vecs = din("vecs", [NVROWS, 128])
    ident_in = din("ident", [128, 128])
    ada_w = din("ada_w", [DEPTH, D, 6 * D], heavy=True)
    ab_w_in = din("ab_w_in", [2, D, ABIN], heavy=True)
    ab_w_out = din("ab_w_out", [2, ABOUT, D], heavy=True)
    conv_w_in = din("conv_w_in", [2, D, 3 * D], heavy=True)
    conv_w_out = din("conv_w_out", [2, D, D], heavy=True)
    mlp_w1 = din("mlp_w1", [DEPTH, D, DFF], heavy=True)
    mlp_w2 = din("mlp_w2", [DEPTH, DFF, D], heavy=True)
    cache_k = din("cache_k", [2, 256, 8, 128])
    cache_v = din("cache_v", [2, 256, 8, 128])
    na_rpb = din("na_rpb", [2, 8, 15, 31])
    glu_w = din("glu_w", [2, 512, 512])
    s5p = din("s5p", [2, 2, 128, S5P_COLS])
    s5h0 = din("s5h0", [2, 2, 2, 128, 16])
    y_out = dout("y", [NTOK, D])
    nck = dout("nck", [4, 2, 256, 8, 128])
    ncv = dout("ncv", [4, 2, 256, 8, 128])
    nsr = dout("nsr", [4, 2, 2, 32, 64])
    nsi = dout("nsi", [4, 2, 2, 32, 64])
    XS = dscr("XS", [D, NTOK], F32)
    ZC = dscr("ZC", [D, NTOK], F32)
    GB = dscr("GB", [D, NTOK], F32)
    MIX = dscr("MIX", [ABOUT, NTOK], BF16)
    Usc = dscr("Usc", [512, NTOK], BF16)
    Qsc = dscr("Qsc", [1024, NTOK], BF16)
    Ksc = dscr("Ksc", [1024, NTOK], BF16)
    Vsc = dscr("Vsc", [NTOK, 1024], BF16)
    BIAS = dscr("BIAS", [2, NTYPES, 128, 8, 640], F32)
    glub = [dscr(f"glub{e}", [512, 512], BF16) for e in range(2)]
    w1b = [dscr(f"w1b{l}", [D, DFF], BF16) for l in range(DEPTH)]
    w2b = [dscr(f"w2b{l}", [16, 128, 64, 128], BF16) for l in range(DEPTH)]
    cwib = [dscr(f"cwib{o}", [D, 3 * D], BF16) for o in range(2)]
    cwob = [dscr(f"cwob{o}", [D, D], BF16) for o in range(2)]
    awib = [dscr(f"awib{e}", [D, ABIN], BF16) for e in range(2)]
    awob = [dscr(f"awob{e}", [ABOUT, D], BF16) for e in range(2)]

    es = contextlib.ExitStack()
    with es:
        ARENA = 206 * 1024
        arena = es.enter_context(nc.sbuf_tensor("arena", [128, ARENA], U8))
        psb = [es.enter_context(nc.psum_tensor(f"ps{i}", [128, 512], F32))[:] for i in range(8)]
        PS = [Buf(f"ps{i}") for i in range(8)]
        sems = {e: es.enter_context(nc.semaphore("s_" + e)) for e in ENGS}

        def view(off, shape, dt):
            esz = 4 if dt == F32 else 2
            n = 1
            for s_ in shape[1:]:
                n *= s_
            a = arena[0:shape[0], off:off + n * esz].bitcast(dt)
            if len(shape) == 2:
                return a
            names = " ".join(f"d{i}" for i in range(1, len(shape)))
            kw = {f"d{i}": shape[i] for i in range(1, len(shape) - 1)}
            return a.rearrange(f"p ({names}) -> p {names}", **kw)

        class Bump:
            def __init__(self, base=0):
                self.off = base

            def get(self, shape, dt):
                esz = 4 if dt == F32 else 2
                n = 1
                for s_ in shape[1:]:
                    n *= s_
                o = self.off
                self.off = (o + n * esz + 63) // 64 * 64
                assert self.off <= ARENA, ("SBUF arena overflow", self.off)
                return view(o, shape, dt)

        pers = Bump(0)
        ident = pers.get([128, 128], F32)
        identb = pers.get([128, 128], BF16)
        onesb = pers.get([128, 128], BF16)
        epsc = pers.get([128, 1], F32)
        VT = pers.get([128, NVROWS], F32)
        ADA = pers.get([128, DEPTH, 96, 2], F32)
        G1 = pers.get([128, DEPTH, 2, KC], F32)
        G2 = pers.get([128, DEPTH, 2, KC], F32)
        QKG = pers.get([128, 4], F32)
        B_const = Buf("const")
        B_ada = Buf("ada")
        PERS_END = pers.off

        d_misc = P.dsem("misc")

        ph = Bump(PERS_END)
        vstage = ph.get([128, NVROWS // 128 + 1, 128], F32)
        B_vst = Buf("vstage")
        nvt = (NVROWS + 127) // 128
        P.dma("sp", lambda e: e.dma_start(out=ident, in_=ident_in[:, :]), d_misc, writes=[B_const])
        for i in range(nvt):
            r0 = i * 128
            r1 = min(NVROWS, r0 + 128)
            dd = P.dsem(f"vst{i}")
            bi = Buf(f"vst{i}")
            P.dma("sp", lambda e, i=i, r0=r0, r1=r1: e.dma_start(out=vstage[0:r1 - r0, i, :], in_=vecs[r0:r1, :]),
                  dd, writes=[bi])
            P.op("pe", lambda e, i=i, r0=r0, r1=r1: e.transpose(out=psb[i % 2][:, 0:r1 - r0],
                                                                  in_=vstage[0:r1 - r0, i, :],
                                                                  identity=ident[0:r1 - r0, 0:r1 - r0]),
                 reads=[bi, B_const], writes=[PS[i % 2]])
            P.op("dve", lambda e, i=i, r0=r0, r1=r1: e.tensor_copy(out=VT[:, r0:r1], in_=psb[i % 2][:, 0:r1 - r0]),
                 reads=[PS[i % 2]], writes=[B_vst])
        P.op("dve", lambda e: e.tensor_copy(out=identb, in_=ident), reads=[B_const], writes=[B_const])
        P.op("dve", lambda e: e.memset(onesb, 1.0), writes=[B_const])
        P.op("dve", lambda e: e.memset(epsc, EPS), writes=[B_const])
        B_VT = B_vst

        WB = {}

        def cast2d(dst, src, rows, cols, dsm):
            piece = cols if cols <= 2048 else (2048 if cols % 2048 == 0 else 1792)
            nrp = 1024
            for r0 in range(0, rows, nrp):
                r1 = min(rows, r0 + nrp)
                s_ = src[r0:r1, :].rearrange("k (a n) -> k a n", n=piece)
                d_ = dst[r0:r1, :].rearrange("k (a n) -> k a n", n=piece)
                P.dma("pool", lambda e, s_=s_, d_=d_: e.dma_start(out=d_, in_=s_), dsm)

        cast_sems = []
        if not big:
            for l in range(n_layers):
                WB[l] = Buf(f"wb{l}")
            P.op("dve", lambda e: e.memset(ADA, 0.0), writes=[Buf("x")])
            P.op("dve", lambda e: e.memset(G1, 1.0), writes=[Buf("x")])
            P.op("dve", lambda e: e.memset(G2, 1.0), writes=[Buf("x")])
        for l in range(n_layers if big else 0):
            dsm = P.dsem(f"cast{l}")
            cast_sems.append(dsm)
            if l % 2 == 0:
                e_ = l // 2
                if with_ab:
                    cast2d(awib[e_], ab_w_in[e_], D, ABIN, dsm)
                    cast2d(awob[e_], ab_w_out[e_], ABOUT, D, dsm)
                    cast2d(glub[e_], glu_w[e_], 512, 512, dsm)
            else:
                o_ = l // 2
                cast2d(cwib[o_], conv_w_in[o_], D, 3 * D, dsm)
                cast2d(cwob[o_], conv_w_out[o_], D, D, dsm)
            cast2d(w1b[l], mlp_w1[l], D, DFF, dsm)
            for n in range(16):
                s_ = mlp_w2[l][:, n * 128:(n + 1) * 128].rearrange("(kc p) m -> p kc m", p=128)
                P.dma("pool", lambda e, s_=s_, n=n, l=l: e.dma_start(out=w2b[l][n], in_=s_), dsm)
            b = Buf(f"wb{l}")
            b.w = ("D", dsm, 16 * dsm.count)
            WB[l] = b

        sc = ph.get([128, 2, KC], F32)
        sc2 = ph.get([128, KC, 2], F32)
        B_sc = Buf("sc")
        P.op("act", lambda e: e.activation(out=sc, in_=VT[:, VOFF["cvec"]:VOFF["cvec"] + 32].rearrange(
            "p (v k) -> p v k", v=2), func=AF.Silu), reads=[B_VT], writes=[B_sc])
        P.op("dve", lambda e: e.tensor_copy(out=sc2, in_=sc.rearrange("p v k -> p k v")), reads=[B_sc], writes=[B_sc])
        NCOL = 512
        awbuf = [ph.get([128, KC, NCOL], F32) for _ in range(2)]
        B_aw = [Buf("aw0"), Buf("aw1")]
        d_aw = [P.dsem("aw0"), P.dsem("aw1")]
        nblk = 6 * D // NCOL
        it = 0
        for l in range(n_layers if big else 0):
            for b in range(nblk):
                s = it % 2
                src = ada_w[l][:, b * NCOL:(b + 1) * NCOL].rearrange("(kc p) n -> p kc n", p=128)
                P.dma("sp", lambda e, s=s, src=src: e.dma_start(out=awbuf[s], in_=src), d_aw[s], writes=[B_aw[s]])
                pb = 2 + (it % 2)

                def mm(e, s=s, b=b, pb=pb):
                    ins = None
                    for jl in range(NCOL // 128):
                        for kc in range(KC):
                            ins = e.matmul(psb[pb][:, jl * 2:jl * 2 + 2], awbuf[s][:, kc, jl * 128:(jl + 1) * 128],
                                           sc2[:, kc, :], start=(kc == 0), stop=(kc == KC - 1))
                    return ins
                P.op("pe", mm, reads=[B_aw[s], B_sc], writes=[PS[pb]])
                j0 = b * (NCOL // 128)
                for cvi in range(2):
                    P.op("dve", lambda e, l=l, j0=j0, pb=pb, cvi=cvi: e.tensor_tensor(
                        out=ADA[:, l, j0:j0 + 4, cvi], in0=psb[pb][:, 0:8].rearrange("p (j v) -> p j v", v=2)[:, :, cvi],
                        in1=VT[:, VOFF["ada_b"] + l * 96 + j0: VOFF["ada_b"] + l * 96 + j0 + 4], op=ALU.add),
                        reads=[PS[pb], B_VT], writes=[B_ada])
                it += 1
        for l in range(n_layers if big else 0):
            for cv in range(2):
                for (G, sidx, nm) in ((G1, 1, "norm1_g"), (G2, 4, "norm2_g")):
                    P.op("dve", lambda e, l=l, cv=cv, G=G, sidx=sidx, nm=nm: e.scalar_tensor_tensor(
                        out=G[:, l, cv, :], in0=ADA[:, l, sidx * 16:(sidx + 1) * 16, cv], scalar=1.0,
                        in1=VT[:, VOFF[nm] + l * 16: VOFF[nm] + (l + 1) * 16], op0=ALU.add, op1=ALU.mult),
                        reads=[B_ada, B_VT], writes=[B_ada])

        def ada_ap(l, sidx, cv, kc):
            return ADA[:, l, sidx * 16 + kc, cv:cv + 1]

        XSB = [Buf(f"xs{t}") for t in range(NT)]
        ZCB = [Buf(f"zc{t}") for t in range(NT)]
        GBB = [Buf(f"gb{t}") for t in range(NT)]
        MIXB = [Buf(f"mixd{t}") for t in range(NT)]
        xin_t = [ph.get([128, D], F32) for _ in range(2)]
        xst = [ph.get([128, KC, 128], F32) for _ in range(2)]
        B_xin = [Buf("xin0"), Buf("xin1")]
        B_xst = [Buf("xst0"), Buf("xst1")]
        d_xin = [P.dsem("xin0"), P.dsem("xin1")]
        d_xst = [P.dsem("xst0"), P.dsem("xst1")]
        for tb in range(NTOK // 128 if big else 0):
            s = tb % 2
            P.dma("sp", lambda e, s=s, tb=tb: e.dma_start(out=xin_t[s], in_=x_in[tb * 128:(tb + 1) * 128, :]),
                  d_xin[s], writes=[B_xin[s]])
            for q in range(4):
                pb = 4 + (tb * 4 + q) % 4

                def tr(e, s=s, q=q, pb=pb):
                    ins = None
                    for i in range(4):
                        kc = q * 4 + i
                        ins = e.transpose(out=psb[pb][:, i * 128:(i + 1) * 128], in_=xin_t[s][:, kc * 128:(kc + 1) * 128],
                                          identity=ident)
                    return ins
                P.op("pe", tr, reads=[B_xin[s], B_const], writes=[PS[pb]])
                eng = "act" if q % 2 == 0 else "dve"
                if eng == "act":
                    P.op("act", lambda e, s=s, q=q, pb=pb: e.copy(
                        out=xst[s][:, q * 4:(q + 1) * 4, :], in_=psb[pb].rearrange("p (a b) -> p a b", a=4)),
                        reads=[PS[pb]], writes=[B_xst[s]])
                else:
                    P.op("dve", lambda e, s=s, q=q, pb=pb: e.tensor_copy(
                        out=xst[s][:, q * 4:(q + 1) * 4, :], in_=psb[pb].rearrange("p (a b) -> p a b", a=4)),
                        reads=[PS[pb]], writes=[B_xst[s]])
            dst = XS[:, tb * 128:(tb + 1) * 128].rearrange("(kc p) t -> p kc t", p=128)
            P.dma("sp", lambda e, s=s, dst=dst: e.dma_start(out=dst, in_=xst[s]), d_xst[s],
                  reads=[B_xst[s]], writes=[XSB[tb // 4]])
        P.barrier()

        def tile_cv(t):
            return 0 if t < NST else 1

        def norm_modulate(l, which, t, xt, B_x, hbuf, B_h, sqtmp, B_sq, rstd, B_r, tmp, B_tmp, psn):
            cv = tile_cv(t)
            G = G1 if which == 1 else G2
            sidx = 0 if which == 1 else 3
            P.op("act", lambda e: e.activation(out=hbuf, in_=xt, func=AF.Square), reads=B_x, writes=[B_h])

            def mm(e):
                ins = None
                for kc in range(KC):
                    ins = e.matmul(psb[psn], onesb, hbuf[:, kc, :], start=(kc == 0), stop=(kc == KC - 1))
                return ins
            P.op("pe", mm, reads=[B_h, B_const], writes=[PS[psn]])
            P.op("act", lambda e: e.activation(out=sqtmp, in_=psb[psn], func=AF.Sqrt, bias=epsc[:, 0:1], scale=1.0 / D),
                 reads=[PS[psn], B_const], writes=[B_sq])
            P.op("dve", lambda e: e.reciprocal(out=rstd, in_=sqtmp), reads=[B_sq], writes=[B_r])
            for kc in range(KC):
                s = kc % 2
                P.op("dve", lambda e, kc=kc, s=s: e.scalar_tensor_tensor(
                    out=tmp[s], in0=xt[:, kc, :], scalar=G[:, l, cv, kc:kc + 1], in1=rstd,
                    op0=ALU.mult, op1=ALU.mult), reads=B_x + [B_r, B_ada], writes=[B_tmp[s]])
                P.op("act", lambda e, kc=kc, s=s: e.activation(
                    out=hbuf[:, kc, :], in_=tmp[s], func=AF.Identity, bias=ada_ap(l, sidx, cv, kc), scale=1.0),
                    reads=[B_tmp[s], B_ada], writes=[B_h])

        def phase_A_conv(l):
            if True:
                ph = Bump(PERS_END)
                xt2 = [ph.get([128, KC, TT], F32) for _ in range(2)]
                B_x2 = [Buf("xA0"), Buf("xA1")]
                d_x2 = [P.dsem(f"xA0_{l}"), P.dsem(f"xA1_{l}")]
                hbuf = ph.get([128, KC, TT], BF16)
                B_h = Buf("hA")
                sqtmp = ph.get([128, TT], F32)
                rstd = ph.get([128, TT], F32)
                tmp = [ph.get([128, TT], F32) for _ in range(2)]
                B_sq, B_r, B_tmp = Buf("sq"), Buf("rstd"), [Buf("tmp0"), Buf("tmp1")]
                wblk = [ph.get([128, KC, 512], BF16) for _ in range(4)]
                B_w = [Buf(f"wA{i}") for i in range(4)]
                d_w = [P.dsem(f"wA{i}_{l}") for i in range(4)]
                tA = [ph.get([128, TT], F32) for _ in range(2)]
                B_tA = [Buf("tA0"), Buf("tA1")]
                zst = [ph.get([128, 4, TT], F32) for _ in range(2)]
                B_zst = [Buf("zst0"), Buf("zst1")]
                d_zst = [P.dsem(f"zst0_{l}"), P.dsem(f"zst1_{l}")]
                wi = 0
                zi = 0
                pbi = 0
                wsrc = cwib[l // 2]

                def load_x(t):
                    s = t % 2
                    src = XS[:, t * TT:(t + 1) * TT].rearrange("(kc p) t -> p kc t", p=128)
                    P.dma("sp", lambda e, s=s, src=src: e.dma_start(out=xt2[s], in_=src), d_x2[s],
                          reads=[XSB[t]], writes=[B_x2[s]])
                load_x(0)
                for t in range(NT):
                    s = t % 2
                    if t + 1 < NT:
                        load_x(t + 1)
                    norm_modulate(l, 1, t, xt2[s], [B_x2[s]], hbuf, B_h, sqtmp, B_sq, rstd, B_r, tmp, B_tmp, 0)
                    for b in range(4):
                        ws = wi % 4
                        wi += 1
                        src = wsrc[:, b * 512:(b + 1) * 512].rearrange("(kc p) n -> p kc n", p=128)
                        P.dma("sp", lambda e, ws=ws, src=src: e.dma_start(out=wblk[ws], in_=src), d_w[ws],
                              reads=[WB[l]], writes=[B_w[ws]])
                        zs = zi % 2
                        zi += 1
                        for jl in range(4):
                            pb = 1 + pbi % 6
                            pbi += 1

                            def mm(e, ws=ws, jl=jl, pb=pb):
                                ins = None
                                for kc in range(KC):
                                    ins = e.matmul(psb[pb], wblk[ws][:, kc, jl * 128:(jl + 1) * 128], hbuf[:, kc, :],
                                                   start=(kc == 0), stop=(kc == KC - 1))
                                return ins
                            P.op("pe", mm, reads=[B_w[ws], B_h], writes=[PS[pb]])
                            P.op("act", lambda e, zs=zs, jl=jl, pb=pb: e.copy(out=zst[zs][:, jl, :], in_=psb[pb]),
                                 reads=[PS[pb]], writes=[B_zst[zs]])
                        dst = GB[b * 512:(b + 1) * 512, t * TT:(t + 1) * TT].rearrange("(c p) t -> p c t", p=128)
                        P.dma("pool", lambda e, zs=zs, dst=dst: e.dma_start(out=dst, in_=zst[zs]), d_zst[zs],
                              reads=[B_zst[zs]], writes=[GBB[t]])
                    for b in range(4):
                        wsa = wi % 4
                        wsb = (wi + 1) % 4
                        wi += 2
                        srca = wsrc[:, 2048 + b * 512:2048 + (b + 1) * 512].rearrange("(kc p) n -> p kc n", p=128)
                        srcb = wsrc[:, 4096 + b * 512:4096 + (b + 1) * 512].rearrange("(kc p) n -> p kc n", p=128)
                        P.dma("sp", lambda e, ws=wsa, src=srca: e.dma_start(out=wblk[ws], in_=src), d_w[wsa],
                              reads=[WB[l]], writes=[B_w[wsa]])
                        P.dma("sp", lambda e, ws=wsb, src=srcb: e.dma_start(out=wblk[ws], in_=src), d_w[wsb],
                              reads=[WB[l]], writes=[B_w[wsb]])
                        zs = zi % 2
                        zi += 1
                        for jl in range(4):
                            pa = 1 + pbi % 6
                            pbb = 1 + (pbi + 1) % 6
                            pbi += 2

                            def mma(e, ws=wsa, jl=jl, pb=pa):
                                ins = None
                                for kc in range(KC):
                                    ins = e.matmul(psb[pb], wblk[ws][:, kc, jl * 128:(jl + 1) * 128], hbuf[:, kc, :],
                                                   start=(kc == 0), stop=(kc == KC - 1))
                                return ins
                            P.op("pe", mma, reads=[B_w[wsa], B_h], writes=[PS[pa]])
                            P.op("pe", lambda e, ws=wsb, jl=jl, pb=pbb: mma(e, ws, jl, pb), reads=[B_w[wsb], B_h],
                                 writes=[PS[pbb]])
                            ts = jl % 2
                            P.op("act", lambda e, ts=ts, pb=pa: e.copy(out=tA[ts], in_=psb[pb]),
                                 reads=[PS[pa]], writes=[B_tA[ts]])
                            P.op("dve", lambda e, ts=ts, zs=zs, jl=jl, pb=pbb: e.tensor_tensor(
                                out=zst[zs][:, jl, :], in0=tA[ts], in1=psb[pb], op=ALU.mult),
                                reads=[B_tA[ts], PS[pbb]], writes=[B_zst[zs]])
                        dst = ZC[b * 512:(b + 1) * 512, t * TT:(t + 1) * TT].rearrange("(c p) t -> p c t", p=128)
                        P.dma("pool", lambda e, zs=zs, dst=dst: e.dma_start(out=dst, in_=zst[zs]), d_zst[zs],
                              reads=[B_zst[zs]], writes=[ZCB[t]])
                P.barrier()

        def phase_D(l):
            is_ab = (l % 2 == 0)
            e_ = l // 2
            ph = Bump(PERS_END)
            xt = ph.get([128, KC, TT], F32)
            B_xk = [Buf(f"xk{k}") for k in range(KC)]
            d_x = P.dsem(f"xD_{l}")
            hbuf = ph.get([128, KC, TT], BF16)
            B_h = Buf("hD")
            mix_off = ph.off
            mixb = ph.get([128, KC, TT], BF16)
            B_mix = [Buf(f"mix{k}") for k in range(KC)]
            d_mix = P.dsem(f"mix_{l}")
            abuf = ph.get([128, 64, TT], BF16)
            B_a = [Buf(f"a{k}") for k in range(64)]
            sqtmp = ph.get([128, TT], F32)
            rstd = ph.get([128, TT], F32)
            tmp = [ph.get([128, TT], F32) for _ in range(2)]
            B_sq, B_r, B_tmp = Buf("sq"), Buf("rstd"), [Buf("tmp0"), Buf("tmp1")]
            wblk = [ph.get([128, KC, 512], BF16) for _ in range(2)]
            B_w = [Buf("wD0"), Buf("wD1")]
            d_w = [P.dsem(f"wD0_{l}"), P.dsem(f"wD1_{l}")]
            rl = [ph.get([128, TT], F32) for _ in range(2)]
            B_rl = [Buf("rl0"), Buf("rl1")]
            if not is_ab:
                zcb = [ph.get([128, TT + 2], F32) for _ in range(2)]
                gbb = [ph.get([128, TT], F32) for _ in range(2)]
                B_zc = [Buf("zcb0"), Buf("zcb1")]
                B_gb = [Buf("gbb0"), Buf("gbb1")]
                d_zc = [P.dsem(f"zcb0_{l}"), P.dsem(f"zcb1_{l}")]
                d_gb = [P.dsem(f"gbb0_{l}"), P.dsem(f"gbb1_{l}")]
                cv1 = [ph.get([128, TT], F32) for _ in range(2)]
                B_cv = [Buf("cv0"), Buf("cv1")]
            last = (l == n_layers - 1)
            if last:
                ost = [view(mix_off + i * 8192, [128, D], F32) for i in range(2)]
                B_ost = [B_mix[0:8], B_mix[8:16]]
                d_ost = [P.dsem("ost0"), P.dsem("ost1")]
                oi = 0
            wi = 0
            pbi = 0
            for t in range(NT):
                cv = tile_cv(t)
                src = XS[:, t * TT:(t + 1) * TT].rearrange("(kc p) t -> p kc t", p=128)
                P.dma("sp", lambda e, src=src: e.dma_start(out=xt, in_=src), d_x, reads=[XSB[t]], writes=B_xk)
                have_mix = True
                if is_ab:
                    have_mix = with_ab
                    KM = 12
                    if have_mix:
                        srcm = MIX[:, t * TT:(t + 1) * TT].rearrange("(c p) t -> p c t", p=128)
                        P.dma("sp", lambda e, srcm=srcm: e.dma_start(out=mixb[:, 0:12, :], in_=srcm), d_mix,
                              reads=[MIXB[t]] if with_ab else [], writes=B_mix)
                    wsrc = awob[e_]
                else:
                    KM = 16
                    o_ = l // 2
                    wsrc = cwob[o_]
                    segs = [(0, TT)] if t < NST else [(0, 256), (256, 256)]
                    cwo = VOFF["conv_w"] + o_ * 48
                    cbo = VOFF["conv_b"] + o_ * 16
                    for j in range(KC):
                        s = j % 2
                        t0 = t * TT
                        lo_ok = (t < NST and t > 0)
                        hi_ok = (t < NST - 1)
                        c0 = t0 - (1 if lo_ok else 0)
                        c1 = t0 + TT + (1 if hi_ok else 0)
                        srcz = ZC[j * 128:(j + 1) * 128, c0:c1]
                        o0 = 0 if lo_ok else 1
                        rd = [ZCB[t]]
                        if lo_ok:
                            rd.append(ZCB[t - 1])
                        if hi_ok:
                            rd.append(ZCB[t + 1])
                        if not lo_ok:
                            P.op("pool", lambda e, s=s: e.memset(zcb[s][:, 0:1], 0.0), writes=[B_zc[s]])
                        if not hi_ok:
                            P.op("pool", lambda e, s=s: e.memset(zcb[s][:, TT + 1:TT + 2], 0.0), writes=[B_zc[s]])
                        P.dma("sp", lambda e, s=s, srcz=srcz, o0=o0, n=c1 - c0: e.dma_start(
                            out=zcb[s][:, o0:o0 + n], in_=srcz), d_zc[s], reads=rd, writes=[B_zc[s]])
                        srcg = GB[j * 128:(j + 1) * 128, t0:t0 + TT]
                        P.dma("sp", lambda e, s=s, srcg=srcg: e.dma_start(out=gbb[s], in_=srcg), d_gb[s],
                              reads=[GBB[t]], writes=[B_gb[s]])
                        w0 = VT[:, cwo + 0 * 16 + j: cwo + 0 * 16 + j + 1]
                        w1 = VT[:, cwo + 1 * 16 + j: cwo + 1 * 16 + j + 1]
                        w2 = VT[:, cwo + 2 * 16 + j: cwo + 2 * 16 + j + 1]
                        cb = VT[:, cbo + j: cbo + j + 1]
                        P.op("dve", lambda e, s=s, w1=w1, cb=cb: e.tensor_scalar(
                            out=cv1[s], in0=zcb[s][:, 1:TT + 1], scalar1=w1, scalar2=cb, op0=ALU.mult, op1=ALU.add),
                            reads=[B_zc[s], B_VT], writes=[B_cv[s]])
                        for (a0, ln) in segs:
                            lskip = 1 if (a0 > 0) else 0
                            P.op("dve", lambda e, s=s, w0=w0, a0=a0, ln=ln, lskip=lskip: e.scalar_tensor_tensor(
                                out=cv1[s][:, a0 + lskip:a0 + ln], in0=zcb[s][:, a0 + lskip:a0 + ln], scalar=w0,
                                in1=cv1[s][:, a0 + lskip:a0 + ln], op0=ALU.mult, op1=ALU.add),
                                reads=[B_zc[s], B_VT, B_cv[s]], writes=[B_cv[s]])
                            rskip = 1 if (a0 + ln < TT) else 0
                            P.op("dve", lambda e, s=s, w2=w2, a0=a0, ln=ln, rskip=rskip: e.scalar_tensor_tensor(
                                out=cv1[s][:, a0:a0 + ln - rskip], in0=zcb[s][:, a0 + 2:a0 + 2 + ln - rskip], scalar=w2,
                                in1=cv1[s][:, a0:a0 + ln - rskip], op0=ALU.mult, op1=ALU.add),
                                reads=[B_zc[s], B_VT, B_cv[s]], writes=[B_cv[s]])
                        P.op("pool", lambda e, s=s, j=j: e.tensor_tensor(
                            out=mixb[:, j, :], in0=cv1[s], in1=gbb[s], op=ALU.mult),
                            reads=[B_cv[s], B_gb[s]], writes=[B_mix[j]])
                if have_mix:
                    for b in range(4):
                        ws = wi % 2
                        wi += 1
                        src = wsrc[:, b * 512:(b + 1) * 512].rearrange("(kc p) n -> p kc n", p=128)
                        P.dma("sp", lambda e, ws=ws, src=src, KM=KM: e.dma_start(out=wblk[ws][:, 0:KM, :], in_=src),
                              d_w[ws], reads=[WB[l]], writes=[B_w[ws]])
                        for jl in range(4):
                            j = b * 4 + jl
                            pb = 1 + pbi % 6
                            pbi += 1

                            def mm(e, ws=ws, jl=jl, pb=pb, KM=KM):
                                ins = None
                                for kc in range(KM):
                                    ins = e.matmul(psb[pb], wblk[ws][:, kc, jl * 128:(jl + 1) * 128], mixb[:, kc, :],
                                                   start=(kc == 0), stop=(kc == KM - 1))
                                return ins
                            P.op("pe", mm, reads=[B_w[ws]] + B_mix[0:KM], writes=[PS[pb]])
                            P.op("dve", lambda e, j=j, pb=pb, cv=cv: e.scalar_tensor_tensor(
                                out=xt[:, j, :], in0=psb[pb], scalar=ada_ap(l, 2, cv, j), in1=xt[:, j, :],
                                op0=ALU.mult, op1=ALU.add), reads=[PS[pb], B_ada, B_xk[j]], writes=[B_xk[j]])
                norm_modulate(l, 2, t, xt, B_xk, hbuf, B_h, sqtmp, B_sq, rstd, B_r, tmp, B_tmp, 0)
                for b in range(16):
                    ws = wi % 2
                    wi += 1
                    src = w1b[l][:, b * 512:(b + 1) * 512].rearrange("(kc p) n -> p kc n", p=128)
                    P.dma("sp", lambda e, ws=ws, src=src: e.dma_start(out=wblk[ws], in_=src), d_w[ws],
                          reads=[WB[l]], writes=[B_w[ws]])
                    for jl in range(4):
                        j = b * 4 + jl
                        pb = 1 + pbi % 6
                        pbi += 1

                        def mm(e, ws=ws, jl=jl, pb=pb):
                            ins = None
                            for kc in range(KC):
                                ins = e.matmul(psb[pb], wblk[ws][:, kc, jl * 128:(jl + 1) * 128], hbuf[:, kc, :],
                                               start=(kc == 0), stop=(kc == KC - 1))
                            return ins
                        P.op("pe", mm, reads=[B_w[ws], B_h], writes=[PS[pb]])
                        rs = j % 2
                        P.op("act", lambda e, rs=rs, pb=pb: e.activation(out=rl[rs], in_=psb[pb], func=AF.Relu),
                             reads=[PS[pb]], writes=[B_rl[rs]])
                        P.op("pool", lambda e, rs=rs, j=j: e.tensor_tensor(out=abuf[:, j, :], in0=rl[rs], in1=rl[rs],
                                                                           op=ALU.mult),
                             reads=[B_rl[rs]], writes=[B_a[j]])
                for n in range(16):
                    ws = wi % 2
                    wi += 1
                    P.dma("sp", lambda e, ws=ws, n=n: e.dma_start(
                        out=wblk[ws].rearrange("p a b -> p (a b)").rearrange("p (k m) -> p k m", m=128),
                        in_=w2b[l][n]), d_w[ws], reads=[WB[l]], writes=[B_w[ws]])
                    pb = 1 + pbi % 6
                    pbi += 1

                    def mm(e, ws=ws, pb=pb):
                        w = wblk[ws].rearrange("p a b -> p (a b)").rearrange("p (k m) -> p k m", m=128)
                        ins = None
                        for kc in range(64):
                            ins = e.matmul(psb[pb], w[:, kc, :], abuf[:, kc, :], start=(kc == 0), stop=(kc == 63))
                        return ins
                    P.op("pe", mm, reads=[B_w[ws]] + B_a, writes=[PS[pb]])
                    P.op("dve", lambda e, n=n, pb=pb, cv=cv: e.scalar_tensor_tensor(
                        out=xt[:, n, :], in0=psb[pb], scalar=ada_ap(l, 5, cv, n), in1=xt[:, n, :],
                        op0=ALU.mult, op1=ALU.add), reads=[PS[pb], B_ada, B_xk[n]], writes=[B_xk[n]])
                if not last:
                    dst = XS[:, t * TT:(t + 1) * TT].rearrange("(kc p) t -> p kc t", p=128)
                    P.dma("pool", lambda e, dst=dst: e.dma_start(out=dst, in_=xt), d_x, reads=B_xk, writes=[XSB[t]])
                else:
                    for tb in range(4):
                        os_ = oi % 2
                        oi += 1
                        for q in range(4):
                            pb = 1 + pbi % 6
                            pbi += 1

                            def tr(e, tb=tb, q=q, pb=pb):
                                ins = None
                                for i in range(4):
                                    kc = q * 4 + i
                                    ins = e.transpose(out=psb[pb][:, i * 128:(i + 1) * 128],
                                                      in_=xt[:, kc, tb * 128:(tb + 1) * 128], identity=ident)
                                return ins
                            P.op("pe", tr, reads=B_xk[q * 4:q * 4 + 4] + [B_const], writes=[PS[pb]])
                            if q % 2 == 0:
                                P.op("act", lambda e, os_=os_, q=q, pb=pb: e.copy(
                                    out=ost[os_][:, q * 512:(q + 1) * 512], in_=psb[pb]),
                                    reads=[PS[pb]], writes=B_ost[os_])
                            else:
                                P.op("dve", lambda e, os_=os_, q=q, pb=pb: e.tensor_copy(
                                    out=ost[os_][:, q * 512:(q + 1) * 512], in_=psb[pb]),
                                    reads=[PS[pb]], writes=B_ost[os_])
                        r0 = t * TT + tb * 128
                        P.dma("pool", lambda e, os_=os_, r0=r0: e.dma_start(out=y_out[r0:r0 + 128, :], in_=ost[os_]),
                              d_ost[os_], reads=B_ost[os_])
            P.barrier()


        B_U = [Buf(f"U{t}") for t in range(NT)]
        B_Q = [Buf(f"Q{t}") for t in range(NT)]
        B_K = [Buf(f"K{t}") for t in range(NT)]
        B_V = [Buf(f"V{t}") for t in range(NT)]
        B_BIAS = Buf("BIAS")
        P.op("dve", lambda e: e.tensor_scalar(out=QKG[:, 0:2], in0=VT[:, VOFF["q_g"]:VOFF["q_g"] + 2],
                                              scalar1=float(128 ** -0.5), scalar2=None, op0=ALU.mult),
             reads=[B_VT], writes=[B_ada])
        P.op("dve", lambda e: e.tensor_copy(out=QKG[:, 2:4], in_=VT[:, VOFF["k_g"]:VOFF["k_g"] + 2]),
             reads=[B_VT], writes=[B_ada])

        def phase_A_ab(l):
            e_ = l // 2
            ph = Bump(PERS_END)
            xt2 = [ph.get([128, KC, TT], F32) for _ in range(2)]
            B_x2 = [Buf("xA0"), Buf("xA1")]
            d_x2 = [P.dsem(f"xB0_{l}"), P.dsem(f"xB1_{l}")]
            hbuf = ph.get([128, KC, TT], BF16)
            B_h = Buf("hA")
            sqtmp = ph.get([128, TT], F32)
            rstd = ph.get([128, TT], F32)
            tmp = [ph.get([128, TT], F32) for _ in range(2)]
            B_sq, B_r, B_tmp = Buf("sq"), Buf("rstd"), [Buf("tmp0"), Buf("tmp1")]
            wblk = [ph.get([128, KC, 512], BF16) for _ in range(2)]
            B_w = [Buf("wB0"), Buf("wB1")]
            d_w = [P.dsem(f"wB0_{l}"), P.dsem(f"wB1_{l}")]
            stg = [ph.get([128, 4, TT], BF16) for _ in range(2)]
            B_stg = [Buf("stg0"), Buf("stg1")]
            d_stg = [P.dsem(f"stg0_{l}"), P.dsem(f"stg1_{l}")]
            hsq = [ph.get([128, TT], BF16) for _ in range(2)]
            B_hsq = [Buf("hsq0"), Buf("hsq1")]
            t1 = [ph.get([128, TT], F32) for _ in range(2)]
            B_t1 = [Buf("t10"), Buf("t11")]
            r1 = [ph.get([128, TT], F32) for _ in range(2)]
            B_r1 = [Buf("r10"), Buf("r11")]
            kf = [ph.get([128, TT], F32) for _ in range(2)]
            B_kf = [Buf("kf0"), Buf("kf1")]
            ktok = [ph.get([128, 4, 128], F32) for _ in range(2)]
            B_ktok = [Buf("ktok0"), Buf("ktok1")]
            d_ktok = [P.dsem(f"ktok0_{l}"), P.dsem(f"ktok1_{l}")]
            vf = [ph.get([128, TT], F32) for _ in range(2)]
            B_vf = [Buf("vf0"), Buf("vf1")]
            d_vf = [P.dsem(f"vf0_{l}"), P.dsem(f"vf1_{l}")]
            cnt = {"w": 0, "pb": 0, "st": 0, "hn": 0, "kt": 0, "vf": 0, "pn": 0}

            def load_x(t):
                s = t % 2
                src = XS[:, t * TT:(t + 1) * TT].rearrange("(kc p) t -> p kc t", p=128)
                P.dma("sp", lambda e, s=s, src=src: e.dma_start(out=xt2[s], in_=src), d_x2[s],
                      reads=[XSB[t]], writes=[B_x2[s]])

            def load_w(b):
                ws = cnt["w"] % 2
                cnt["w"] += 1
                src = awib[e_][:, b * 512:(b + 1) * 512].rearrange("(kc p) n -> p kc n", p=128)
                P.dma("sp", lambda e, ws=ws, src=src: e.dma_start(out=wblk[ws], in_=src), d_w[ws],
                      reads=[WB[l]], writes=[B_w[ws]])
                return ws

            def proj_fm(ws, jl):
                pb = 1 + cnt["pb"] % 4
                cnt["pb"] += 1

                def mm(e, ws=ws, jl=jl, pb=pb):
                    ins = None
                    for kc in range(KC):
                        ins = e.matmul(psb[pb], wblk[ws][:, kc, jl * 128:(jl + 1) * 128], hbuf[:, kc, :],
                                       start=(kc == 0), stop=(kc == KC - 1))
                    return ins
                P.op("pe", mm, reads=[B_w[ws], B_h], writes=[PS[pb]])
                return pb

            load_x(0)
            for t in range(NT):
                s = t % 2
                is_prompt = t >= NST
                if t + 1 < NT:
                    load_x(t + 1)
                norm_modulate(l, 1, t, xt2[s], [B_x2[s]], hbuf, B_h, sqtmp, B_sq, rstd, B_r, tmp, B_tmp, 0)
                ws = load_w(0)
                ss = cnt["st"] % 2
                cnt["st"] += 1
                for jl in range(4):
                    pb = proj_fm(ws, jl)
                    P.op("act", lambda e, ss=ss, jl=jl, pb=pb: e.copy(out=stg[ss][:, jl, :], in_=psb[pb]),
                         reads=[PS[pb]], writes=[B_stg[ss]])
                dst = Usc[:, t * TT:(t + 1) * TT].rearrange("(c p) t -> p c t", p=128)
                P.dma("pool", lambda e, ss=ss, dst=dst: e.dma_start(out=dst, in_=stg[ss]), d_stg[ss],
                      reads=[B_stg[ss]], writes=[B_U[t]])
                for which in range(0 if "Aqk" not in DBG_SKIP else 2, 2):
                    for hb in range(2):
                        ws = load_w(1 + which * 2 + hb)
                        ss = cnt["st"] % 2
                        cnt["st"] += 1
                        for jl in range(4):
                            h = hb * 4 + jl
                            pb = proj_fm(ws, jl)
                            hs = cnt["hn"] % 2
                            cnt["hn"] += 1
                            pn = 5 + cnt["pn"] % 2
                            cnt["pn"] += 1
                            P.op("act", lambda e, hs=hs, pb=pb: e.activation(out=hsq[hs], in_=psb[pb], func=AF.Square),
                                 reads=[PS[pb]], writes=[B_hsq[hs]])
                            P.op("pe", lambda e, hs=hs, pn=pn: e.matmul(psb[pn], onesb, hsq[hs], start=True, stop=True),
                                 reads=[B_hsq[hs], B_const], writes=[PS[pn]])
                            P.op("act", lambda e, hs=hs, pn=pn: e.activation(out=t1[hs], in_=psb[pn], func=AF.Sqrt,
                                                                             bias=epsc[:, 0:1], scale=1.0 / 128),
                                 reads=[PS[pn], B_const], writes=[B_t1[hs]])
                            P.op("dve", lambda e, hs=hs: e.reciprocal(out=r1[hs], in_=t1[hs]),
                                 reads=[B_t1[hs]], writes=[B_r1[hs]])
                            gcol = which * 2 + e_
                            if which == 1 and is_prompt and "Akt" not in DBG_SKIP:
                                P.op("dve", lambda e, hs=hs, pb=pb, gcol=gcol: e.scalar_tensor_tensor(
                                    out=kf[hs], in0=psb[pb], scalar=QKG[:, gcol:gcol + 1], in1=r1[hs],
                                    op0=ALU.mult, op1=ALU.mult), reads=[PS[pb], B_r1[hs], B_ada], writes=[B_kf[hs]])
                                P.op("act", lambda e, hs=hs, ss=ss, jl=jl: e.copy(out=stg[ss][:, jl, :], in_=kf[hs]),
                                     reads=[B_kf[hs]], writes=[B_stg[ss]])
                                ks = cnt["kt"] % 2
                                cnt["kt"] += 1

                                def tr(e, hs=hs):
                                    ins = None
                                    for tb in range(4):
                                        ins = e.transpose(out=psb[7][:, tb * 128:(tb + 1) * 128],
                                                          in_=kf[hs][:, tb * 128:(tb + 1) * 128], identity=ident)
                                    return ins
                                P.op("pe", tr, reads=[B_kf[hs], B_const], writes=[PS[7]])
                                P.op("dve", lambda e, ks=ks: e.tensor_copy(
                                    out=ktok[ks], in_=psb[7].rearrange("p (a b) -> p a b", a=4)),
                                    reads=[PS[7]], writes=[B_ktok[ks]])
                                s0 = (t - NST) * 2
                                for sq2 in range(2):
                                    dstk = nck[s0 + sq2, e_, :, h, :].rearrange("(tb p) d -> p tb d", p=128)
                                    P.dma("pool", lambda e, ks=ks, dstk=dstk, sq2=sq2: e.dma_start(
                                        out=dstk, in_=ktok[ks][:, 2 * sq2:2 * sq2 + 2, :]), d_ktok[ks],
                                        reads=[B_ktok[ks]])
                            else:
                                P.op("dve", lambda e, hs=hs, pb=pb, gcol=gcol, ss=ss, jl=jl: e.scalar_tensor_tensor(
                                    out=stg[ss][:, jl, :], in0=psb[pb], scalar=QKG[:, gcol:gcol + 1], in1=r1[hs],
                                    op0=ALU.mult, op1=ALU.mult), reads=[PS[pb], B_r1[hs], B_ada], writes=[B_stg[ss]])
                        dsc = Qsc if which == 0 else Ksc
                        dst = dsc[hb * 512:(hb + 1) * 512, t * TT:(t + 1) * TT].rearrange("(c p) t -> p c t", p=128)
                        P.dma("pool", lambda e, ss=ss, dst=dst: e.dma_start(out=dst, in_=stg[ss]), d_stg[ss],
                              reads=[B_stg[ss]], writes=[(B_Q if which == 0 else B_K)[t]])
                for ch in range(2 if "Av" not in DBG_SKIP else 0):
                    ws = load_w(5 + ch)
                    ss = cnt["st"] % 2
                    cnt["st"] += 1
                    for tb in range(4):
                        pb = 1 + cnt["pb"] % 4
                        cnt["pb"] += 1

                        def mmv(e, ws=ws, tb=tb, pb=pb):
                            ins = None
                            for kc in range(KC):
                                ins = e.matmul(psb[pb], hbuf[:, kc, tb * 128:(tb + 1) * 128], wblk[ws][:, kc, :],
                                               start=(kc == 0), stop=(kc == KC - 1))
                            return ins
                        P.op("pe", mmv, reads=[B_w[ws], B_h], writes=[PS[pb]])
                        if is_prompt and "Avf" not in DBG_SKIP:
                            vs = cnt["vf"] % 2
                            cnt["vf"] += 1
                            P.op("act", lambda e, vs=vs, pb=pb: e.copy(out=vf[vs], in_=psb[pb]),
                                 reads=[PS[pb]], writes=[B_vf[vs]])
                            P.op("dve", lambda e, vs=vs, ss=ss, tb=tb: e.tensor_copy(out=stg[ss][:, tb, :], in_=vf[vs]),
                                 reads=[B_vf[vs]], writes=[B_stg[ss]])
                            sq_ = (t - NST) * 2 + tb // 2
                            dstv = ncv[sq_, e_, (tb % 2) * 128:(tb % 2) * 128 + 128, ch * 4:(ch + 1) * 4, :].rearrange(
                                "p h d -> p (h d)")
                            P.dma("pool", lambda e, vs=vs, dstv=dstv: e.dma_start(out=dstv, in_=vf[vs]), d_vf[vs],
                                  reads=[B_vf[vs]])
                        else:
                            P.op("act", lambda e, ss=ss, tb=tb, pb=pb: e.copy(out=stg[ss][:, tb, :], in_=psb[pb]),
                                 reads=[PS[pb]], writes=[B_stg[ss]])
                    dst = Vsc[t * TT:(t + 1) * TT, ch * 512:(ch + 1) * 512].rearrange("(tb p) c -> p tb c", p=128)
                    P.dma("pool", lambda e, ss=ss, dst=dst: e.dma_start(out=dst, in_=stg[ss]), d_stg[ss],
                          reads=[B_stg[ss]], writes=[B_V[t]])
            P.barrier()


        POWS = [0, 1, 2, 3, 4, 5, 6, 7, 8, 16, 32, 64, 128, 256, 512, 1024, 2048]
        NPW = len(POWS)
        TWO_PI = 6.283185307179586
        PI_ = 3.141592653589793
        B_UF = {(g, fc): Buf(f"uf{g}{fc}") for g in range(2) for fc in range(4)}

        def phase_S5(l):
            e_ = l // 2
            ph = Bump(PERS_END)
            WA = ph.get([128, 2, 2, 4, 8, 128], BF16)
            WC = ph.get([128, 2, 2, 8, 4, 160], BF16)
            WK = ph.get([128, 2, 8, 4, 128], BF16)
            PR = ph.get([128, 2, NPW, 16], F32)
            PIm = ph.get([128, 2, NPW, 16], F32)
            NPI = ph.get([128, 2, NPW, 16], F32)
            H0 = ph.get([128, 2, 2, 16], F32)
            ZER = ph.get([128, 8], F32)
            HPI = ph.get([128, 1], F32)
            W_END = ph.off
            B_W = Buf("s5w")
            B_pow = Buf("s5pow")
            SPt = ph.get([128, 2, S5P_COLS], F32)
            B_SP = Buf("s5sp")
            d_sp = P.dsem(f"s5sp_{l}")
            for d in range(2):
                P.dma("sp", lambda e, d=d: e.dma_start(out=SPt[:, d, :], in_=s5p[e_, d]), d_sp, writes=[B_SP])
                P.dma("sp", lambda e, d=d: e.dma_start(out=H0[:, d, :, :], in_=s5h0[e_, d].rearrange("r p g -> p r g")),
                      d_sp, writes=[B_SP])
            P.op("pool", lambda e: e.memset(ZER, 0.0), writes=[B_pow])
            P.op("pool", lambda e: e.memset(HPI, PI_ / 2), writes=[B_pow])
            P.op("pool", lambda e: e.memset(WC, 0.0), writes=[B_W])
            lamr, lami, ldt = SPt[:, :, 0:16], SPt[:, :, 16:32], SPt[:, :, 32:48]
            sm = [ph.get([128, 2, 16], F32) for _ in range(12)]
            dt_, ar_, ai_, t1_, t2_, den_, nr_, fre, fim, a1_, a2_, _x = sm
            big_ = [ph.get([128, 2, NPW, 16], F32) for _ in range(5)]
            ANG, MGL, TF, Rr, M1 = big_
            TI = ph.get([128, 2, NPW, 16], F32).bitcast(mybir.dt.int32)

            def f2(a):
                return a.rearrange("p d n g -> p (d n g)")
            Bp = B_pow
            P.op("act", lambda e: e.activation(out=dt_, in_=ldt, func=AF.Exp), reads=[B_SP], writes=[Bp])
            P.op("dve", lambda e: e.tensor_tensor(out=ar_, in0=lamr, in1=dt_, op=ALU.mult), reads=[B_SP, Bp], writes=[Bp])
            P.op("dve", lambda e: e.tensor_tensor(out=ai_, in0=lami, in1=dt_, op=ALU.mult), reads=[B_SP, Bp], writes=[Bp])
            for i, n in enumerate(POWS):
                P.op("dve", lambda e, i=i, n=n: e.tensor_scalar(out=ANG[:, :, i, :], in0=ai_, scalar1=float(n), scalar2=None,
                                                                op0=ALU.mult), reads=[Bp], writes=[Bp])
                P.op("dve", lambda e, i=i, n=n: e.tensor_scalar(out=MGL[:, :, i, :], in0=ar_, scalar1=float(n), scalar2=None,
                                                                op0=ALU.mult), reads=[Bp], writes=[Bp])
            P.op("dve", lambda e: e.tensor_scalar(out=f2(TF), in0=f2(ANG), scalar1=1.0 / TWO_PI, scalar2=None, op0=ALU.mult),
                 reads=[Bp], writes=[Bp])
            P.op("dve", lambda e: e.tensor_copy(out=f2(TI), in_=f2(TF)), reads=[Bp], writes=[Bp])
            P.op("dve", lambda e: e.tensor_copy(out=f2(TF), in_=f2(TI)), reads=[Bp], writes=[Bp])
            P.op("dve", lambda e: e.scalar_tensor_tensor(out=f2(Rr), in0=f2(TF), scalar=-TWO_PI, in1=f2(ANG), op0=ALU.mult,
                                                         op1=ALU.add), reads=[Bp], writes=[Bp])
            P.op("dve", lambda e: e.tensor_scalar(out=f2(M1), in0=f2(Rr), scalar1=PI_, scalar2=-TWO_PI, op0=ALU.is_gt,
                                                  op1=ALU.mult), reads=[Bp], writes=[Bp])
            P.op("dve", lambda e: e.tensor_tensor(out=f2(Rr), in0=f2(Rr), in1=f2(M1), op=ALU.add), reads=[Bp], writes=[Bp])
            P.op("dve", lambda e: e.tensor_scalar(out=f2(M1), in0=f2(Rr), scalar1=-PI_, scalar2=TWO_PI, op0=ALU.is_lt,
                                                  op1=ALU.mult), reads=[Bp], writes=[Bp])
            P.op("dve", lambda e: e.tensor_tensor(out=f2(Rr), in0=f2(Rr), in1=f2(M1), op=ALU.add), reads=[Bp], writes=[Bp])
            P.op("act", lambda e: e.activation(out=f2(TF), in_=f2(Rr), func=AF.Sin), reads=[Bp], writes=[Bp])
            P.op("dve", lambda e: e.tensor_scalar(out=f2(M1), in0=f2(Rr), scalar1=-1.0, scalar2=None, op0=ALU.mult),
                 reads=[Bp], writes=[Bp])
            P.op("dve", lambda e: e.tensor_tensor(out=f2(M1), in0=f2(M1), in1=f2(Rr), op=ALU.max), reads=[Bp], writes=[Bp])
            P.op("act", lambda e: e.activation(out=f2(ANG), in_=f2(M1), func=AF.Sin, bias=HPI[:, 0:1], scale=-1.0),
                 reads=[Bp], writes=[Bp])
            P.op("act", lambda e: e.activation(out=f2(MGL), in_=f2(MGL), func=AF.Exp), reads=[Bp], writes=[Bp])
            P.op("dve", lambda e: e.tensor_tensor(out=f2(PR), in0=f2(MGL), in1=f2(ANG), op=ALU.mult), reads=[Bp], writes=[Bp])
            P.op("dve", lambda e: e.tensor_tensor(out=f2(PIm), in0=f2(MGL), in1=f2(TF), op=ALU.mult), reads=[Bp], writes=[Bp])
            P.op("dve", lambda e: e.tensor_scalar(out=f2(NPI), in0=f2(PIm), scalar1=-1.0, scalar2=None, op0=ALU.mult),
                 reads=[Bp], writes=[Bp])
            pr1, pi1 = PR[:, :, 1, :], PIm[:, :, 1, :]

            def tt(out, a, b, op):
                P.op("dve", lambda e: e.tensor_tensor(out=out, in0=a, in1=b, op=op), reads=[Bp, B_SP], writes=[Bp])
            tt(t1_, lamr, lamr, ALU.mult)
            tt(t2_, lami, lami, ALU.mult)
            tt(den_, t1_, t2_, ALU.add)
            P.op("dve", lambda e: e.reciprocal(out=den_, in_=den_), reads=[Bp], writes=[Bp])
            P.op("dve", lambda e: e.tensor_scalar(out=nr_, in0=pr1, scalar1=-1.0, scalar2=None, op0=ALU.add),
                 reads=[Bp], writes=[Bp])
            tt(a1_, nr_, lamr, ALU.mult)
            tt(a2_, pi1, lami, ALU.mult)
            tt(fre, a1_, a2_, ALU.add)
            tt(fre, fre, den_, ALU.mult)
            tt(a1_, pi1, lamr, ALU.mult)
            tt(a2_, nr_, lami, ALU.mult)
            tt(fim, a1_, a2_, ALU.subtract)
            tt(fim, fim, den_, ALU.mult)
            ER = ph.get([128, 2, 8, 16], F32)
            EI = ph.get([128, 2, 8, 16], F32)
            e1 = ph.get([128, 2, 8, 16], F32)
            e2 = ph.get([128, 2, 8, 16], F32)
            freb = fre.unsqueeze(2).broadcast_to([128, 2, 8, 16])
            fimb = fim.unsqueeze(2).broadcast_to([128, 2, 8, 16])
            tt(e1, PR[:, :, 0:8, :], freb, ALU.mult)
            tt(e2, PIm[:, :, 0:8, :], fimb, ALU.mult)
            tt(ER, e1, e2, ALU.subtract)
            tt(e1, PR[:, :, 0:8, :], fimb, ALU.mult)
            tt(e2, PIm[:, :, 0:8, :], freb, ALU.mult)
            tt(EI, e1, e2, ALU.add)
            NAT = [[ph.get([128, 16, 32], F32) for _ in range(2)] for _ in range(2)]
            B_NAT = [Buf("nat0"), Buf("nat1")]
            q1 = [ph.get([128, 16, 32], F32) for _ in range(2)]
            BB = [[ph.get([128, 16, 32], F32) for _ in range(2)] for _ in range(2)]
            B_BB = Buf("bb")
            B_q = Buf("q1")

            def bc(a):
                return a.unsqueeze(2).broadcast_to([128, 16, 32])
            it = 0
            for d in range(2):
                Br = SPt[:, d, 48:560].rearrange("p (g c) -> p g c", c=32)
                Bi = SPt[:, d, 560:1072].rearrange("p (g c) -> p g c", c=32)
                for n in range(8):
                    st_ = it % 2
                    it += 1
                    er, ei = bc(ER[:, d, n, :]), bc(EI[:, d, n, :])
                    Nr, Ni = NAT[st_]
                    rd = [Bp, B_SP, B_q]

                    def o(eng, out, a, b, op, rds, wrs):
                        P.op(eng, lambda e: e.tensor_tensor(out=out, in0=a, in1=b, op=op), reads=rds, writes=wrs)
                    o("dve", q1[0], Br, er, ALU.mult, rd, [B_q])
                    o("dve", q1[1], Bi, ei, ALU.mult, rd, [B_q])
                    o("dve", Nr, q1[0], q1[1], ALU.subtract, [B_q], [B_NAT[st_]])
                    o("dve", q1[0], Bi, er, ALU.mult, rd, [B_q])
                    o("dve", q1[1], Br, ei, ALU.mult, rd, [B_q])
                    o("dve", Ni, q1[0], q1[1], ALU.add, [B_q], [B_NAT[st_]])
                    if n == 0:
                        for r_ in range(2):
                            P.op("pool", lambda e, d=d, r_=r_, st_=st_: e.tensor_copy(out=BB[d][r_], in_=NAT[st_][r_]),
                                 reads=[B_NAT[st_]], writes=[B_BB])
                    s_idx = (7 - n) if d == 0 else n
                    for r_ in range(2):
                        pb = (it * 2 + r_) % 4

                        def tr(e, st_=st_, r_=r_, pb=pb):
                            ins = None
                            for fc in range(4):
                                ins = e.transpose(out=psb[pb][:, fc * 128:(fc + 1) * 128],
                                                  in_=NAT[st_][r_][:, 4 * fc:4 * fc + 4, :].rearrange("p a b -> p (a b)"),
                                                  identity=ident)
                            return ins
                        P.op("pe", tr, reads=[B_NAT[st_], B_const], writes=[PS[pb]])
                        P.op("act", lambda e, d=d, r_=r_, s_idx=s_idx, pb=pb: e.copy(
                            out=WA[:, d, r_, :, s_idx, :], in_=psb[pb].rearrange("p (a b) -> p a b", a=4)),
                            reads=[PS[pb]], writes=[B_W])
            Bpad = ph.get([128, 2, 16, 128], F32)
            B_Bpad = Buf("bpad")
            Cc = [[ph.get([128, 16, 32], F32) for _ in range(2)] for _ in range(2)]
            B_Cc = [Buf("cc0"), Buf("cc1")]
            _ccp = [ph.get([128, 16, 128], F32) for _ in range(2)]
            CcP = [_ccp, _ccp]
            _bccp = Buf("ccp")
            B_CcP = [_bccp, _bccp]
            P.op("pool", lambda e: e.memset(Bpad, 0.0), writes=[B_Bpad])
            for r_ in range(2):
                P.op("pool", lambda e, r_=r_: e.memset(CcP[0][r_], 0.0), writes=[B_CcP[0]])
            P.op("pool", lambda e: e.memset(WK, 0.0), writes=[B_W])
            it = 0
            for d in range(2):
                for r_ in range(2):
                    for g4 in range(4):
                        P.op("pool", lambda e, d=d, r_=r_, g4=g4: e.tensor_copy(
                            out=Bpad[:, r_, :, :].rearrange("p (fc g) c -> p fc g c", g=4)[:, :, g4, 32 * g4:32 * g4 + 32],
                            in_=BB[d][r_].rearrange("p (fc g) c -> p fc g c", g=4)[:, :, g4, :]),
                            reads=[B_BB], writes=[B_Bpad])
                CTr = SPt[:, d, 1072:1584].rearrange("p (g c) -> p g c", c=32)
                CTi = SPt[:, d, 1584:2096].rearrange("p (g c) -> p g c", c=32)
                for n in range(9):
                    st_ = it % 2
                    it += 1
                    prb, pib, npb = bc(PR[:, d, n, :]), bc(PIm[:, d, n, :]), bc(NPI[:, d, n, :])
                    CR, CN = Cc[st_]
                    rd = [Bp, B_SP, B_q]
                    o("dve", q1[0], CTr, prb, ALU.mult, rd, [B_q])
                    o("dve", q1[1], CTi, pib, ALU.mult, rd, [B_q])
                    o("dve", CR, q1[0], q1[1], ALU.subtract, [B_q], [B_Cc[st_]])
                    o("dve", q1[0], CTr, npb, ALU.mult, rd, [B_q])
                    o("dve", q1[1], CTi, prb, ALU.mult, rd, [B_q])
                    o("dve", CN, q1[0], q1[1], ALU.subtract, [B_q], [B_Cc[st_]])
                    if n >= 1:
                        for r_ in range(2):
                            src4 = Cc[st_][r_].rearrange("p (fc g) c -> p fc g c", g=4)
                            P.op("act", lambda e, d=d, r_=r_, n=n, src4=src4: e.copy(
                                out=WC[:, d, r_, n - 1, :, 0:96].rearrange("p fc (g c) -> p fc g c", c=32),
                                in_=src4[:, :, 0:3, :]), reads=[B_Cc[st_]], writes=[B_W])
                            P.op("act", lambda e, d=d, r_=r_, n=n, src4=src4: e.copy(
                                out=WC[:, d, r_, n - 1, :, 128:160], in_=src4[:, :, 3, :]), reads=[B_Cc[st_]], writes=[B_W])
                    if n <= 7:
                        for r_ in range(2):
                            for g4 in range(4):
                                P.op("pool", lambda e, st_=st_, r_=r_, g4=g4: e.tensor_copy(
                                    out=CcP[st_][r_].rearrange("p (fc g) c -> p fc g c", g=4)[:, :, g4, 32 * g4:32 * g4 + 32],
                                    in_=Cc[st_][r_].rearrange("p (fc g) c -> p fc g c", g=4)[:, :, g4, :]),
                                    reads=[B_Cc[st_]], writes=[B_CcP[st_]])
                        pb = 4 + it % 4

                        def kmm(e, st_=st_, pb=pb):
                            ins = None
                            for fc in range(4):
                                k = 0
                                for g4 in range(4):
                                    for r_ in range(2):
                                        ins = e.matmul(psb[pb][:, fc * 128:(fc + 1) * 128], Bpad[:, r_, 4 * fc + g4, :],
                                                       CcP[st_][r_][:, 4 * fc + g4, :], start=(k == 0), stop=(k == 7))
                                        k += 1
                            return ins
                        P.op("pe", kmm, reads=[B_Bpad, B_CcP[st_]], writes=[PS[pb]])
                        P.op("act", lambda e, d=d, n=n, pb=pb: e.copy(
                            out=WK[:, d, n, :, :], in_=psb[pb].rearrange("p (a b) -> p a b", a=4)),
                            reads=[PS[pb]], writes=[B_W])
            P.barrier()

            rt = Bump(W_END)
            ubuf = rt.get([128, 4096], BF16)
            B_u = Buf("ubuf")
            d_u = P.dsem(f"s5u_{l}")
            um = [rt.get([128, 4096], BF16) for _ in range(4)]
            B_um = [Buf(f"um{i}") for i in range(4)]
            d_um = [P.dsem(f"s5um{i}_{l}") for i in range(4)]
            XS_ = [[rt.get([128, 2, 512], F32) for _ in range(2)] for _ in range(2)]
            B_XS = [[[Buf(f"xs{s_}{i}m"), Buf(f"xs{s_}{i}e")] for i in range(2)] for s_ in range(2)]
            XP = rt.get([128, 16, 512], BF16)
            B_XP = [Buf(f"xp{i}") for i in range(16)]
            yj = [rt.get([128, 512], F32) for _ in range(2)]
            B_yj = [Buf("yj0"), Buf("yj1")]
            gt_ = [rt.get([128, 512], F32) for _ in range(3)]
            B_gt = [Buf("gt0"), Buf("gt1"), Buf("gt2")]
            gst = rt.get([128, 4096], BF16)
            B_gst = Buf("gst")
            d_gst = P.dsem(f"s5g_{l}")
            FS = rt.get([128, 2, 2, 4, 16], F32)
            B_FS = Buf("fs")
            for i in range(4):
                P.op("pool", lambda e, i=i: e.memset(um[i], 0.0), writes=[B_um[i]])
            cnt = {"px": 0, "py": 0, "ch": 0, "yj": 0}

            ptmp = rt.get([128, 512], F32)
            B_ptmp = Buf("ptmp")

            def stt_any(eng, out, in0, scal, in1, rds, wrs, tmpv):
                if eng == "dve":
                    P.op("dve", lambda e: e.scalar_tensor_tensor(out=out, in0=in0, scalar=scal, in1=in1,
                                                                 op0=ALU.mult, op1=ALU.add), reads=rds, writes=wrs)
                else:
                    P.op("pool", lambda e: e.tensor_scalar(out=tmpv, in0=in0, scalar1=scal, scalar2=None, op0=ALU.mult),
                         reads=rds, writes=[B_ptmp])
                    P.op("pool", lambda e: e.tensor_tensor(out=out, in0=tmpv, in1=in1, op=ALU.add),
                         reads=rds + [B_ptmp], writes=wrs)

            def run_group(gi, nseq, Cs, tok0, npass, is_sample):
                NC_ = nseq * Cs
                NTK = NC_ * 8

                def v3(ap2):
                    return ap2[:, 0:NC_].rearrange("p (q c) -> p q c", c=Cs)
                for fc in range(4):
                    P.dma("sp", lambda e, fc=fc: e.dma_start(out=ubuf[:, 0:NTK], in_=Usc[fc * 128:(fc + 1) * 128,
                                                                                         tok0:tok0 + NTK]),
                          d_u, reads=[B_UF[(gi, fc)]] + (B_U if gi == 0 and fc == 0 else []), writes=[B_u])
                    u4 = ubuf[:, 0:NTK].rearrange("p (c s) -> p c s", s=8)
                    for g4 in range(4):
                        gp = 4 * fc + g4
                        P.dma("sp", lambda e, fc=fc, g4=g4: e.dma_start(
                            out=um[g4][32 * g4:32 * g4 + 32, 0:NTK],
                            in_=Usc[fc * 128 + 32 * g4:fc * 128 + 32 * g4 + 32, tok0:tok0 + NTK]), d_um[g4],
                            reads=[B_UF[(gi, fc)]], writes=[B_um[g4]])
                        um4 = um[g4][:, 0:NTK].rearrange("p (c s) -> p c s", s=8)
                        for d in range(2):
                            eng = "dve" if d == 0 else "pool"
                            set_ = cnt["ch"] % 2
                            cnt["ch"] += 1
                            X = XS_[set_]
                            BX = B_XS[set_]
                            for r_ in range(2):
                                pb = cnt["px"] % 4
                                cnt["px"] += 1

                                def amm(e, d=d, r_=r_, fc=fc, pb=pb, um4=um4):
                                    ins = None
                                    for s_ in range(8):
                                        ins = e.matmul(psb[pb][:, 0:NC_], WA[:, d, r_, fc, s_, :], um4[:, :, s_],
                                                       start=(s_ == 0), stop=(s_ == 7))
                                    return ins
                                P.op("pe", amm, reads=[B_W, B_um[g4]], writes=[PS[pb]])
                                P.op("act", lambda e, r_=r_, pb=pb, X=X: e.copy(out=X[0][:, r_, 0:NC_], in_=psb[pb][:, 0:NC_]),
                                     reads=[PS[pb]], writes=BX[0])
                            pr8, pi8, npi8 = (PR[:, d, 8, gp:gp + 1], PIm[:, d, 8, gp:gp + 1], NPI[:, d, 8, gp:gp + 1])
                            if is_sample:
                                col = 0 if d == 0 else NC_ - 1
                                xc = X[0][:, :, col]
                                P.op("dve", lambda e, xc=xc, d=d, gp=gp, pr8=pr8: e.scalar_tensor_tensor(
                                    out=xc, in0=H0[:, d, :, gp], scalar=pr8, in1=xc, op0=ALU.mult, op1=ALU.add),
                                    reads=BX[0] + [B_pow, B_SP], writes=BX[0])
                                P.op("dve", lambda e, X=X, col=col, d=d, gp=gp, npi8=npi8: e.scalar_tensor_tensor(
                                    out=X[0][:, 0, col:col + 1], in0=H0[:, d, 1, gp:gp + 1], scalar=npi8,
                                    in1=X[0][:, 0, col:col + 1], op0=ALU.mult, op1=ALU.add),
                                    reads=BX[0] + [B_pow, B_SP], writes=BX[0])
                                P.op("dve", lambda e, X=X, col=col, d=d, gp=gp, pi8=pi8: e.scalar_tensor_tensor(
                                    out=X[0][:, 1, col:col + 1], in0=H0[:, d, 0, gp:gp + 1], scalar=pi8,
                                    in1=X[0][:, 1, col:col + 1], op0=ALU.mult, op1=ALU.add),
                                    reads=BX[0] + [B_pow, B_SP], writes=BX[0])

                            def v4(t):
                                return t[:, :, 0:NC_].rearrange("p r (q c) -> p r q c", c=Cs)
                            cur = 0
                            for k in range(npass):
                                sh = 1 << k
                                pr_, pi_, npi_ = (PR[:, d, 8 + k, gp:gp + 1], PIm[:, d, 8 + k, gp:gp + 1],
                                                  NPI[:, d, 8 + k, gp:gp + 1])
                                src, dst = v4(X[cur]), v4(X[1 - cur])
                                Bs, Bd = BX[cur], BX[1 - cur]
                                if d == 0:
                                    a_, b_, c_ = slice(sh, Cs), slice(0, Cs - sh), slice(0, sh)
                                else:
                                    a_, b_, c_ = slice(0, Cs - sh), slice(sh, Cs), slice(Cs - sh, Cs)
                                if nseq == 1:
                                    P.op("dve", lambda e, src=src, dst=dst, a_=a_, b_=b_, pr_=pr_: e.scalar_tensor_tensor(
                                        out=dst[:, :, 0, a_], in0=src[:, :, 0, b_], scalar=pr_, in1=src[:, :, 0, a_],
                                        op0=ALU.mult, op1=ALU.add), reads=Bs + [B_pow], writes=[Bd[0]])
                                else:
                                    for r2 in range(2):
                                        P.op("dve", lambda e, src=src, dst=dst, a_=a_, b_=b_, pr_=pr_, r2=r2: e.scalar_tensor_tensor(
                                            out=dst[:, r2, :, a_], in0=src[:, r2, :, b_], scalar=pr_, in1=src[:, r2, :, a_],
                                            op0=ALU.mult, op1=ALU.add), reads=Bs + [B_pow], writes=[Bd[0]])
                                P.op("dve", lambda e, src=src, dst=dst, a_=a_, b_=b_, npi_=npi_: e.scalar_tensor_tensor(
                                    out=dst[:, 0, :, a_], in0=src[:, 1, :, b_], scalar=npi_, in1=dst[:, 0, :, a_],
                                    op0=ALU.mult, op1=ALU.add), reads=Bs + [Bd[0], B_pow], writes=[Bd[0]])
                                P.op("dve", lambda e, src=src, dst=dst, a_=a_, b_=b_, pi_=pi_: e.scalar_tensor_tensor(
                                    out=dst[:, 1, :, a_], in0=src[:, 0, :, b_], scalar=pi_, in1=dst[:, 1, :, a_],
                                    op0=ALU.mult, op1=ALU.add), reads=Bs + [Bd[0], B_pow], writes=[Bd[0]])
                                if nseq == 1:
                                    P.op("act", lambda e, src=src, dst=dst, c_=c_: e.copy(out=dst[:, :, 0, c_], in_=src[:, :, 0, c_]),
                                         reads=Bs, writes=[Bd[1]])
                                else:
                                    for r2 in range(2):
                                        P.op("act", lambda e, src=src, dst=dst, c_=c_, r2=r2: e.copy(
                                            out=dst[:, r2, :, c_], in_=src[:, r2, :, c_]), reads=Bs, writes=[Bd[1]])
                                cur = 1 - cur
                            for r_ in range(2):
                                Xf = v4(X[cur])[:, r_, :, :]
                                BXf = BX[cur]
                                slot = (g4 * 2 + d) * 2 + r_
                                xp3 = v3(XP[:, slot, :])
                                if not is_sample:
                                    ecol = Cs - 1 if d == 0 else 0
                                    P.op("act", lambda e, d=d, r_=r_, gp=gp, Xf=Xf, ecol=ecol: e.copy(
                                        out=FS[:, r_, d, :, gp], in_=Xf[:, :, ecol]), reads=BXf, writes=[B_FS])
                                if d == 0:
                                    P.op("act", lambda e, xp3=xp3, Xf=Xf: e.copy(out=xp3[:, :, 1:Cs], in_=Xf[:, :, 0:Cs - 1]),
                                         reads=BXf, writes=[B_XP[slot]])
                                    ec = 0
                                else:
                                    P.op("act", lambda e, xp3=xp3, Xf=Xf: e.copy(out=xp3[:, :, 0:Cs - 1], in_=Xf[:, :, 1:Cs]),
                                         reads=BXf, writes=[B_XP[slot]])
                                    ec = Cs - 1
                                if is_sample:
                                    hsrc = H0[:, d, r_, gp:gp + 1]
                                    P.op("act", lambda e, xp3=xp3, ec=ec, hsrc=hsrc: e.copy(out=xp3[:, 0, ec:ec + 1], in_=hsrc),
                                         reads=[B_SP], writes=[B_XP[slot]])
                                else:
                                    P.op("act", lambda e, xp3=xp3, ec=ec: e.copy(out=xp3[:, :, ec], in_=ZER[:, 0:nseq]),
                                         reads=[B_pow], writes=[B_XP[slot]])
                    g3 = gst[:, 0:NTK].rearrange("p (c s) -> p c s", s=8)
                    dcol = VT[:, VOFF["s5_d"] + e_ * 4 + fc: VOFF["s5_d"] + e_ * 4 + fc + 1]
                    for j in range(8):
                        pb = 4 + cnt["py"] % 4
                        cnt["py"] += 1

                        def ymm(e, j=j, fc=fc, pb=pb, u4=u4):
                            first = True
                            for s_ in range(0, j + 1):
                                e.matmul(psb[pb][:, 0:NC_], WK[:, 0, j - s_, fc, :], u4[:, :, s_], start=first, stop=False)
                                first = False
                            for s_ in range(j, 8):
                                e.matmul(psb[pb][:, 0:NC_], WK[:, 1, s_ - j, fc, :], u4[:, :, s_], start=False, stop=False)
                            ins = None
                            for g4 in range(4):
                                for d in range(2):
                                    nidx = j if d == 0 else 7 - j
                                    for r_ in range(2):
                                        slot = (g4 * 2 + d) * 2 + r_
                                        last = (g4 == 3 and d == 1 and r_ == 1)
                                        if g4 < 3:
                                            ins = e.matmul(psb[pb][32 * g4:32 * g4 + 32, 0:NC_],
                                                           WC[:, d, r_, nidx, fc, 32 * g4:32 * g4 + 32], XP[:, slot, 0:NC_],
                                                           start=False, stop=last)
                                        else:
                                            ins = e.matmul(psb[pb][64:128, 0:NC_], WC[:, d, r_, nidx, fc, 96:160],
                                                           XP[:, slot, 0:NC_], start=False, stop=last)
                            return ins
                        P.op("pe", ymm, reads=[B_W, B_u] + B_XP, writes=[PS[pb]])
                        ys = cnt["yj"] % 2
                        cnt["yj"] += 1
                        P.op("dve", lambda e, j=j, pb=pb, ys=ys, u4=u4, dcol=dcol: e.scalar_tensor_tensor(
                            out=yj[ys][:, 0:NC_], in0=u4[:, :, j], scalar=dcol, in1=psb[pb][:, 0:NC_],
                            op0=ALU.mult, op1=ALU.add), reads=[PS[pb], B_u, B_VT], writes=[B_yj[ys]])
                        P.op("act", lambda e, ys=ys: e.activation(out=gt_[0][:, 0:NC_], in_=yj[ys][:, 0:NC_], func=AF.Square),
                             reads=[B_yj[ys]], writes=[B_gt[0]])
                        P.op("dve", lambda e: e.tensor_scalar(out=gt_[1][:, 0:NC_], in0=gt_[0][:, 0:NC_], scalar1=0.044715,
                                                              scalar2=1.0, op0=ALU.mult, op1=ALU.add),
                             reads=[B_gt[0]], writes=[B_gt[1]])
                        P.op("dve", lambda e, ys=ys: e.tensor_tensor(out=gt_[1][:, 0:NC_], in0=gt_[1][:, 0:NC_],
                                                                     in1=yj[ys][:, 0:NC_], op=ALU.mult),
                             reads=[B_gt[1], B_yj[ys]], writes=[B_gt[1]])
                        P.op("act", lambda e: e.activation(out=gt_[2][:, 0:NC_], in_=gt_[1][:, 0:NC_], func=AF.Sigmoid,
                                                           scale=1.5957691216057308), reads=[B_gt[1]], writes=[B_gt[2]])
                        P.op("dve", lambda e, j=j, ys=ys, g3=g3: e.tensor_tensor(
                            out=g3[:, :, j], in0=yj[ys][:, 0:NC_], in1=gt_[2][:, 0:NC_], op=ALU.mult),
                            reads=[B_yj[ys], B_gt[2]], writes=[B_gst])
                    P.dma("pool", lambda e, fc=fc: e.dma_start(out=Usc[fc * 128:(fc + 1) * 128, tok0:tok0 + NTK],
                                                               in_=gst[:, 0:NTK]), d_gst, reads=[B_gst],
                          writes=[B_UF[(gi, fc)]])

            run_group(0, 1, 512, 0, 9, True)
            run_group(1, 4, 32, 4096, 5, False)
            fso = [rt.get([128, 128], F32) for _ in range(2)]
            B_fso = [Buf("fso0"), Buf("fso1")]
            d_fso = [P.dsem(f"fso0_{l}"), P.dsem(f"fso1_{l}")]
            for r_ in range(2):
                P.op("pe", lambda e, r_=r_: e.transpose(out=psb[r_][:, 0:128],
                                                        in_=FS[:, r_, :, :, :].rearrange("p d q g -> p (d q g)"),
                                                        identity=ident), reads=[B_FS, B_const], writes=[PS[r_]])
                P.op("act", lambda e, r_=r_: e.copy(out=fso[r_], in_=psb[r_][:, 0:128]), reads=[PS[r_]], writes=[B_fso[r_]])
                dst_t = nsr if r_ == 0 else nsi
                for d in range(2):
                    for q_ in range(4):
                        dstf = dst_t[q_, e_, d, :, :].rearrange("(gp g2) p -> gp (g2 p)", g2=2)
                        P.dma("sp", lambda e, r_=r_, d=d, q_=q_, dstf=dstf: e.dma_start(
                            out=dstf, in_=fso[r_][64 * d + 16 * q_:64 * d + 16 * q_ + 16, :]), d_fso[r_],
                            reads=[B_fso[r_]])
            P.barrier()

            gl = Bump(W_END)
            GW = gl.get([128, 4, 512], BF16)
            B_GW = Buf("gw")
            d_gw = P.dsem(f"gw_{l}")
            P.dma("sp", lambda e: e.dma_start(out=GW, in_=glub[e_].rearrange("(kc p) n -> p kc n", p=128)), d_gw,
                  reads=[WB[l]], writes=[B_GW])
            gtile = [gl.get([128, 4, TT], BF16) for _ in range(2)]
            B_gtile = [Buf("gti0"), Buf("gti1")]
            d_gtile = [P.dsem(f"gti0_{l}"), P.dsem(f"gti1_{l}")]
            sgt = [gl.get([128, TT], F32) for _ in range(2)]
            B_sgt = [Buf("sg0"), Buf("sg1")]
            ostg = [gl.get([128, 4, TT], BF16) for _ in range(2)]
            B_ostg = [Buf("os0"), Buf("os1")]
            d_ostg = [P.dsem(f"os0_{l}"), P.dsem(f"os1_{l}")]
            for t in range(NT):
                s_ = t % 2
                P.dma("sp", lambda e, s_=s_, t=t: e.dma_start(
                    out=gtile[s_], in_=Usc[:, t * TT:(t + 1) * TT].rearrange("(c p) t -> p c t", p=128)), d_gtile[s_],
                    writes=[B_gtile[s_]])
                for n in range(4):
                    pb = (t * 4 + n) % 4

                    def gmm(e, s_=s_, n=n, pb=pb):
                        ins = None
                        for kc in range(4):
                            ins = e.matmul(psb[pb], GW[:, kc, n * 128:(n + 1) * 128], gtile[s_][:, kc, :],
                                           start=(kc == 0), stop=(kc == 3))
                        return ins
                    P.op("pe", gmm, reads=[B_GW, B_gtile[s_]], writes=[PS[pb]])
                    ss_ = n % 2
                    bcol = VT[:, VOFF["glu_b"] + e_ * 4 + n: VOFF["glu_b"] + e_ * 4 + n + 1]
                    P.op("act", lambda e, ss_=ss_, pb=pb, bcol=bcol: e.activation(out=sgt[ss_], in_=psb[pb], func=AF.Sigmoid,
                                                                                 bias=bcol, scale=1.0),
                         reads=[PS[pb], B_VT], writes=[B_sgt[ss_]])
                    P.op("dve", lambda e, s_=s_, ss_=ss_, n=n: e.tensor_tensor(out=ostg[s_][:, n, :], in0=gtile[s_][:, n, :],
                                                                               in1=sgt[ss_], op=ALU.mult),
                         reads=[B_gtile[s_], B_sgt[ss_]], writes=[B_ostg[s_]])
                P.dma("pool", lambda e, s_=s_, t=t: e.dma_start(
                    out=MIX[0:512, t * TT:(t + 1) * TT].rearrange("(c p) t -> p c t", p=128), in_=ostg[s_]), d_ostg[s_],
                    reads=[B_ostg[s_]], writes=[MIXB[t]])
            P.barrier()

        def phase_S5_zero(l):
            ph = Bump(PERS_END)
            zt = ph.get([128, NTOK], BF16)
            bz = Buf("zt")
            dz = P.dsem(f"zt_{l}")
            P.op("pool", lambda e: e.memset(zt, 0.0), writes=[bz])
            for c in range(4):
                P.dma("sp", lambda e, c=c: e.dma_start(out=MIX[c * 128:(c + 1) * 128, :], in_=zt), dz,
                      reads=[bz], writes=MIXB)
            P.barrier()

        NEG = -30000.0

        def phase_nab(e_):
            ph = Bump(PERS_END)
            E2 = ph.get([128, 8, 15, 64], F32)
            B_E2 = Buf("E2")
            Tt = [ph.get([128, 8, 10, 64], F32) for _ in range(2)]
            B_Tt = [Buf("Tt0"), Buf("Tt1")]
            d_Tt = [P.dsem(f"Tt0_{e_}"), P.dsem(f"Tt1_{e_}")]
            d_e2 = P.dsem(f"e2_{e_}")
            P.op("pool", lambda e: e.memset(E2, NEG), writes=[B_E2])
            for p in range(128):
                col = p % 64
                c0 = min(max(col - 8, 0), 48)
                off = c0 - col + 15
                P.dma("sp", lambda e, p=p, c0=c0, off=off: e.dma_start(
                    out=E2[p:p + 1, :, :, c0:c0 + 16], in_=na_rpb[e_:e_ + 1, :, :, off:off + 16]),
                    d_e2, writes=[B_E2])
            for ti, key in enumerate(PAIR_TYPES):
                s = ti % 2
                P.op("pool", lambda e, s=s: e.memset(Tt[s], NEG), writes=[B_Tt[s]])
                for rr in range(2):
                    wr_lo, i_lo = key[rr]
                    P.op("dve", lambda e, s=s, rr=rr, wr_lo=wr_lo, i_lo=i_lo: e.tensor_copy(
                        out=Tt[s][rr * 64:(rr + 1) * 64, :, wr_lo:wr_lo + 8, :],
                        in_=E2[rr * 64:(rr + 1) * 64, :, i_lo:i_lo + 8, :]), reads=[B_E2], writes=[B_Tt[s]])
                P.dma("sp", lambda e, s=s, ti=ti: e.dma_start(
                    out=BIAS[e_, ti], in_=Tt[s].rearrange("p h w c -> p h (w c)")), d_Tt[s],
                    reads=[B_Tt[s]], writes=[B_BIAS])
            P.barrier()

        def phase_C(l):
            e_ = l // 2
            ph = Bump(PERS_END)
            qh = [ph.get([128, NTOK], BF16) for _ in range(2)]
            kh = [ph.get([128, NTOK], BF16) for _ in range(2)]
            vh = [ph.get([128, NTOK // 128, 128], BF16) for _ in range(2)]
            B_qh = [Buf("qh0"), Buf("qh1")]
            B_kh = [Buf("kh0"), Buf("kh1")]
            B_vh = [Buf("vh0"), Buf("vh1")]
            d_qh = [P.dsem(f"qh0_{l}"), P.dsem(f"qh1_{l}")]
            d_kh = [P.dsem(f"kh0_{l}"), P.dsem(f"kh1_{l}")]
            d_vh = [P.dsem(f"vh0_{l}"), P.dsem(f"vh1_{l}")]
            bh = [ph.get([128, NTYPES, 640], F32) for _ in range(2)]
            B_bh = [Buf("bh0"), Buf("bh1")]
            d_bh = [P.dsem(f"bh0_{l}"), P.dsem(f"bh1_{l}")]
            ck32 = [ph.get([128, 2, 128], F32) for _ in range(2)]
            cv32 = [ph.get([128, 2, 128], F32) for _ in range(2)]
            B_ck32 = [Buf("ck0"), Buf("ck1")]
            B_cv32 = [Buf("cv0"), Buf("cv1")]
            d_ck = [P.dsem(f"ck0_{l}"), P.dsem(f"ck1_{l}")]
            d_cv = [P.dsem(f"cv0_{l}"), P.dsem(f"cv1_{l}")]
            kcb = [ph.get([128, 256], BF16) for _ in range(2)]
            vcb = [ph.get([128, 2, 128], BF16) for _ in range(2)]
            B_kcb = [Buf("kcb0"), Buf("kcb1")]
            B_vcb = [Buf("vcb0"), Buf("vcb1")]
            at = [ph.get([128, NTOK], BF16) for _ in range(2)]
            B_at = [Buf("at0"), Buf("at1")]
            d_at = [P.dsem(f"at0_{l}"), P.dsem(f"at1_{l}")]
            NSET = 4
            sS = [ph.get([128, 896], F32) for _ in range(NSET)]
            B_sS = [Buf(f"sS{i}") for i in range(NSET)]
            Pb = [ph.get([128, 896], BF16) for _ in range(NSET)]
            B_Pb = [Buf(f"Pb{i}") for i in range(NSET)]
            PT = [ph.get([128, 7, 128], BF16) for _ in range(NSET)]
            B_PT = [Buf(f"PT{i}") for i in range(NSET)]
            nmx = [ph.get([128, 1], F32) for _ in range(NSET)]
            B_mx = [Buf(f"mx{i}") for i in range(NSET)]
            junk = ph.get([128, 896], BF16)
            B_junk = Buf("junk")
            rsb = [ph.get([128, 128], F32) for _ in range(NSET)]
            B_rsb = [Buf(f"rsb{i}") for i in range(NSET)]
            psT = [psb[4].bitcast(BF16), psb[5].bitcast(BF16)]
            ucnt = [0]

            def softmax_pv(NKC, vch, vbufs, out_ap, B_out):
                u = ucnt[0]
                ucnt[0] += 1
                p = u % 2
                q = u % NSET
                NK = NKC * 128
                P.op("dve", lambda e: e.tensor_scalar(out=junk[:, 0:NK], in0=sS[q][:, 0:NK], scalar1=-1.0, scalar2=None,
                                                      op0=ALU.mult, op1=ALU.min, accum_out=nmx[q][:, 0:1]),
                     reads=[B_sS[q]], writes=[B_mx[q], B_junk])
                P.op("act", lambda e: e.activation(out=Pb[q][:, 0:NK], in_=sS[q][:, 0:NK], func=AF.Exp,
                                                   bias=nmx[q][:, 0:1], scale=1.0),
                     reads=[B_sS[q], B_mx[q]], writes=[B_Pb[q]])

                def tr(e):
                    ins = None
                    for c in range(NKC):
                        ins = e.transpose(out=psT[p][:, c * 128:(c + 1) * 128], in_=Pb[q][:, c * 128:(c + 1) * 128],
                                          identity=identb)
                    return ins
                P.op("pe", tr, reads=[B_Pb[q], B_const], writes=[PS[4 + p]])
                P.op("act", lambda e: e.copy(out=PT[q][:, 0:NKC, :],
                                             in_=psT[p][:, 0:NKC * 128].rearrange("p (c q) -> p c q", q=128)),
                     reads=[PS[4 + p]], writes=[B_PT[q]])

                def pv(e):
                    ins = None
                    for c in range(NKC):
                        ins = e.matmul(psb[6 + p][:, 0:128], vch[c], PT[q][:, c, :], start=(c == 0), stop=(c == NKC - 1))
                    for c in range(NKC):
                        ins = e.matmul(psb[6 + p][:, 128:256], onesb, PT[q][:, c, :], start=(c == 0),
                                       stop=(c == NKC - 1))
                    return ins
                P.op("pe", pv, reads=[B_PT[q], B_const] + vbufs, writes=[PS[6 + p]])
                P.op("dve", lambda e: e.reciprocal(out=rsb[q], in_=psb[6 + p][:, 128:256]),
                     reads=[PS[6 + p]], writes=[B_rsb[q]])
                P.op("dve", lambda e: e.tensor_tensor(out=out_ap, in0=psb[6 + p][:, 0:128], in1=rsb[q], op=ALU.mult),
                     reads=[PS[6 + p], B_rsb[q]], writes=[B_out])

            def load_head(h):
                hp = h % 2
                P.dma("sp", lambda e: e.dma_start(out=qh[hp], in_=Qsc[h * 128:(h + 1) * 128, :]), d_qh[hp],
                      reads=B_Q, writes=[B_qh[hp]])
                P.dma("sp", lambda e: e.dma_start(out=kh[hp], in_=Ksc[h * 128:(h + 1) * 128, :]), d_kh[hp],
                      reads=B_K, writes=[B_kh[hp]])
                P.dma("sp", lambda e: e.dma_start(
                    out=vh[hp], in_=Vsc[:, h * 128:(h + 1) * 128].rearrange("(c p) d -> p c d", p=128)), d_vh[hp],
                    reads=B_V, writes=[B_vh[hp]])
                P.dma("sp", lambda e: e.dma_start(
                    out=bh[hp], in_=BIAS[e_, :, :, h, :].rearrange("t p k -> p t k")), d_bh[hp],
                    reads=[B_BIAS], writes=[B_bh[hp]])
                P.dma("sp", lambda e: e.dma_start(
                    out=ck32[hp], in_=cache_k[e_, :, h, :].rearrange("(c p) d -> p c d", p=128)), d_ck[hp],
                    writes=[B_ck32[hp]])
                P.dma("sp", lambda e: e.dma_start(
                    out=cv32[hp], in_=cache_v[e_, :, h, :].rearrange("(c p) d -> p c d", p=128)), d_cv[hp],
                    writes=[B_cv32[hp]])

            load_head(0)
            for h in range(8):
                hp = h % 2
                if h + 1 < 8:
                    load_head(h + 1)
                pbk = 4 + hp

                def trk(e, hp=hp, pbk=pbk):
                    ins = None
                    for c in range(2):
                        ins = e.transpose(out=psb[pbk][:, c * 128:(c + 1) * 128], in_=ck32[hp][:, c, :], identity=ident)
                    return ins
                P.op("pe", trk, reads=[B_ck32[hp], B_const], writes=[PS[pbk]])
                P.op("act", lambda e, hp=hp, pbk=pbk: e.copy(out=kcb[hp], in_=psb[pbk][:, 0:256]),
                     reads=[PS[pbk]], writes=[B_kcb[hp]])
                P.op("act", lambda e, hp=hp: e.copy(out=vcb[hp], in_=cv32[hp]), reads=[B_cv32[hp]], writes=[B_vcb[hp]])
                for a in range(32):
                    ty, w0 = PAIR_MAP[a]
                    u = ucnt[0]
                    p = u % 2
                    q = u % NSET
                    k0 = 64 * w0
                    q0 = 128 * a

                    def smm(e, hp=hp, p=p, k0=k0, q0=q0):
                        e.matmul(psb[p], qh[hp][:, q0:q0 + 128], kh[hp][:, k0:k0 + 512], start=True, stop=True)
                        e.matmul(psb[2 + p][:, 0:128], qh[hp][:, q0:q0 + 128], kh[hp][:, k0 + 512:k0 + 640],
                                 start=True, stop=True)
                        return e.matmul(psb[2 + p][:, 128:384], qh[hp][:, q0:q0 + 128], kcb[hp], start=True, stop=True)
                    P.op("pe", smm, reads=[B_qh[hp], B_kh[hp], B_kcb[hp]], writes=[PS[p], PS[2 + p]])
                    P.op("dve", lambda e, hp=hp, p=p, q=q, ty=ty: e.tensor_tensor(
                        out=sS[q][:, 0:512], in0=psb[p], in1=bh[hp][:, ty, 0:512], op=ALU.add),
                        reads=[PS[p], B_bh[hp]], writes=[B_sS[q]])
                    P.op("dve", lambda e, hp=hp, p=p, q=q, ty=ty: e.tensor_tensor(
                        out=sS[q][:, 512:640], in0=psb[2 + p][:, 0:128], in1=bh[hp][:, ty, 512:640], op=ALU.add),
                        reads=[PS[2 + p], B_bh[hp]], writes=[B_sS[q]])
                    P.op("dve", lambda e, p=p, q=q: e.tensor_copy(out=sS[q][:, 640:896], in_=psb[2 + p][:, 128:384]),
                         reads=[PS[2 + p]], writes=[B_sS[q]])
                    vt0 = k0 // 128
                    vch = [vh[hp][:, vt0 + c, :] for c in range(5)] + [vcb[hp][:, c, :] for c in range(2)]
                    softmax_pv(7, vch, [B_vh[hp], B_vcb[hp]], at[hp][:, q0:q0 + 128], B_at[hp])
                for sq_ in range(4):
                    tok0 = 4096 + 256 * sq_
                    for qb in range(2):
                        u = ucnt[0]
                        p = u % 2
                        q = u % NSET
                        q0 = tok0 + qb * 128
                        P.op("pe", lambda e, hp=hp, p=p, q0=q0, tok0=tok0: e.matmul(
                            psb[p][:, 0:256], qh[hp][:, q0:q0 + 128], kh[hp][:, tok0:tok0 + 256], start=True, stop=True),
                            reads=[B_qh[hp], B_kh[hp]], writes=[PS[p]])
                        P.op("act", lambda e, p=p, q=q: e.copy(out=sS[q][:, 0:256], in_=psb[p][:, 0:256]),
                             reads=[PS[p]], writes=[B_sS[q]])
                        vt0 = tok0 // 128
                        vch = [vh[hp][:, vt0 + c, :] for c in range(2)]
                        softmax_pv(2, vch, [B_vh[hp]], at[hp][:, q0:q0 + 128], B_at[hp])
                P.dma("pool", lambda e, hp=hp, h=h: e.dma_start(out=MIX[512 + h * 128:512 + (h + 1) * 128, :], in_=at[hp]),
                      d_at[hp], reads=[B_at[hp]], writes=MIXB)
            P.barrier()

        if mode == "nab":
            phase_nab(0)
        elif mode == "C":
            phase_C(0)
        elif mode == "A":
            phase_A_ab(0)
        elif mode == "S5":
            phase_S5(0)
        for l in range(n_layers if big else 0):
            if l % 2 == 1:
                phase_A_conv(l)
            elif with_ab:
                if "A" not in DBG_SKIP:
                    phase_A_ab(l)
                if with_s5:
                    phase_S5(l)
                else:
                    phase_S5_zero(l)
                if "nab" not in DBG_SKIP:
                    phase_nab(l // 2)
                if "C" not in DBG_SKIP:
                    phase_C(l)
            if "D" not in DBG_SKIP:
                phase_D(l)
        P.barrier()
        P.finalize()
        for d in P.dsems:
            d.h = es.enter_context(nc.semaphore("d_" + d.name))
        with nc.Block() as block:
            @block.tensor
            def _(e):
                P.emit("pe", e, sems)

            @block.scalar
            def _(e):
                P.emit("act", e, sems)

            @block.vector
            def _(e):
                P.emit("dve", e, sems)

            @block.gpsimd
            def _(e):
                P.emit("pool", e, sems)

            @block.sync
            def _(e):
                P.emit("sp", e, sems)
    return nc


VOFF = {}
_rows = 0
for _nm, _n in (("cvec", 32), ("ada_b", 4 * 96), ("norm1_g", 64), ("norm2_g", 64), ("conv_w", 96), ("conv_b", 32),
                ("s5_d", 8), ("glu_b", 8), ("q_g", 2), ("k_g", 2)):
    VOFF[_nm] = _rows
    _rows += _n
NVROWS = _rows


def _pack_vecs(inp, core):
    rows = []
    cv = np.stack([inp["c"][core], inp["c_ctx"]], axis=0)
    rows.append(cv.reshape(32, 128))
    rows.append(inp["ada_b"].reshape(4 * 96, 128))
    rows.append(inp["norm1_g"].reshape(64, 128))
    rows.append(inp["norm2_g"].reshape(64, 128))
    rows.append(inp["conv_w"].reshape(96, 128))
    rows.append(inp["conv_b"].reshape(32, 128))
    rows.append(inp["s5_d"].reshape(8, 128))
    rows.append(inp["s5_glu_b"].reshape(8, 128))
    rows.append(inp["q_norm_g"].reshape(2, 128))
    rows.append(inp["k_norm_g"].reshape(2, 128))
    return np.ascontiguousarray(np.concatenate(rows, axis=0).astype(np.float32))


def _pair_types():
    types, tmap = [], []
    for a in range(32):
        w0 = min(max(2 * a - 4, 0), 54)
        key = []
        for rr in range(2):
            r = 2 * a + rr
            r0 = min(max(r - 4, 0), 56)
            key.append((r0 - w0, r0 - r + 7))
        key = tuple(key)
        if key not in types:
            types.append(key)
        tmap.append((types.index(key), w0))
    return types, tmap


PAIR_TYPES, PAIR_MAP = _pair_types()
NTYPES = len(PAIR_TYPES)
S5P_COLS = 16 * 3 + 4 * 16 * 32

_NC_CACHE = {}


def _get_nc(key=(DEPTH, True)):
    if key not in _NC_CACHE:
        _NC_CACHE[key] = build_program(*key)
    return _NC_CACHE[key]


def _lay_gp(a):
    return a.reshape(16, 2, 64).transpose(1, 2, 0).reshape(128, 16)


def _pack_s5(inp):
    out = np.zeros((2, 2, 128, S5P_COLS), np.float32)
    for e in range(2):
        for d in range(2):
            out[e, d, :, 0:16] = _lay_gp(inp["s5_lam_re"][e, d])
            out[e, d, :, 16:32] = _lay_gp(inp["s5_lam_im"][e, d])
            ldt = inp["s5_log_dt"][e, d].reshape(16, 2).T
            out[e, d, :, 32:48] = np.broadcast_to(ldt[:, None, :], (2, 64, 16)).reshape(128, 16)
            col = 48
            for nm, is_c in (("s5_b_re", False), ("s5_b_im", False), ("s5_c_re", True), ("s5_c_im", True)):
                a = inp[nm][e, d]
                blk = np.zeros((2, 64, 16, 2, 16), np.float32)
                for g2 in range(2):
                    if is_c:
                        blk[g2, :, :, g2, :] = a.reshape(16, 2, 16, 64)[:, g2].transpose(2, 0, 1)
                    else:
                        blk[g2, :, :, g2, :] = a.reshape(16, 2, 64, 16)[:, g2].transpose(1, 0, 2)
                out[e, d, :, col:col + 512] = blk.reshape(128, 512)
                col += 512
    return out


def make_in_maps(inp):
    ident = np.eye(128, dtype=np.float32)
    s5p = _pack_s5(inp)
    maps = []
    for i in range(8):
        x_in = np.concatenate([inp["x_sample"][i], inp["x_prompt"][4 * i:4 * i + 4].reshape(1024, D)], axis=0)
        m = {
            "x_in": np.ascontiguousarray(x_in, dtype=np.float32),
            "vecs": _pack_vecs(inp, i),
            "ident": ident,
            "ada_w": inp["ada_w"], "ab_w_in": inp["ab_w_in"], "ab_w_out": inp["ab_w_out"],
            "conv_w_in": inp["conv_w_in"], "conv_w_out": inp["conv_w_out"],
            "mlp_w1": inp["mlp_w1"], "mlp_w2": inp["mlp_w2"],
            "cache_k": np.ascontiguousarray(inp["cache_k"][i]), "cache_v": np.ascontiguousarray(inp["cache_v"][i]),
            "na_rpb": inp["na_rpb"], "glu_w": inp["s5_glu_w"], "s5p": s5p,
            "s5h0": np.ascontiguousarray(np.stack([
                np.stack([np.stack([_lay_gp(inp["state_ssm_re"][i, e, d]), _lay_gp(inp["state_ssm_im"][i, e, d])], 0)
                          for d in range(2)], 0) for e in range(2)], 0), dtype=np.float32),
        }
        maps.append(m)
    return maps


def kernel(**inp):
    inp = {k: np.asarray(v) for k, v in inp.items()}
    nc = _get_nc()
    res = run_bass_kernel_spmd(nc, make_in_maps(inp), core_ids=list(range(8)))
    rs = res.results
    y_s = np.stack([rs[i]["y"][:4096] for i in range(8)], axis=0)
    y_p = np.concatenate([rs[i]["y"][4096:].reshape(4, 256, D) for i in range(8)], axis=0)
    nk = np.concatenate([rs[i]["nck"] for i in range(8)], axis=0)
    nv = np.concatenate([rs[i]["ncv"] for i in range(8)], axis=0)
    sr = np.concatenate([rs[i]["nsr"] for i in range(8)], axis=0)
    si = np.concatenate([rs[i]["nsi"] for i in range(8)], axis=0)
    return (y_p.astype(np.float32), y_s.astype(np.float32), nk.astype(np.float32), nv.astype(np.float32),
            sr.astype(np.float32), si.astype(np.float32))
```

```python
import contextlib
import numpy as np
import concourse.bass as bass
import concourse.mybir as mybir
from concourse.bass_utils import run_bass_kernel_spmd

F32 = mybir.dt.float32
BF16 = mybir.dt.bfloat16
U8 = mybir.dt.uint8
AF = mybir.ActivationFunctionType
ALU = mybir.AluOpType

D = 2048
KC = 16
DEPTH = 4
TT = 512
NTOK = 5120
NT = NTOK // TT
NST = 8
DFF = 8192
EPS = 1e-6
ABIN = 3584
ABOUT = 1536


class Buf:
    __slots__ = ("name", "w", "r")

    def __init__(self, name):
        self.name = name
        self.w = None
        self.r = {}


class DSem:
    def __init__(self, name):
        self.name = name
        self.count = 0
        self.h = None


class Op:
    __slots__ = ("eng", "fn", "waits", "inc", "val", "dsem", "dval")

    def __init__(self, eng, fn, waits, dsem=None):
        self.eng = eng
        self.fn = fn
        self.waits = waits
        self.inc = False
        self.val = 0
        self.dsem = dsem
        self.dval = 0


ENGS = ["pe", "act", "dve", "pool", "sp"]
DBG_SKIP = set()
_I, _O = "ExternalInput", "ExternalOutput"
MODE_IO = {
    "nab": {"BIAS": _O},
    "C": {"Qsc": _I, "Ksc": _I, "Vsc": _I, "BIAS": _I, "MIX": _O},
    "A": {"XS": _I, "awib0": _I, "Usc": _O, "Qsc": _O, "Ksc": _O, "Vsc": _O},
    "S5": {"Usc": _I, "glub0": _I, "MIX": _O},
}


class Prog:
    def __init__(self):
        self.ops = {e: [] for e in ENGS}
        self.dsems = []
        self.last = {e: None for e in ENGS}

    def dsem(self, name):
        key = name.split("_")[0]
        for d in self.dsems:
            if d.name == key:
                return d
        d = DSem(key)
        self.dsems.append(d)
        return d

    def _deps(self, eng, reads, writes):
        toks = []
        for b in reads:
            if b.w is not None:
                toks.append(b.w)
        for b in writes:
            if b.w is not None:
                toks.append(b.w)
            toks.extend(b.r.values())
        out = []
        for t in toks:
            if t[0] == "E" and t[1].eng == "pe" and eng == "pe":
                continue
            out.append(t)
        return out

    def op(self, eng, fn, reads=(), writes=()):
        o = Op(eng, fn, self._deps(eng, reads, writes))
        self.ops[eng].append(o)
        self.last[eng] = o
        tok = ("E", o)
        for b in reads:
            b.r[eng] = tok
        for b in writes:
            b.w = tok
            b.r = {}
        return o

    def dma(self, eng, fn, dsem, reads=(), writes=()):
        o = Op(eng, fn, self._deps(eng, reads, writes), dsem=dsem)
        dsem.count += 1
        o.dval = 16 * dsem.count
        self.ops[eng].append(o)
        tok = ("D", dsem, o.dval)
        for b in reads:
            b.r["dma_" + dsem.name] = tok
        for b in writes:
            b.w = tok
            b.r = {}
        return o

    def barrier(self):
        toks = []
        for e in ENGS:
            if self.last[e] is not None:
                toks.append(("E", self.last[e]))
        for d in self.dsems:
            if d.count:
                toks.append(("D", d, 16 * d.count))
        for e in ENGS:
            o = Op(e, None, list(toks))
            self.ops[e].append(o)

    def finalize(self):
        for e in ENGS:
            for o in self.ops[e]:
                for t in o.waits:
                    if t[0] == "E":
                        t[1].inc = True
        for e in ENGS:
            n = 0
            for o in self.ops[e]:
                if o.inc and o.dsem is None and o.fn is not None:
                    n += 1
                    o.val = n
                elif o.inc:
                    o.val = n

    def emit(self, ename, eng, sems):
        known = {}
        for o in self.ops[ename]:
            for t in o.waits:
                if t[0] == "E":
                    p = t[1]
                    if p.dsem is not None:
                        s, v = p.dsem.h, p.dval
                        key = ("d", p.dsem.name)
                    else:
                        s, v = sems[p.eng], p.val
                        key = ("e", p.eng)
                else:
                    s, v = t[1].h, t[2]
                    key = ("d", t[1].name)
                if v <= 0:
                    continue
                if known.get(key, 0) < v:
                    eng.wait_ge(s, v)
                    known[key] = v
            if o.fn is None:
                continue
            ins = o.fn(eng)
            if o.dsem is not None:
                ins.then_inc(o.dsem.h, 16)
            elif o.inc:
                ins.then_inc(sems[ename], 1)


def build_program(n_layers=DEPTH, with_ab=True, with_s5=True, mode="full"):
    nc = bass.Bass("TRN2", target_bir_lowering=False)
    P = Prog()

    big = (mode == "full")
    io = MODE_IO.get(mode, {})

    def din(name, shape, dt=F32, heavy=False):
        if heavy and not big:
            shape = [1, 1]
        return nc.dram_tensor(name, list(shape), dt, kind="ExternalInput").ap()

    def dout(name, shape, dt=F32):
        return nc.dram_tensor(name, list(shape), dt, kind="ExternalOutput").ap()

    def dscr(name, shape, dt):
        if name in io:
            return nc.dram_tensor(name, list(shape), dt, kind=io[name]).ap()
        return nc.dram_tensor(name, list(shape), dt).ap()

    x_in = din("x_in", [NTOK, D], heavy=True)
    vecs = din("vecs", [NVROWS, 128])
    ident_in = din("ident", [128, 128])
    ada_w = din("ada_w", [DEPTH, D, 6 * D], heavy=True)
    ab_w_in = din("ab_w_in", [2, D, ABIN], heavy=True)
    ab_w_out = din("ab_w_out", [2, ABOUT, D], heavy=True)
    conv_w_in = din("conv_w_in", [2, D, 3 * D], heavy=True)
    conv_w_out = din("conv_w_out", [2, D, D], heavy=True)
    mlp_w1 = din("mlp_w1", [DEPTH, D, DFF], heavy=True)
    mlp_w2 = din("mlp_w2", [DEPTH, DFF, D], heavy=True)
    cache_k = din("cache_k", [2, 256, 8, 128])
    cache_v = din("cache_v", [2, 256, 8, 128])
    na_rpb = din("na_rpb", [2, 8, 15, 31])
    glu_w = din("glu_w", [2, 512, 512])
    s5p = din("s5p", [2, 2, 128, S5P_COLS])
    s5h0 = din("s5h0", [2, 2, 2, 128, 16])
    y_out = dout("y", [NTOK, D])
    nck = dout("nck", [4, 2, 256, 8, 128])
    ncv = dout("ncv", [4, 2, 256, 8, 128])
    nsr = dout("nsr", [4, 2, 2, 32, 64])
    nsi = dout("nsi", [4, 2, 2, 32, 64])
    XS = dscr("XS", [D, NTOK], F32)
    ZC = dscr("ZC", [D, NTOK], F32)
    GB = dscr("GB", [D, NTOK], F32)
    MIX = dscr("MIX", [ABOUT, NTOK], BF16)
    Usc = dscr("Usc", [512, NTOK], BF16)
    Qsc = dscr("Qsc", [1024, NTOK], BF16)
    Ksc = dscr("Ksc", [1024, NTOK], BF16)
    Vsc = dscr("Vsc", [NTOK, 1024], BF16)
    BIAS = dscr("BIAS", [2, NTYPES, 128, 8, 640], F32)
    glub = [dscr(f"glub{e}", [512, 512], BF16) for e in range(2)]
    w1b = [dscr(f"w1b{l}", [D, DFF], BF16) for l in range(DEPTH)]
    w2b = [dscr(f"w2b{l}", [16, 128, 64, 128], BF16) for l in range(DEPTH)]
    cwib = [dscr(f"cwib{o}", [D, 3 * D], BF16) for o in range(2)]
    cwob = [dscr(f"cwob{o}", [D, D], BF16) for o in range(2)]
    awib = [dscr(f"awib{e}", [D, ABIN], BF16) for e in range(2)]
    awob = [dscr(f"awob{e}", [ABOUT, D], BF16) for e in range(2)]

    es = contextlib.ExitStack()
    with es:
        ARENA = 206 * 1024
        arena = es.enter_context(nc.sbuf_tensor("arena", [128, ARENA], U8))
        psall = es.enter_context(nc.psum_tensor("psall", [128, 4096], F32))[:]
        psb = [psall[:, i * 512:(i + 1) * 512] for i in range(8)]
        PS = [Buf(f"ps{i}") for i in range(8)]
        sems = {e: es.enter_context(nc.semaphore("s_" + e)) for e in ENGS}

        def view(off, shape, dt):
            esz = 4 if dt == F32 else 2
            n = 1
            for s_ in shape[1:]:
                n *= s_
            a = arena[0:shape[0], off:off + n * esz].bitcast(dt)
            if len(shape) == 2:
                return a
            names = " ".join(f"d{i}" for i in range(1, len(shape)))
            kw = {f"d{i}": shape[i] for i in range(1, len(shape) - 1)}
            return a.rearrange(f"p ({names}) -> p {names}", **kw)

        class Bump:
            def __init__(self, base=0):
                self.off = base

            def get(self, shape, dt):
                esz = 4 if dt == F32 else 2
                n = 1
                for s_ in shape[1:]:
                    n *= s_
                o = self.off
                self.off = (o + n * esz + 63) // 64 * 64
                assert self.off <= ARENA, ("SBUF arena overflow", self.off)
                return view(o, shape, dt)

        pers = Bump(0)
        ident = pers.get([128, 128], F32)
        identb = pers.get([128, 128], BF16)
        onesb = pers.get([128, 128], BF16)
        epsc = pers.get([128, 1], F32)
        VT = pers.get([128, NVROWS], F32)
        ADA = pers.get([128, DEPTH, 96, 2], F32)
        G1 = pers.get([128, DEPTH, 2, KC], F32)
        G2 = pers.get([128, DEPTH, 2, KC], F32)
        QKG = pers.get([128, 4], F32)
        B_const = Buf("const")
        B_ada = Buf("ada")
        PERS_END = pers.off

        d_misc = P.dsem("misc")

        ph = Bump(PERS_END)
        vstage = ph.get([128, NVROWS // 128 + 1, 128], F32)
        B_vst = Buf("vstage")
        nvt = (NVROWS + 127) // 128
        P.dma("sp", lambda e: e.dma_start(out=ident, in_=ident_in[:, :]), d_misc, writes=[B_const])
        for i in range(nvt):
            r0 = i * 128
            r1 = min(NVROWS, r0 + 128)
            dd = P.dsem(f"vst{i}")
            bi = Buf(f"vst{i}")
            P.dma("sp", lambda e, i=i, r0=r0, r1=r1: e.dma_start(out=vstage[0:r1 - r0, i, :], in_=vecs[r0:r1, :]),
                  dd, writes=[bi])
            P.op("pe", lambda e, i=i, r0=r0, r1=r1: e.transpose(out=psb[i % 2][:, 0:r1 - r0],
                                                                  in_=vstage[0:r1 - r0, i, :],
                                                                  identity=ident[0:r1 - r0, 0:r1 - r0]),
                 reads=[bi, B_const], writes=[PS[i % 2]])
            P.op("dve", lambda e, i=i, r0=r0, r1=r1: e.tensor_copy(out=VT[:, r0:r1], in_=psb[i % 2][:, 0:r1 - r0]),
                 reads=[PS[i % 2]], writes=[B_vst])
        P.op("dve", lambda e: e.tensor_copy(out=identb, in_=ident), reads=[B_const], writes=[B_const])
        P.op("dve", lambda e: e.memset(onesb, 1.0), writes=[B_const])
        P.op("dve", lambda e: e.memset(epsc, EPS), writes=[B_const])
        B_VT = B_vst

        WB = {}

        def cast2d(dst, src, rows, cols, dsm):
            piece = cols if cols <= 2048 else (2048 if cols % 2048 == 0 else 1792)
            nrp = 1024
            for r0 in range(0, rows, nrp):
                r1 = min(rows, r0 + nrp)
                s_ = src[r0:r1, :].rearrange("k (a n) -> k a n", n=piece)
                d_ = dst[r0:r1, :].rearrange("k (a n) -> k a n", n=piece)
                P.dma("pool", lambda e, s_=s_, d_=d_: e.dma_start(out=d_, in_=s_), dsm)

        cast_sems = []
        if not big:
            for l in range(n_layers):
                WB[l] = Buf(f"wb{l}")
            P.op("dve", lambda e: e.memset(ADA, 0.0), writes=[Buf("x")])
            P.op("dve", lambda e: e.memset(G1, 1.0), writes=[Buf("x")])
            P.op("dve", lambda e: e.memset(G2, 1.0), writes=[Buf("x")])
        for l in range(n_layers if big else 0):
            dsm = P.dsem(f"cast{l}")
            cast_sems.append(dsm)
            if l % 2 == 0:
                e_ = l // 2
                if with_ab:
                    cast2d(awib[e_], ab_w_in[e_], D, ABIN, dsm)
                    cast2d(awob[e_], ab_w_out[e_], ABOUT, D, dsm)
                    cast2d(glub[e_], glu_w[e_], 512, 512, dsm)
            else:
                o_ = l // 2
                cast2d(cwib[o_], conv_w_in[o_], D, 3 * D, dsm)
                cast2d(cwob[o_], conv_w_out[o_], D, D, dsm)
            cast2d(w1b[l], mlp_w1[l], D, DFF, dsm)
            for n in range(16):
                s_ = mlp_w2[l][:, n * 128:(n + 1) * 128].rearrange("(kc p) m -> p kc m", p=128)
                P.dma("pool", lambda e, s_=s_, n=n, l=l: e.dma_start(out=w2b[l][n], in_=s_), dsm)
            b = Buf(f"wb{l}")
            b.w = ("D", dsm, 16 * dsm.count)
            WB[l] = b

        sc = ph.get([128, 2, KC], F32)
        sc2 = ph.get([128, KC, 2], F32)
        B_sc = Buf("sc")
        P.op("act", lambda e: e.activation(out=sc, in_=VT[:, VOFF["cvec"]:VOFF["cvec"] + 32].rearrange(
            "p (v k) -> p v k", v=2), func=AF.Silu), reads=[B_VT], writes=[B_sc])
        P.op("dve", lambda e: e.tensor_copy(out=sc2, in_=sc.rearrange("p v k -> p k v")), reads=[B_sc], writes=[B_sc])
        NCOL = 512
        awbuf = [ph.get([128, KC, NCOL], F32) for _ in range(2)]
        B_aw = [Buf("aw0"), Buf("aw1")]
        d_aw = [P.dsem("aw0"), P.dsem("aw1")]
        nblk = 6 * D // NCOL
        it = 0
        for l in range(n_layers if big else 0):
            for b in range(nblk):
                s = it % 2
                src = ada_w[l][:, b * NCOL:(b + 1) * NCOL].rearrange("(kc p) n -> p kc n", p=128)
                P.dma("sp", lambda e, s=s, src=src: e.dma_start(out=awbuf[s], in_=src), d_aw[s], writes=[B_aw[s]])
                pb = 2 + (it % 2)

                def mm(e, s=s, b=b, pb=pb):
                    ins = None
                    for jl in range(NCOL // 128):
                        for kc in range(KC):
                            ins = e.matmul(psb[pb][:, jl * 2:jl * 2 + 2], awbuf[s][:, kc, jl * 128:(jl + 1) * 128],
                                           sc2[:, kc, :], start=(kc == 0), stop=(kc == KC - 1))
                    return ins
                P.op("pe", mm, reads=[B_aw[s], B_sc], writes=[PS[pb]])
                j0 = b * (NCOL // 128)
                for cvi in range(2):
                    P.op("dve", lambda e, l=l, j0=j0, pb=pb, cvi=cvi: e.tensor_tensor(
                        out=ADA[:, l, j0:j0 + 4, cvi], in0=psb[pb][:, 0:8].rearrange("p (j v) -> p j v", v=2)[:, :, cvi],
                        in1=VT[:, VOFF["ada_b"] + l * 96 + j0: VOFF["ada_b"] + l * 96 + j0 + 4], op=ALU.add),
                        reads=[PS[pb], B_VT], writes=[B_ada])
                it += 1
        for l in range(n_layers if big else 0):
            for cv in range(2):
                for (G, sidx, nm) in ((G1, 1, "norm1_g"), (G2, 4, "norm2_g")):
                    P.op("dve", lambda e, l=l, cv=cv, G=G, sidx=sidx, nm=nm: e.scalar_tensor_tensor(
                        out=G[:, l, cv, :], in0=ADA[:, l, sidx * 16:(sidx + 1) * 16, cv], scalar=1.0,
                        in1=VT[:, VOFF[nm] + l * 16: VOFF[nm] + (l + 1) * 16], op0=ALU.add, op1=ALU.mult),
                        reads=[B_ada, B_VT], writes=[B_ada])

        def ada_ap(l, sidx, cv, kc):
            return ADA[:, l, sidx * 16 + kc, cv:cv + 1]

        XSB = [Buf(f"xs{t}") for t in range(NT)]
        ZCB = [Buf(f"zc{t}") for t in range(NT)]
        GBB = [Buf(f"gb{t}") for t in range(NT)]
        MIXB = [Buf(f"mixd{t}") for t in range(NT)]
        xin_t = [ph.get([128, D], F32) for _ in range(2)]
        xst = [ph.get([128, KC, 128], F32) for _ in range(2)]
        B_xin = [Buf("xin0"), Buf("xin1")]
        B_xst = [Buf("xst0"), Buf("xst1")]
        d_xin = [P.dsem("xin0"), P.dsem("xin1")]
        d_xst = [P.dsem("xst0"), P.dsem("xst1")]
        for tb in range(NTOK // 128 if big else 0):
            s = tb % 2
            P.dma("sp", lambda e, s=s, tb=tb: e.dma_start(out=xin_t[s], in_=x_in[tb * 128:(tb + 1) * 128, :]),
                  d_xin[s], writes=[B_xin[s]])
            for q in range(4):
                pb = 4 + (tb * 4 + q) % 4

                def tr(e, s=s, q=q, pb=pb):
                    ins = None
                    for i in range(4):
                        kc = q * 4 + i
                        ins = e.transpose(out=psb[pb][:, i * 128:(i + 1) * 128], in_=xin_t[s][:, kc * 128:(kc + 1) * 128],
                                          identity=ident)
                    return ins
                P.op("pe", tr, reads=[B_xin[s], B_const], writes=[PS[pb]])
                eng = "act" if q % 2 == 0 else "dve"
                if eng == "act":
                    P.op("act", lambda e, s=s, q=q, pb=pb: e.copy(
                        out=xst[s][:, q * 4:(q + 1) * 4, :], in_=psb[pb].rearrange("p (a b) -> p a b", a=4)),
                        reads=[PS[pb]], writes=[B_xst[s]])
                else:
                    P.op("dve", lambda e, s=s, q=q, pb=pb: e.tensor_copy(
                        out=xst[s][:, q * 4:(q + 1) * 4, :], in_=psb[pb].rearrange("p (a b) -> p a b", a=4)),
                        reads=[PS[pb]], writes=[B_xst[s]])
            dst = XS[:, tb * 128:(tb + 1) * 128].rearrange("(kc p) t -> p kc t", p=128)
            P.dma("sp", lambda e, s=s, dst=dst: e.dma_start(out=dst, in_=xst[s]), d_xst[s],
                  reads=[B_xst[s]], writes=[XSB[tb // 4]])
        P.barrier()

        def tile_cv(t):
            return 0 if t < NST else 1

        def norm_modulate(l, which, t, xt, B_x, hbuf, B_h, sqtmp, B_sq, rstd, B_r, tmp, B_tmp, psn):
            cv = tile_cv(t)
            G = G1 if which == 1 else G2
            sidx = 0 if which == 1 else 3
            P.op("act", lambda e: e.activation(out=hbuf, in_=xt, func=AF.Square), reads=B_x, writes=[B_h])

            def mm(e):
                ins = None
                for kc in range(KC):
                    ins = e.matmul(psb[psn], onesb, hbuf[:, kc, :], start=(kc == 0), stop=(kc == KC - 1))
                return ins
            P.op("pe", mm, reads=[B_h, B_const], writes=[PS[psn]])
            P.op("act", lambda e: e.activation(out=sqtmp, in_=psb[psn], func=AF.Sqrt, bias=epsc[:, 0:1], scale=1.0 / D),
                 reads=[PS[psn], B_const], writes=[B_sq])
            P.op("dve", lambda e: e.reciprocal(out=rstd, in_=sqtmp), reads=[B_sq], writes=[B_r])
            for kc in range(KC):
                s = kc % 2
                P.op("dve", lambda e, kc=kc, s=s: e.scalar_tensor_tensor(
                    out=tmp[s], in0=xt[:, kc, :], scalar=G[:, l, cv, kc:kc + 1], in1=rstd,
                    op0=ALU.mult, op1=ALU.mult), reads=B_x + [B_r, B_ada], writes=[B_tmp[s]])
                P.op("act", lambda e, kc=kc, s=s: e.activation(
                    out=hbuf[:, kc, :], in_=tmp[s], func=AF.Identity, bias=ada_ap(l, sidx, cv, kc), scale=1.0),
                    reads=[B_tmp[s], B_ada], writes=[B_h])

        def phase_A_conv(l):
            if True:
                ph = Bump(PERS_END)
                xt2 = [ph.get([128, KC, TT], F32) for _ in range(2)]
                B_x2 = [Buf("xA0"), Buf("xA1")]
                d_x2 = [P.dsem(f"xA0_{l}"), P.dsem(f"xA1_{l}")]
                hbuf = ph.get([128, KC, TT], BF16)
                B_h = Buf("hA")
                sqtmp = ph.get([128, TT], F32)
                rstd = ph.get([128, TT], F32)
                tmp = [ph.get([128, TT], F32) for _ in range(2)]
                B_sq, B_r, B_tmp = Buf("sq"), Buf("rstd"), [Buf("tmp0"), Buf("tmp1")]
                wblk = [ph.get([128, KC, 512], BF16) for _ in range(4)]
                B_w = [Buf(f"wA{i}") for i in range(4)]
                d_w = [P.dsem(f"wA{i}_{l}") for i in range(4)]
                tA = [ph.get([128, TT], F32) for _ in range(2)]
                B_tA = [Buf("tA0"), Buf("tA1")]
                zst = [ph.get([128, 4, TT], F32) for _ in range(2)]
                B_zst = [Buf("zst0"), Buf("zst1")]
                d_zst = [P.dsem(f"zst0_{l}"), P.dsem(f"zst1_{l}")]
                wi = 0
                zi = 0
                pbi = 0
                wsrc = cwib[l // 2]

                def load_x(t):
                    s = t % 2
                    src = XS[:, t * TT:(t + 1) * TT].rearrange("(kc p) t -> p kc t", p=128)
                    P.dma("sp", lambda e, s=s, src=src: e.dma_start(out=xt2[s], in_=src), d_x2[s],
                          reads=[XSB[t]], writes=[B_x2[s]])
                load_x(0)
                for t in range(NT):
                    s = t % 2
                    if t + 1 < NT:
                        load_x(t + 1)
                    norm_modulate(l, 1, t, xt2[s], [B_x2[s]], hbuf, B_h, sqtmp, B_sq, rstd, B_r, tmp, B_tmp, 0)
                    for b in range(4):
                        ws = wi % 4
                        wi += 1
                        src = wsrc[:, b * 512:(b + 1) * 512].rearrange("(kc p) n -> p kc n", p=128)
                        P.dma("sp", lambda e, ws=ws, src=src: e.dma_start(out=wblk[ws], in_=src), d_w[ws],
                              reads=[WB[l]], writes=[B_w[ws]])
                        zs = zi % 2
                        zi += 1
                        for jl in range(4):
                            pb = 1 + pbi % 6
                            pbi += 1

                            def mm(e, ws=ws, jl=jl, pb=pb):
                                ins = None
                                for kc in range(KC):
                                    ins = e.matmul(psb[pb], wblk[ws][:, kc, jl * 128:(jl + 1) * 128], hbuf[:, kc, :],
                                                   start=(kc == 0), stop=(kc == KC - 1))
                                return ins
                            P.op("pe", mm, reads=[B_w[ws], B_h], writes=[PS[pb]])
                            P.op("act", lambda e, zs=zs, jl=jl, pb=pb: e.copy(out=zst[zs][:, jl, :], in_=psb[pb]),
                                 reads=[PS[pb]], writes=[B_zst[zs]])
                        dst = GB[b * 512:(b + 1) * 512, t * TT:(t + 1) * TT].rearrange("(c p) t -> p c t", p=128)
                        P.dma("pool", lambda e, zs=zs, dst=dst: e.dma_start(out=dst, in_=zst[zs]), d_zst[zs],
                              reads=[B_zst[zs]], writes=[GBB[t]])
                    for b in range(4):
                        wsa = wi % 4
                        wsb = (wi + 1) % 4
                        wi += 2
                        srca = wsrc[:, 2048 + b * 512:2048 + (b + 1) * 512].rearrange("(kc p) n -> p kc n", p=128)
                        srcb = wsrc[:, 4096 + b * 512:4096 + (b + 1) * 512].rearrange("(kc p) n -> p kc n", p=128)
                        P.dma("sp", lambda e, ws=wsa, src=srca: e.dma_start(out=wblk[ws], in_=src), d_w[wsa],
                              reads=[WB[l]], writes=[B_w[wsa]])
                        P.dma("sp", lambda e, ws=wsb, src=srcb: e.dma_start(out=wblk[ws], in_=src), d_w[wsb],
                              reads=[WB[l]], writes=[B_w[wsb]])
                        zs = zi % 2
                        zi += 1
                        for jl in range(4):
                            pa = 1 + pbi % 6
                            pbb = 1 + (pbi + 1) % 6
                            pbi += 2

                            def mma(e, ws=wsa, jl=jl, pb=pa):
                                ins = None
                                for kc in range(KC):
                                    ins = e.matmul(psb[pb], wblk[ws][:, kc, jl * 128:(jl + 1) * 128], hbuf[:, kc, :],
                                                   start=(kc == 0), stop=(kc == KC - 1))
                                return ins
                            P.op("pe", mma, reads=[B_w[wsa], B_h], writes=[PS[pa]])
                            P.op("pe", lambda e, ws=wsb, jl=jl, pb=pbb: mma(e, ws, jl, pb), reads=[B_w[wsb], B_h],
                                 writes=[PS[pbb]])
                            ts = jl % 2
                            P.op("act", lambda e, ts=ts, pb=pa: e.copy(out=tA[ts], in_=psb[pb]),
                                 reads=[PS[pa]], writes=[B_tA[ts]])
                            P.op("dve", lambda e, ts=ts, zs=zs, jl=jl, pb=pbb: e.tensor_tensor(
                                out=zst[zs][:, jl, :], in0=tA[ts], in1=psb[pb], op=ALU.mult),
                                reads=[B_tA[ts], PS[pbb]], writes=[B_zst[zs]])
                        dst = ZC[b * 512:(b + 1) * 512, t * TT:(t + 1) * TT].rearrange("(c p) t -> p c t", p=128)
                        P.dma("pool", lambda e, zs=zs, dst=dst: e.dma_start(out=dst, in_=zst[zs]), d_zst[zs],
                              reads=[B_zst[zs]], writes=[ZCB[t]])
                P.barrier()

        def phase_D(l):
            is_ab = (l % 2 == 0)
            e_ = l // 2
            ph = Bump(PERS_END)
            xt = ph.get([128, KC, TT], F32)
            B_xk = [Buf(f"xk{k}") for k in range(KC)]
            d_x = P.dsem(f"xD_{l}")
            hbuf = ph.get([128, KC, TT], BF16)
            B_h = Buf("hD")
            mix_off = ph.off
            mixb = ph.get([128, KC, TT], BF16)
            B_mix = [Buf(f"mix{k}") for k in range(KC)]
            d_mix = P.dsem(f"mix_{l}")
            abuf = ph.get([128, 64, TT], BF16)
            B_a = [Buf(f"a{k}") for k in range(64)]
            sqtmp = ph.get([128, TT], F32)
            rstd = ph.get([128, TT], F32)
            tmp = [ph.get([128, TT], F32) for _ in range(2)]
            B_sq, B_r, B_tmp = Buf("sq"), Buf("rstd"), [Buf("tmp0"), Buf("tmp1")]
            wblk = [ph.get([128, KC, 512], BF16) for _ in range(2)]
            B_w = [Buf("wD0"), Buf("wD1")]
            d_w = [P.dsem(f"wD0_{l}"), P.dsem(f"wD1_{l}")]
            rl = [ph.get([128, TT], F32) for _ in range(2)]
            B_rl = [Buf("rl0"), Buf("rl1")]
            if not is_ab:
                zcb = [ph.get([128, TT + 2], F32) for _ in range(2)]
                gbb = [ph.get([128, TT], F32) for _ in range(2)]
                B_zc = [Buf("zcb0"), Buf("zcb1")]
                B_gb = [Buf("gbb0"), Buf("gbb1")]
                d_zc = [P.dsem(f"zcb0_{l}"), P.dsem(f"zcb1_{l}")]
                d_gb = [P.dsem(f"gbb0_{l}"), P.dsem(f"gbb1_{l}")]
                cv1 = [ph.get([128, TT], F32) for _ in range(2)]
                B_cv = [Buf("cv0"), Buf("cv1")]
            last = (l == n_layers - 1)
            if last:
                ost = [view(mix_off + i * 8192, [128, D], F32) for i in range(2)]
                B_ost = [B_mix[0:8], B_mix[8:16]]
                d_ost = [P.dsem("ost0"), P.dsem("ost1")]
                oi = 0
            wi = 0
            pbi = 0
            for t in range(NT):
                cv = tile_cv(t)
                src = XS[:, t * TT:(t + 1) * TT].rearrange("(kc p) t -> p kc t", p=128)
                P.dma("sp", lambda e, src=src: e.dma_start(out=xt, in_=src), d_x, reads=[XSB[t]], writes=B_xk)
                have_mix = True
                if is_ab:
                    have_mix = with_ab
                    KM = 12
                    if have_mix:
                        srcm = MIX[:, t * TT:(t + 1) * TT].rearrange("(c p) t -> p c t", p=128)
                        P.dma("sp", lambda e, srcm=srcm: e.dma_start(out=mixb[:, 0:12, :], in_=srcm), d_mix,
                              reads=[MIXB[t]] if with_ab else [], writes=B_mix)
                    wsrc = awob[e_]
                else:
                    KM = 16
                    o_ = l // 2
                    wsrc = cwob[o_]
                    segs = [(0, TT)] if t < NST else [(0, 256), (256, 256)]
                    cwo = VOFF["conv_w"] + o_ * 48
                    cbo = VOFF["conv_b"] + o_ * 16
                    for j in range(KC):
                        s = j % 2
                        t0 = t * TT
                        lo_ok = (t < NST and t > 0)
                        hi_ok = (t < NST - 1)
                        c0 = t0 - (1 if lo_ok else 0)
                        c1 = t0 + TT + (1 if hi_ok else 0)
                        srcz = ZC[j * 128:(j + 1) * 128, c0:c1]
                        o0 = 0 if lo_ok else 1
                        rd = [ZCB[t]]
                        if lo_ok:
                            rd.append(ZCB[t - 1])
                        if hi_ok:
                            rd.append(ZCB[t + 1])
                        if not lo_ok:
                            P.op("pool", lambda e, s=s: e.memset(zcb[s][:, 0:1], 0.0), writes=[B_zc[s]])
                        if not hi_ok:
                            P.op("pool", lambda e, s=s: e.memset(zcb[s][:, TT + 1:TT + 2], 0.0), writes=[B_zc[s]])
                        P.dma("sp", lambda e, s=s, srcz=srcz, o0=o0, n=c1 - c0: e.dma_start(
                            out=zcb[s][:, o0:o0 + n], in_=srcz), d_zc[s], reads=rd, writes=[B_zc[s]])
                        srcg = GB[j * 128:(j + 1) * 128, t0:t0 + TT]
                        P.dma("sp", lambda e, s=s, srcg=srcg: e.dma_start(out=gbb[s], in_=srcg), d_gb[s],
                              reads=[GBB[t]], writes=[B_gb[s]])
                        w0 = VT[:, cwo + 0 * 16 + j: cwo + 0 * 16 + j + 1]
                        w1 = VT[:, cwo + 1 * 16 + j: cwo + 1 * 16 + j + 1]
                        w2 = VT[:, cwo + 2 * 16 + j: cwo + 2 * 16 + j + 1]
                        cb = VT[:, cbo + j: cbo + j + 1]
                        P.op("dve", lambda e, s=s, w1=w1, cb=cb: e.tensor_scalar(
                            out=cv1[s], in0=zcb[s][:, 1:TT + 1], scalar1=w1, scalar2=cb, op0=ALU.mult, op1=ALU.add),
                            reads=[B_zc[s], B_VT], writes=[B_cv[s]])
                        for (a0, ln) in segs:
                            lskip = 1 if (a0 > 0) else 0
                            P.op("dve", lambda e, s=s, w0=w0, a0=a0, ln=ln, lskip=lskip: e.scalar_tensor_tensor(
                                out=cv1[s][:, a0 + lskip:a0 + ln], in0=zcb[s][:, a0 + lskip:a0 + ln], scalar=w0,
                                in1=cv1[s][:, a0 + lskip:a0 + ln], op0=ALU.mult, op1=ALU.add),
                                reads=[B_zc[s], B_VT, B_cv[s]], writes=[B_cv[s]])
                            rskip = 1 if (a0 + ln < TT) else 0
                            P.op("dve", lambda e, s=s, w2=w2, a0=a0, ln=ln, rskip=rskip: e.scalar_tensor_tensor(
                                out=cv1[s][:, a0:a0 + ln - rskip], in0=zcb[s][:, a0 + 2:a0 + 2 + ln - rskip], scalar=w2,
                                in1=cv1[s][:, a0:a0 + ln - rskip], op0=ALU.mult, op1=ALU.add),
                                reads=[B_zc[s], B_VT, B_cv[s]], writes=[B_cv[s]])
                        P.op("pool", lambda e, s=s, j=j: e.tensor_tensor(
                            out=mixb[:, j, :], in0=cv1[s], in1=gbb[s], op=ALU.mult),
                            reads=[B_cv[s], B_gb[s]], writes=[B_mix[j]])
                if have_mix:
                    for b in range(4):
                        ws = wi % 2
                        wi += 1
                        src = wsrc[:, b * 512:(b + 1) * 512].rearrange("(kc p) n -> p kc n", p=128)
                        P.dma("sp", lambda e, ws=ws, src=src, KM=KM: e.dma_start(out=wblk[ws][:, 0:KM, :], in_=src),
                              d_w[ws], reads=[WB[l]], writes=[B_w[ws]])
                        for jl in range(4):
                            j = b * 4 + jl
                            pb = 1 + pbi % 6
                            pbi += 1

                            def mm(e, ws=ws, jl=jl, pb=pb, KM=KM):
                                ins = None
                                for kc in range(KM):
                                    ins = e.matmul(psb[pb], wblk[ws][:, kc, jl * 128:(jl + 1) * 128], mixb[:, kc, :],
                                                   start=(kc == 0), stop=(kc == KM - 1))
                                return ins
                            P.op("pe", mm, reads=[B_w[ws]] + B_mix[0:KM], writes=[PS[pb]])
                            P.op("dve", lambda e, j=j, pb=pb, cv=cv: e.scalar_tensor_tensor(
                                out=xt[:, j, :], in0=psb[pb], scalar=ada_ap(l, 2, cv, j), in1=xt[:, j, :],
                                op0=ALU.mult, op1=ALU.add), reads=[PS[pb], B_ada, B_xk[j]], writes=[B_xk[j]])
                norm_modulate(l, 2, t, xt, B_xk, hbuf, B_h, sqtmp, B_sq, rstd, B_r, tmp, B_tmp, 0)
                for b in range(16):
                    ws = wi % 2
                    wi += 1
                    src = w1b[l][:, b * 512:(b + 1) * 512].rearrange("(kc p) n -> p kc n", p=128)
                    P.dma("sp", lambda e, ws=ws, src=src: e.dma_start(out=wblk[ws], in_=src), d_w[ws],
                          reads=[WB[l]], writes=[B_w[ws]])
                    for jl in range(4):
                        j = b * 4 + jl
                        pb = 1 + pbi % 6
                        pbi += 1

                        def mm(e, ws=ws, jl=jl, pb=pb):
                            ins = None
                            for kc in range(KC):
                                ins = e.matmul(psb[pb], wblk[ws][:, kc, jl * 128:(jl + 1) * 128], hbuf[:, kc, :],
                                               start=(kc == 0), stop=(kc == KC - 1))
                            return ins
                        P.op("pe", mm, reads=[B_w[ws], B_h], writes=[PS[pb]])
                        rs = j % 2
                        P.op("act", lambda e, rs=rs, pb=pb: e.activation(out=rl[rs], in_=psb[pb], func=AF.Relu),
                             reads=[PS[pb]], writes=[B_rl[rs]])
                        P.op("pool", lambda e, rs=rs, j=j: e.tensor_tensor(out=abuf[:, j, :], in0=rl[rs], in1=rl[rs],
                                                                           op=ALU.mult),
                             reads=[B_rl[rs]], writes=[B_a[j]])
                for n in range(16):
                    ws = wi % 2
                    wi += 1
                    P.dma("sp", lambda e, ws=ws, n=n: e.dma_start(
                        out=wblk[ws].rearrange("p a b -> p (a b)").rearrange("p (k m) -> p k m", m=128),
                        in_=w2b[l][n]), d_w[ws], reads=[WB[l]], writes=[B_w[ws]])
                    pb = 1 + pbi % 6
                    pbi += 1

                    def mm(e, ws=ws, pb=pb):
                        w = wblk[ws].rearrange("p a b -> p (a b)").rearrange("p (k m) -> p k m", m=128)
                        ins = None
                        for kc in range(64):
                            ins = e.matmul(psb[pb], w[:, kc, :], abuf[:, kc, :], start=(kc == 0), stop=(kc == 63))
                        return ins
                    P.op("pe", mm, reads=[B_w[ws]] + B_a, writes=[PS[pb]])
                    P.op("dve", lambda e, n=n, pb=pb, cv=cv: e.scalar_tensor_tensor(
                        out=xt[:, n, :], in0=psb[pb], scalar=ada_ap(l, 5, cv, n), in1=xt[:, n, :],
                        op0=ALU.mult, op1=ALU.add), reads=[PS[pb], B_ada, B_xk[n]], writes=[B_xk[n]])
                if not last:
                    dst = XS[:, t * TT:(t + 1) * TT].rearrange("(kc p) t -> p kc t", p=128)
                    P.dma("pool", lambda e, dst=dst: e.dma_start(out=dst, in_=xt), d_x, reads=B_xk, writes=[XSB[t]])
                else:
                    for tb in range(4):
                        os_ = oi % 2
                        oi += 1
                        for q in range(4):
                            pb = 1 + pbi % 6
                            pbi += 1

                            def tr(e, tb=tb, q=q, pb=pb):
                                ins = None
                                for i in range(4):
                                    kc = q * 4 + i
                                    ins = e.transpose(out=psb[pb][:, i * 128:(i + 1) * 128],
                                                      in_=xt[:, kc, tb * 128:(tb + 1) * 128], identity=ident)
                                return ins
                            P.op("pe", tr, reads=B_xk[q * 4:q * 4 + 4] + [B_const], writes=[PS[pb]])
                            if q % 2 == 0:
                                P.op("act", lambda e, os_=os_, q=q, pb=pb: e.copy(
                                    out=ost[os_][:, q * 512:(q + 1) * 512], in_=psb[pb]),
                                    reads=[PS[pb]], writes=B_ost[os_])
                            else:
                                P.op("dve", lambda e, os_=os_, q=q, pb=pb: e.tensor_copy(
                                    out=ost[os_][:, q * 512:(q + 1) * 512], in_=psb[pb]),
                                    reads=[PS[pb]], writes=B_ost[os_])
                        r0 = t * TT + tb * 128
                        P.dma("pool", lambda e, os_=os_, r0=r0: e.dma_start(out=y_out[r0:r0 + 128, :], in_=ost[os_]),
                              d_ost[os_], reads=B_ost[os_])
            P.barrier()


        B_U = [Buf(f"U{t}") for t in range(NT)]
        B_Q = [Buf(f"Q{t}") for t in range(NT)]
        B_K = [Buf(f"K{t}") for t in range(NT)]
        B_V = [Buf(f"V{t}") for t in range(NT)]
        B_BIAS = Buf("BIAS")
        P.op("dve", lambda e: e.tensor_scalar(out=QKG[:, 0:2], in0=VT[:, VOFF["q_g"]:VOFF["q_g"] + 2],
                                              scalar1=float(128 ** -0.5), scalar2=None, op0=ALU.mult),
             reads=[B_VT], writes=[B_ada])
        P.op("dve", lambda e: e.tensor_copy(out=QKG[:, 2:4], in_=VT[:, VOFF["k_g"]:VOFF["k_g"] + 2]),
             reads=[B_VT], writes=[B_ada])

        def phase_A_ab(l):
            e_ = l // 2
            ph = Bump(PERS_END)
            xt2 = [ph.get([128, KC, TT], F32) for _ in range(2)]
            B_x2 = [Buf("xA0"), Buf("xA1")]
            d_x2 = [P.dsem(f"xB0_{l}"), P.dsem(f"xB1_{l}")]
            hbuf = ph.get([128, KC, TT], BF16)
            B_h = Buf("hA")
            sqtmp = ph.get([128, TT], F32)
            rstd = ph.get([128, TT], F32)
            tmp = [ph.get([128, TT], F32) for _ in range(2)]
            B_sq, B_r, B_tmp = Buf("sq"), Buf("rstd"), [Buf("tmp0"), Buf("tmp1")]
            wblk = [ph.get([128, KC, 512], BF16) for _ in range(2)]
            B_w = [Buf("wB0"), Buf("wB1")]
            d_w = [P.dsem(f"wB0_{l}"), P.dsem(f"wB1_{l}")]
            stg = [ph.get([128, 4, TT], BF16) for _ in range(2)]
            B_stg = [Buf("stg0"), Buf("stg1")]
            d_stg = [P.dsem(f"stg0_{l}"), P.dsem(f"stg1_{l}")]
            hsq = [ph.get([128, TT], BF16) for _ in range(2)]
            B_hsq = [Buf("hsq0"), Buf("hsq1")]
            t1 = [ph.get([128, TT], F32) for _ in range(2)]
            B_t1 = [Buf("t10"), Buf("t11")]
            r1 = [ph.get([128, TT], F32) for _ in range(2)]
            B_r1 = [Buf("r10"), Buf("r11")]
            kf = [ph.get([128, TT], F32) for _ in range(2)]
            B_kf = [Buf("kf0"), Buf("kf1")]
            ktok = [ph.get([128, 4, 128], F32) for _ in range(2)]
            B_ktok = [Buf("ktok0"), Buf("ktok1")]
            d_ktok = [P.dsem(f"ktok0_{l}"), P.dsem(f"ktok1_{l}")]
            vf = [ph.get([128, TT], F32) for _ in range(2)]
            B_vf = [Buf("vf0"), Buf("vf1")]
            d_vf = [P.dsem(f"vf0_{l}"), P.dsem(f"vf1_{l}")]
            cnt = {"w": 0, "pb": 0, "st": 0, "hn": 0, "kt": 0, "vf": 0, "pn": 0}

            def load_x(t):
                s = t % 2
                src = XS[:, t * TT:(t + 1) * TT].rearrange("(kc p) t -> p kc t", p=128)
                P.dma("sp", lambda e, s=s, src=src: e.dma_start(out=xt2[s], in_=src), d_x2[s],
                      reads=[XSB[t]], writes=[B_x2[s]])

            def load_w(b):
                ws = cnt["w"] % 2
                cnt["w"] += 1
                src = awib[e_][:, b * 512:(b + 1) * 512].rearrange("(kc p) n -> p kc n", p=128)
                P.dma("sp", lambda e, ws=ws, src=src: e.dma_start(out=wblk[ws], in_=src), d_w[ws],
                      reads=[WB[l]], writes=[B_w[ws]])
                return ws

            def proj_fm(ws, jl):
                pb = 1 + cnt["pb"] % 4
                cnt["pb"] += 1

                def mm(e, ws=ws, jl=jl, pb=pb):
                    ins = None
                    for kc in range(KC):
                        ins = e.matmul(psb[pb], wblk[ws][:, kc, jl * 128:(jl + 1) * 128], hbuf[:, kc, :],
                                       start=(kc == 0), stop=(kc == KC - 1))
                    return ins
                P.op("pe", mm, reads=[B_w[ws], B_h], writes=[PS[pb]])
                return pb

            load_x(0)
            for t in range(NT):
                s = t % 2
                is_prompt = t >= NST
                if t + 1 < NT:
                    load_x(t + 1)
                norm_modulate(l, 1, t, xt2[s], [B_x2[s]], hbuf, B_h, sqtmp, B_sq, rstd, B_r, tmp, B_tmp, 0)
                ws = load_w(0)
                ss = cnt["st"] % 2
                cnt["st"] += 1
                for jl in range(4):
                    pb = proj_fm(ws, jl)
                    P.op("act", lambda e, ss=ss, jl=jl, pb=pb: e.copy(out=stg[ss][:, jl, :], in_=psb[pb]),
                         reads=[PS[pb]], writes=[B_stg[ss]])
                dst = Usc[:, t * TT:(t + 1) * TT].rearrange("(c p) t -> p c t", p=128)
                P.dma("pool", lambda e, ss=ss, dst=dst: e.dma_start(out=dst, in_=stg[ss]), d_stg[ss],
                      reads=[B_stg[ss]], writes=[B_U[t]])
                for which in range(0 if "Aqk" not in DBG_SKIP else 2, 2):
                    for hb in range(2):
                        ws = load_w(1 + which * 2 + hb)
                        ss = cnt["st"] % 2
                        cnt["st"] += 1
                        for jl in range(4):
                            h = hb * 4 + jl
                            pb = proj_fm(ws, jl)
                            hs = cnt["hn"] % 2
                            cnt["hn"] += 1
                            pn = 5 + cnt["pn"] % 2
                            cnt["pn"] += 1
                            P.op("act", lambda e, hs=hs, pb=pb: e.activation(out=hsq[hs], in_=psb[pb], func=AF.Square),
                                 reads=[PS[pb]], writes=[B_hsq[hs]])
                            P.op("pe", lambda e, hs=hs, pn=pn: e.matmul(psb[pn], onesb, hsq[hs], start=True, stop=True),
                                 reads=[B_hsq[hs], B_const], writes=[PS[pn]])
                            P.op("act", lambda e, hs=hs, pn=pn: e.activation(out=t1[hs], in_=psb[pn], func=AF.Sqrt,
                                                                             bias=epsc[:, 0:1], scale=1.0 / 128),
                                 reads=[PS[pn], B_const], writes=[B_t1[hs]])
                            P.op("dve", lambda e, hs=hs: e.reciprocal(out=r1[hs], in_=t1[hs]),
                                 reads=[B_t1[hs]], writes=[B_r1[hs]])
                            gcol = which * 2 + e_
                            if which == 1 and is_prompt and "Akt" not in DBG_SKIP:
                                P.op("dve", lambda e, hs=hs, pb=pb, gcol=gcol: e.scalar_tensor_tensor(
                                    out=kf[hs], in0=psb[pb], scalar=QKG[:, gcol:gcol + 1], in1=r1[hs],
                                    op0=ALU.mult, op1=ALU.mult), reads=[PS[pb], B_r1[hs], B_ada], writes=[B_kf[hs]])
                                P.op("act", lambda e, hs=hs, ss=ss, jl=jl: e.copy(out=stg[ss][:, jl, :], in_=kf[hs]),
                                     reads=[B_kf[hs]], writes=[B_stg[ss]])
                                ks = cnt["kt"] % 2
                                cnt["kt"] += 1

                                def tr(e, hs=hs):
                                    ins = None
                                    for tb in range(4):
                                        ins = e.transpose(out=psb[7][:, tb * 128:(tb + 1) * 128],
                                                          in_=kf[hs][:, tb * 128:(tb + 1) * 128], identity=ident)
                                    return ins
                                P.op("pe", tr, reads=[B_kf[hs], B_const], writes=[PS[7]])
                                P.op("dve", lambda e, ks=ks: e.tensor_copy(
                                    out=ktok[ks], in_=psb[7].rearrange("p (a b) -> p a b", a=4)),
                                    reads=[PS[7]], writes=[B_ktok[ks]])
                                s0 = (t - NST) * 2
                                for sq2 in range(2):
                                    dstk = nck[s0 + sq2, e_, :, h, :].rearrange("(tb p) d -> p tb d", p=128)
                                    P.dma("pool", lambda e, ks=ks, dstk=dstk, sq2=sq2: e.dma_start(
                                        out=dstk, in_=ktok[ks][:, 2 * sq2:2 * sq2 + 2, :]), d_ktok[ks],
                                        reads=[B_ktok[ks]])
                            else:
                                P.op("dve", lambda e, hs=hs, pb=pb, gcol=gcol, ss=ss, jl=jl: e.scalar_tensor_tensor(
                                    out=stg[ss][:, jl, :], in0=psb[pb], scalar=QKG[:, gcol:gcol + 1], in1=r1[hs],
                                    op0=ALU.mult, op1=ALU.mult), reads=[PS[pb], B_r1[hs], B_ada], writes=[B_stg[ss]])
                        dsc = Qsc if which == 0 else Ksc
                        dst = dsc[hb * 512:(hb + 1) * 512, t * TT:(t + 1) * TT].rearrange("(c p) t -> p c t", p=128)
                        P.dma("pool", lambda e, ss=ss, dst=dst: e.dma_start(out=dst, in_=stg[ss]), d_stg[ss],
                              reads=[B_stg[ss]], writes=[(B_Q if which == 0 else B_K)[t]])
                for ch in range(2 if "Av" not in DBG_SKIP else 0):
                    ws = load_w(5 + ch)
                    ss = cnt["st"] % 2
                    cnt["st"] += 1
                    for tb in range(4):
                        pb = 1 + cnt["pb"] % 4
                        cnt["pb"] += 1

                        def mmv(e, ws=ws, tb=tb, pb=pb):
                            ins = None
                            for kc in range(KC):
                                ins = e.matmul(psb[pb], hbuf[:, kc, tb * 128:(tb + 1) * 128], wblk[ws][:, kc, :],
                                               start=(kc == 0), stop=(kc == KC - 1))
                            return ins
                        P.op("pe", mmv, reads=[B_w[ws], B_h], writes=[PS[pb]])
                        if is_prompt and "Avf" not in DBG_SKIP:
                            vs = cnt["vf"] % 2
                            cnt["vf"] += 1
                            P.op("act", lambda e, vs=vs, pb=pb: e.copy(out=vf[vs], in_=psb[pb]),
                                 reads=[PS[pb]], writes=[B_vf[vs]])
                            P.op("dve", lambda e, vs=vs, ss=ss, tb=tb: e.tensor_copy(out=stg[ss][:, tb, :], in_=vf[vs]),
                                 reads=[B_vf[vs]], writes=[B_stg[ss]])
                            sq_ = (t - NST) * 2 + tb // 2
                            dstv = ncv[sq_, e_, (tb % 2) * 128:(tb % 2) * 128 + 128, ch * 4:(ch + 1) * 4, :].rearrange(
                                "p h d -> p (h d)")
                            P.dma("pool", lambda e, vs=vs, dstv=dstv: e.dma_start(out=dstv, in_=vf[vs]), d_vf[vs],
                                  reads=[B_vf[vs]])
                        else:
                            P.op("act", lambda e, ss=ss, tb=tb, pb=pb: e.copy(out=stg[ss][:, tb, :], in_=psb[pb]),
                                 reads=[PS[pb]], writes=[B_stg[ss]])
                    dst = Vsc[t * TT:(t + 1) * TT, ch * 512:(ch + 1) * 512].rearrange("(tb p) c -> p tb c", p=128)
                    P.dma("pool", lambda e, ss=ss, dst=dst: e.dma_start(out=dst, in_=stg[ss]), d_stg[ss],
                          reads=[B_stg[ss]], writes=[B_V[t]])
            P.barrier()


        POWS = [0, 1, 2, 3, 4, 5, 6, 7, 8, 16, 32, 64, 128, 256, 512, 1024, 2048]
        NPW = len(POWS)
        TWO_PI = 6.283185307179586
        PI_ = 3.141592653589793
        B_UF = {(g, fc): Buf(f"uf{g}{fc}") for g in range(2) for fc in range(4)}

        def phase_S5(l):
            e_ = l // 2
            ph = Bump(PERS_END)
            WA = ph.get([128, 2, 2, 4, 8, 128], BF16)
            WC = ph.get([128, 2, 2, 8, 4, 160], BF16)
            WK = ph.get([128, 2, 8, 4, 128], BF16)
            PR = ph.get([128, 2, NPW, 16], F32)
            PIm = ph.get([128, 2, NPW, 16], F32)
            NPI = ph.get([128, 2, NPW, 16], F32)
            H0 = ph.get([128, 2, 2, 16], F32)
            ZER = ph.get([128, 8], F32)
            HPI = ph.get([128, 1], F32)
            W_END = ph.off
            B_W = Buf("s5w")
            B_pow = Buf("s5pow")
            SPt = ph.get([128, 2, S5P_COLS], F32)
            B_SP = Buf("s5sp")
            d_sp = P.dsem(f"s5sp_{l}")
            for d in range(2):
                P.dma("sp", lambda e, d=d: e.dma_start(out=SPt[:, d, :], in_=s5p[e_, d]), d_sp, writes=[B_SP])
                P.dma("sp", lambda e, d=d: e.dma_start(out=H0[:, d, :, :], in_=s5h0[e_, d].rearrange("r p g -> p r g")),
                      d_sp, writes=[B_SP])
            P.op("pool", lambda e: e.memset(ZER, 0.0), writes=[B_pow])
            P.op("pool", lambda e: e.memset(HPI, PI_ / 2), writes=[B_pow])
            P.op("pool", lambda e: e.memset(WC, 0.0), writes=[B_W])
            lamr, lami, ldt = SPt[:, :, 0:16], SPt[:, :, 16:32], SPt[:, :, 32:48]
            sm = [ph.get([128, 2, 16], F32) for _ in range(12)]
            dt_, ar_, ai_, t1_, t2_, den_, nr_, fre, fim, a1_, a2_, _x = sm
            big_ = [ph.get([128, 2, NPW, 16], F32) for _ in range(5)]
            ANG, MGL, TF, Rr, M1 = big_
            TI = ph.get([128, 2, NPW, 16], F32).bitcast(mybir.dt.int32)

            def f2(a):
                return a.rearrange("p d n g -> p (d n g)")
            Bp = B_pow
            P.op("act", lambda e: e.activation(out=dt_, in_=ldt, func=AF.Exp), reads=[B_SP], writes=[Bp])
            P.op("dve", lambda e: e.tensor_tensor(out=ar_, in0=lamr, in1=dt_, op=ALU.mult), reads=[B_SP, Bp], writes=[Bp])
            P.op("dve", lambda e: e.tensor_tensor(out=ai_, in0=lami, in1=dt_, op=ALU.mult), reads=[B_SP, Bp], writes=[Bp])
            for i, n in enumerate(POWS):
                P.op("dve", lambda e, i=i, n=n: e.tensor_scalar(out=ANG[:, :, i, :], in0=ai_, scalar1=float(n), scalar2=None,
                                                                op0=ALU.mult), reads=[Bp], writes=[Bp])
                P.op("dve", lambda e, i=i, n=n: e.tensor_scalar(out=MGL[:, :, i, :], in0=ar_, scalar1=float(n), scalar2=None,
                                                                op0=ALU.mult), reads=[Bp], writes=[Bp])
            P.op("dve", lambda e: e.tensor_scalar(out=f2(TF), in0=f2(ANG), scalar1=1.0 / TWO_PI, scalar2=None, op0=ALU.mult),
                 reads=[Bp], writes=[Bp])
            P.op("dve", lambda e: e.tensor_copy(out=f2(TI), in_=f2(TF)), reads=[Bp], writes=[Bp])
            P.op("dve", lambda e: e.tensor_copy(out=f2(TF), in_=f2(TI)), reads=[Bp], writes=[Bp])
            P.op("dve", lambda e: e.scalar_tensor_tensor(out=f2(Rr), in0=f2(TF), scalar=-TWO_PI, in1=f2(ANG), op0=ALU.mult,
                                                         op1=ALU.add), reads=[Bp], writes=[Bp])
            P.op("dve", lambda e: e.tensor_scalar(out=f2(M1), in0=f2(Rr), scalar1=PI_, scalar2=-TWO_PI, op0=ALU.is_gt,
                                                  op1=ALU.mult), reads=[Bp], writes=[Bp])
            P.op("dve", lambda e: e.tensor_tensor(out=f2(Rr), in0=f2(Rr), in1=f2(M1), op=ALU.add), reads=[Bp], writes=[Bp])
            P.op("dve", lambda e: e.tensor_scalar(out=f2(M1), in0=f2(Rr), scalar1=-PI_, scalar2=TWO_PI, op0=ALU.is_lt,
                                                  op1=ALU.mult), reads=[Bp], writes=[Bp])
            P.op("dve", lambda e: e.tensor_tensor(out=f2(Rr), in0=f2(Rr), in1=f2(M1), op=ALU.add), reads=[Bp], writes=[Bp])
            P.op("act", lambda e: e.activation(out=f2(TF), in_=f2(Rr), func=AF.Sin), reads=[Bp], writes=[Bp])
            P.op("dve", lambda e: e.tensor_scalar(out=f2(M1), in0=f2(Rr), scalar1=-1.0, scalar2=None, op0=ALU.mult),
                 reads=[Bp], writes=[Bp])
            P.op("dve", lambda e: e.tensor_tensor(out=f2(M1), in0=f2(M1), in1=f2(Rr), op=ALU.max), reads=[Bp], writes=[Bp])
            P.op("act", lambda e: e.activation(out=f2(ANG), in_=f2(M1), func=AF.Sin, bias=HPI[:, 0:1], scale=-1.0),
                 reads=[Bp], writes=[Bp])
            P.op("act", lambda e: e.activation(out=f2(MGL), in_=f2(MGL), func=AF.Exp), reads=[Bp], writes=[Bp])
            P.op("dve", lambda e: e.tensor_tensor(out=f2(PR), in0=f2(MGL), in1=f2(ANG), op=ALU.mult), reads=[Bp], writes=[Bp])
            P.op("dve", lambda e: e.tensor_tensor(out=f2(PIm), in0=f2(MGL), in1=f2(TF), op=ALU.mult), reads=[Bp], writes=[Bp])
            P.op("dve", lambda e: e.tensor_scalar(out=f2(NPI), in0=f2(PIm), scalar1=-1.0, scalar2=None, op0=ALU.mult),
                 reads=[Bp], writes=[Bp])
            pr1, pi1 = PR[:, :, 1, :], PIm[:, :, 1, :]

            def tt(out, a, b, op):
                P.op("dve", lambda e: e.tensor_tensor(out=out, in0=a, in1=b, op=op), reads=[Bp, B_SP], writes=[Bp])
            tt(t1_, lamr, lamr, ALU.mult)
            tt(t2_, lami, lami, ALU.mult)
            tt(den_, t1_, t2_, ALU.add)
            P.op("dve", lambda e: e.reciprocal(out=den_, in_=den_), reads=[Bp], writes=[Bp])
            P.op("dve", lambda e: e.tensor_scalar(out=nr_, in0=pr1, scalar1=-1.0, scalar2=None, op0=ALU.add),
                 reads=[Bp], writes=[Bp])
            tt(a1_, nr_, lamr, ALU.mult)
            tt(a2_, pi1, lami, ALU.mult)
            tt(fre, a1_, a2_, ALU.add)
            tt(fre, fre, den_, ALU.mult)
            tt(a1_, pi1, lamr, ALU.mult)
            tt(a2_, nr_, lami, ALU.mult)
            tt(fim, a1_, a2_, ALU.subtract)
            tt(fim, fim, den_, ALU.mult)
            ER = ph.get([128, 2, 8, 16], F32)
            EI = ph.get([128, 2, 8, 16], F32)
            e1 = ph.get([128, 2, 8, 16], F32)
            e2 = ph.get([128, 2, 8, 16], F32)
            freb = fre.unsqueeze(2).broadcast_to([128, 2, 8, 16])
            fimb = fim.unsqueeze(2).broadcast_to([128, 2, 8, 16])
            tt(e1, PR[:, :, 0:8, :], freb, ALU.mult)
            tt(e2, PIm[:, :, 0:8, :], fimb, ALU.mult)
            tt(ER, e1, e2, ALU.subtract)
            tt(e1, PR[:, :, 0:8, :], fimb, ALU.mult)
            tt(e2, PIm[:, :, 0:8, :], freb, ALU.mult)
            tt(EI, e1, e2, ALU.add)
            NAT = [[ph.get([128, 16, 32], F32) for _ in range(2)] for _ in range(2)]
            B_NAT = [Buf("nat0"), Buf("nat1")]
            q1 = [ph.get([128, 16, 32], F32) for _ in range(2)]
            BB = [[ph.get([128, 16, 32], F32) for _ in range(2)] for _ in range(2)]
            B_BB = Buf("bb")
            B_q = Buf("q1")

            def bc(a):
                return a.unsqueeze(2).broadcast_to([128, 16, 32])
            it = 0
            for d in range(2):
                Br = SPt[:, d, 48:560].rearrange("p (g c) -> p g c", c=32)
                Bi = SPt[:, d, 560:1072].rearrange("p (g c) -> p g c", c=32)
                for n in range(8):
                    st_ = it % 2
                    it += 1
                    er, ei = bc(ER[:, d, n, :]), bc(EI[:, d, n, :])
                    Nr, Ni = NAT[st_]
                    rd = [Bp, B_SP, B_q]

                    def o(eng, out, a, b, op, rds, wrs):
                        P.op(eng, lambda e: e.tensor_tensor(out=out, in0=a, in1=b, op=op), reads=rds, writes=wrs)
                    o("dve", q1[0], Br, er, ALU.mult, rd, [B_q])
                    o("dve", q1[1], Bi, ei, ALU.mult, rd, [B_q])
                    o("dve", Nr, q1[0], q1[1], ALU.subtract, [B_q], [B_NAT[st_]])
                    o("dve", q1[0], Bi, er, ALU.mult, rd, [B_q])
                    o("dve", q1[1], Br, ei, ALU.mult, rd, [B_q])
                    o("dve", Ni, q1[0], q1[1], ALU.add, [B_q], [B_NAT[st_]])
                    if n == 0:
                        for r_ in range(2):
                            P.op("pool", lambda e, d=d, r_=r_, st_=st_: e.tensor_copy(out=BB[d][r_], in_=NAT[st_][r_]),
                                 reads=[B_NAT[st_]], writes=[B_BB])
                    s_idx = (7 - n) if d == 0 else n
                    for r_ in range(2):
                        pb = (it * 2 + r_) % 4

                        def tr(e, st_=st_, r_=r_, pb=pb):
                            ins = None
                            for fc in range(4):
                                ins = e.transpose(out=psb[pb][:, fc * 128:(fc + 1) * 128],
                                                  in_=NAT[st_][r_][:, 4 * fc:4 * fc + 4, :].rearrange("p a b -> p (a b)"),
                                                  identity=ident)
                            return ins
                        P.op("pe", tr, reads=[B_NAT[st_], B_const], writes=[PS[pb]])
                        P.op("act", lambda e, d=d, r_=r_, s_idx=s_idx, pb=pb: e.copy(
                            out=WA[:, d, r_, :, s_idx, :], in_=psb[pb].rearrange("p (a b) -> p a b", a=4)),
                            reads=[PS[pb]], writes=[B_W])
            Bpad = ph.get([128, 2, 16, 128], F32)
            B_Bpad = Buf("bpad")
            Cc = [[ph.get([128, 16, 32], F32) for _ in range(2)] for _ in range(2)]
            B_Cc = [Buf("cc0"), Buf("cc1")]
            _ccp = [ph.get([128, 16, 128], F32) for _ in range(2)]
            CcP = [_ccp, _ccp]
            _bccp = Buf("ccp")
            B_CcP = [_bccp, _bccp]
            P.op("pool", lambda e: e.memset(Bpad, 0.0), writes=[B_Bpad])
            for r_ in range(2):
                P.op("pool", lambda e, r_=r_: e.memset(CcP[0][r_], 0.0), writes=[B_CcP[0]])
            P.op("pool", lambda e: e.memset(WK, 0.0), writes=[B_W])
            it = 0
            for d in range(2):
                for r_ in range(2):
                    for g4 in range(4):
                        P.op("pool", lambda e, d=d, r_=r_, g4=g4: e.tensor_copy(
                            out=Bpad[:, r_, :, :].rearrange("p (fc g) c -> p fc g c", g=4)[:, :, g4, 32 * g4:32 * g4 + 32],
                            in_=BB[d][r_].rearrange("p (fc g) c -> p fc g c", g=4)[:, :, g4, :]),
                            reads=[B_BB], writes=[B_Bpad])
                CTr = SPt[:, d, 1072:1584].rearrange("p (g c) -> p g c", c=32)
                CTi = SPt[:, d, 1584:2096].rearrange("p (g c) -> p g c", c=32)
                for n in range(9):
                    st_ = it % 2
                    it += 1
                    prb, pib, npb = bc(PR[:, d, n, :]), bc(PIm[:, d, n, :]), bc(NPI[:, d, n, :])
                    CR, CN = Cc[st_]
                    rd = [Bp, B_SP, B_q]
                    o("dve", q1[0], CTr, prb, ALU.mult, rd, [B_q])
                    o("dve", q1[1], CTi, pib, ALU.mult, rd, [B_q])
                    o("dve", CR, q1[0], q1[1], ALU.subtract, [B_q], [B_Cc[st_]])
                    o("dve", q1[0], CTr, npb, ALU.mult, rd, [B_q])
                    o("dve", q1[1], CTi, prb, ALU.mult, rd, [B_q])
                    o("dve", CN, q1[0], q1[1], ALU.subtract, [B_q], [B_Cc[st_]])
                    if n >= 1:
                        for r_ in range(2):
                            src4 = Cc[st_][r_].rearrange("p (fc g) c -> p fc g c", g=4)
                            P.op("act", lambda e, d=d, r_=r_, n=n, src4=src4: e.copy(
                                out=WC[:, d, r_, n - 1, :, 0:96].rearrange("p fc (g c) -> p fc g c", c=32),
                                in_=src4[:, :, 0:3, :]), reads=[B_Cc[st_]], writes=[B_W])
                            P.op("act", lambda e, d=d, r_=r_, n=n, src4=src4: e.copy(
                                out=WC[:, d, r_, n - 1, :, 128:160], in_=src4[:, :, 3, :]), reads=[B_Cc[st_]], writes=[B_W])
                    if n <= 7:
                        for r_ in range(2):
                            for g4 in range(4):
                                P.op("pool", lambda e, st_=st_, r_=r_, g4=g4: e.tensor_copy(
                                    out=CcP[st_][r_].rearrange("p (fc g) c -> p fc g c", g=4)[:, :, g4, 32 * g4:32 * g4 + 32],
                                    in_=Cc[st_][r_].rearrange("p (fc g) c -> p fc g c", g=4)[:, :, g4, :]),
                                    reads=[B_Cc[st_]], writes=[B_CcP[st_]])
                        pb = 4 + it % 4

                        def kmm(e, st_=st_, pb=pb):
                            ins = None
                            for fc in range(4):
                                k = 0
                                for g4 in range(4):
                                    for r_ in range(2):
                                        ins = e.matmul(psb[pb][:, fc * 128:(fc + 1) * 128], Bpad[:, r_, 4 * fc + g4, :],
                                                       CcP[st_][r_][:, 4 * fc + g4, :], start=(k == 0), stop=(k == 7))
                                        k += 1
                            return ins
                        P.op("pe", kmm, reads=[B_Bpad, B_CcP[st_]], writes=[PS[pb]])
                        P.op("act", lambda e, d=d, n=n, pb=pb: e.copy(
                            out=WK[:, d, n, :, :], in_=psb[pb].rearrange("p (a b) -> p a b", a=4)),
                            reads=[PS[pb]], writes=[B_W])
            P.barrier()

            rt = Bump(W_END)
            ubuf = rt.get([128, 4096], BF16)
            B_u = Buf("ubuf")
            d_u = P.dsem(f"s5u_{l}")
            um = [rt.get([128, 4096], BF16) for _ in range(4)]
            B_um = [Buf(f"um{i}") for i in range(4)]
            d_um = [P.dsem(f"s5um{i}_{l}") for i in range(4)]
            XS_ = [[rt.get([128, 2, 512], F32) for _ in range(2)] for _ in range(2)]
            B_XS = [[[Buf(f"xs{s_}{i}m"), Buf(f"xs{s_}{i}e")] for i in range(2)] for s_ in range(2)]
            XP = rt.get([128, 16, 512], BF16)
            B_XP = [Buf(f"xp{i}") for i in range(16)]
            yj = [rt.get([128, 512], F32) for _ in range(2)]
            B_yj = [Buf("yj0"), Buf("yj1")]
            gt_ = [rt.get([128, 512], F32) for _ in range(3)]
            B_gt = [Buf("gt0"), Buf("gt1"), Buf("gt2")]
            gst = rt.get([128, 4096], BF16)
            B_gst = Buf("gst")
            d_gst = P.dsem(f"s5g_{l}")
            FS = rt.get([128, 2, 2, 4, 16], F32)
            B_FS = Buf("fs")
            for i in range(4):
                P.op("pool", lambda e, i=i: e.memset(um[i], 0.0), writes=[B_um[i]])
            cnt = {"px": 0, "py": 0, "ch": 0, "yj": 0}

            ptmp = rt.get([128, 512], F32)
            B_ptmp = Buf("ptmp")

            def stt_any(eng, out, in0, scal, in1, rds, wrs, tmpv):
                if eng == "dve":
                    P.op("dve", lambda e: e.scalar_tensor_tensor(out=out, in0=in0, scalar=scal, in1=in1,
                                                                 op0=ALU.mult, op1=ALU.add), reads=rds, writes=wrs)
                else:
                    P.op("pool", lambda e: e.tensor_scalar(out=tmpv, in0=in0, scalar1=scal, scalar2=None, op0=ALU.mult),
                         reads=rds, writes=[B_ptmp])
                    P.op("pool", lambda e: e.tensor_tensor(out=out, in0=tmpv, in1=in1, op=ALU.add),
                         reads=rds + [B_ptmp], writes=wrs)

            def run_group(gi, nseq, Cs, tok0, npass, is_sample):
                NC_ = nseq * Cs
                NTK = NC_ * 8

                def v3(ap2):
                    return ap2[:, 0:NC_].rearrange("p (q c) -> p q c", c=Cs)
                for fc in range(4):
                    P.dma("sp", lambda e, fc=fc: e.dma_start(out=ubuf[:, 0:NTK], in_=Usc[fc * 128:(fc + 1) * 128,
                                                                                         tok0:tok0 + NTK]),
                          d_u, reads=[B_UF[(gi, fc)]] + (B_U if gi == 0 and fc == 0 else []), writes=[B_u])
                    u4 = ubuf[:, 0:NTK].rearrange("p (c s) -> p c s", s=8)
                    for g4 in range(4):
                        gp = 4 * fc + g4
                        P.dma("sp", lambda e, fc=fc, g4=g4: e.dma_start(
                            out=um[g4][32 * g4:32 * g4 + 32, 0:NTK],
                            in_=Usc[fc * 128 + 32 * g4:fc * 128 + 32 * g4 + 32, tok0:tok0 + NTK]), d_um[g4],
                            reads=[B_UF[(gi, fc)]], writes=[B_um[g4]])
                        um4 = um[g4][:, 0:NTK].rearrange("p (c s) -> p c s", s=8)
                        for d in range(2):
                            eng = "dve" if d == 0 else "pool"
                            set_ = cnt["ch"] % 2
                            cnt["ch"] += 1
                            X = XS_[set_]
                            BX = B_XS[set_]
                            for r_ in range(2):
                                pb = cnt["px"] % 4
                                cnt["px"] += 1

                                def amm(e, d=d, r_=r_, fc=fc, pb=pb, um4=um4):
                                    ins = None
                                    for s_ in range(8):
                                        ins = e.matmul(psb[pb][:, 0:NC_], WA[:, d, r_, fc, s_, :], um4[:, :, s_],
                                                       start=(s_ == 0), stop=(s_ == 7))
                                    return ins
                                P.op("pe", amm, reads=[B_W, B_um[g4]], writes=[PS[pb]])
                                P.op("act", lambda e, r_=r_, pb=pb, X=X: e.copy(out=X[0][:, r_, 0:NC_], in_=psb[pb][:, 0:NC_]),
                                     reads=[PS[pb]], writes=BX[0])
                            pr8, pi8, npi8 = (PR[:, d, 8, gp:gp + 1], PIm[:, d, 8, gp:gp + 1], NPI[:, d, 8, gp:gp + 1])
                            if is_sample:
                                col = 0 if d == 0 else NC_ - 1
                                xc = X[0][:, :, col]
                                P.op("dve", lambda e, xc=xc, d=d, gp=gp, pr8=pr8: e.scalar_tensor_tensor(
                                    out=xc, in0=H0[:, d, :, gp], scalar=pr8, in1=xc, op0=ALU.mult, op1=ALU.add),
                                    reads=BX[0] + [B_pow, B_SP], writes=BX[0])
                                P.op("dve", lambda e, X=X, col=col, d=d, gp=gp, npi8=npi8: e.scalar_tensor_tensor(
                                    out=X[0][:, 0, col:col + 1], in0=H0[:, d, 1, gp:gp + 1], scalar=npi8,
                                    in1=X[0][:, 0, col:col + 1], op0=ALU.mult, op1=ALU.add),
                                    reads=BX[0] + [B_pow, B_SP], writes=BX[0])
                                P.op("dve", lambda e, X=X, col=col, d=d, gp=gp, pi8=pi8: e.scalar_tensor_tensor(
                                    out=X[0][:, 1, col:col + 1], in0=H0[:, d, 0, gp:gp + 1], scalar=pi8,
                                    in1=X[0][:, 1, col:col + 1], op0=ALU.mult, op1=ALU.add),
                                    reads=BX[0] + [B_pow, B_SP], writes=BX[0])

                            def v4(t):
                                return t[:, :, 0:NC_].rearrange("p r (q c) -> p r q c", c=Cs)
                            cur = 0
                            for k in range(npass):
                                sh = 1 << k
                                pr_, pi_, npi_ = (PR[:, d, 8 + k, gp:gp + 1], PIm[:, d, 8 + k, gp:gp + 1],
                                                  NPI[:, d, 8 + k, gp:gp + 1])
                                src, dst = v4(X[cur]), v4(X[1 - cur])
                                Bs, Bd = BX[cur], BX[1 - cur]
                                if d == 0:
                                    a_, b_, c_ = slice(sh, Cs), slice(0, Cs - sh), slice(0, sh)
                                else:
                                    a_, b_, c_ = slice(0, Cs - sh), slice(sh, Cs), slice(Cs - sh, Cs)
                                if nseq == 1:
                                    P.op("dve", lambda e, src=src, dst=dst, a_=a_, b_=b_, pr_=pr_: e.scalar_tensor_tensor(
                                        out=dst[:, :, 0, a_], in0=src[:, :, 0, b_], scalar=pr_, in1=src[:, :, 0, a_],
                                        op0=ALU.mult, op1=ALU.add), reads=Bs + [B_pow], writes=[Bd[0]])
                                else:
                                    for r2 in range(2):
                                        P.op("dve", lambda e, src=src, dst=dst, a_=a_, b_=b_, pr_=pr_, r2=r2: e.scalar_tensor_tensor(
                                            out=dst[:, r2, :, a_], in0=src[:, r2, :, b_], scalar=pr_, in1=src[:, r2, :, a_],
                                            op0=ALU.mult, op1=ALU.add), reads=Bs + [B_pow], writes=[Bd[0]])
                                P.op("dve", lambda e, src=src, dst=dst, a_=a_, b_=b_, npi_=npi_: e.scalar_tensor_tensor(
                                    out=dst[:, 0, :, a_], in0=src[:, 1, :, b_], scalar=npi_, in1=dst[:, 0, :, a_],
                                    op0=ALU.mult, op1=ALU.add), reads=Bs + [Bd[0], B_pow], writes=[Bd[0]])
                                P.op("dve", lambda e, src=src, dst=dst, a_=a_, b_=b_, pi_=pi_: e.scalar_tensor_tensor(
                                    out=dst[:, 1, :, a_], in0=src[:, 0, :, b_], scalar=pi_, in1=dst[:, 1, :, a_],
                                    op0=ALU.mult, op1=ALU.add), reads=Bs + [Bd[0], B_pow], writes=[Bd[0]])
                                if nseq == 1:
                                    P.op("act", lambda e, src=src, dst=dst, c_=c_: e.copy(out=dst[:, :, 0, c_], in_=src[:, :, 0, c_]),
                                         reads=Bs, writes=[Bd[1]])
                                else:
                                    for r2 in range(2):
                                        P.op("act", lambda e, src=src, dst=dst, c_=c_, r2=r2: e.copy(
                                            out=dst[:, r2, :, c_], in_=src[:, r2, :, c_]), reads=Bs, writes=[Bd[1]])
                                cur = 1 - cur
                            for r_ in range(2):
                                Xf = v4(X[cur])[:, r_, :, :]
                                BXf = BX[cur]
                                slot = (g4 * 2 + d) * 2 + r_
                                xp3 = v3(XP[:, slot, :])
                                if not is_sample:
                                    ecol = Cs - 1 if d == 0 else 0
                                    P.op("act", lambda e, d=d, r_=r_, gp=gp, Xf=Xf, ecol=ecol: e.copy(
                                        out=FS[:, r_, d, :, gp], in_=Xf[:, :, ecol]), reads=BXf, writes=[B_FS])
                                if d == 0:
                                    P.op("act", lambda e, xp3=xp3, Xf=Xf: e.copy(out=xp3[:, :, 1:Cs], in_=Xf[:, :, 0:Cs - 1]),
                                         reads=BXf, writes=[B_XP[slot]])
                                    ec = 0
                                else:
                                    P.op("act", lambda e, xp3=xp3, Xf=Xf: e.copy(out=xp3[:, :, 0:Cs - 1], in_=Xf[:, :, 1:Cs]),
                                         reads=BXf, writes=[B_XP[slot]])
                                    ec = Cs - 1
                                if is_sample:
                                    hsrc = H0[:, d, r_, gp:gp + 1]
                                    P.op("act", lambda e, xp3=xp3, ec=ec, hsrc=hsrc: e.copy(out=xp3[:, 0, ec:ec + 1], in_=hsrc),
                                         reads=[B_SP], writes=[B_XP[slot]])
                                else:
                                    P.op("act", lambda e, xp3=xp3, ec=ec: e.copy(out=xp3[:, :, ec], in_=ZER[:, 0:nseq]),
                                         reads=[B_pow], writes=[B_XP[slot]])
                    g3 = gst[:, 0:NTK].rearrange("p (c s) -> p c s", s=8)
                    dcol = VT[:, VOFF["s5_d"] + e_ * 4 + fc: VOFF["s5_d"] + e_ * 4 + fc + 1]
                    for j in range(8):
                        pb = 4 + cnt["py"] % 4
                        cnt["py"] += 1

                        def ymm(e, j=j, fc=fc, pb=pb, u4=u4):
                            first = True
                            for s_ in range(0, j + 1):
                                e.matmul(psb[pb][:, 0:NC_], WK[:, 0, j - s_, fc, :], u4[:, :, s_], start=first, stop=False)
                                first = False
                            for s_ in range(j, 8):
                                e.matmul(psb[pb][:, 0:NC_], WK[:, 1, s_ - j, fc, :], u4[:, :, s_], start=False, stop=False)
                            ins = None
                            for g4 in range(4):
                                for d in range(2):
                                    nidx = j if d == 0 else 7 - j
                                    for r_ in range(2):
                                        slot = (g4 * 2 + d) * 2 + r_
                                        last = (g4 == 3 and d == 1 and r_ == 1)
                                        if g4 < 3:
                                            ins = e.matmul(psb[pb][32 * g4:32 * g4 + 32, 0:NC_],
                                                           WC[:, d, r_, nidx, fc, 32 * g4:32 * g4 + 32], XP[:, slot, 0:NC_],
                                                           start=False, stop=last)
                                        else:
                                            ins = e.matmul(psb[pb][64:128, 0:NC_], WC[:, d, r_, nidx, fc, 96:160],
                                                           XP[:, slot, 0:NC_], start=False, stop=last)
                            return ins
                        P.op("pe", ymm, reads=[B_W, B_u] + B_XP, writes=[PS[pb]])
                        ys = cnt["yj"] % 2
                        cnt["yj"] += 1
                        P.op("dve", lambda e, j=j, pb=pb, ys=ys, u4=u4, dcol=dcol: e.scalar_tensor_tensor(
                            out=yj[ys][:, 0:NC_], in0=u4[:, :, j], scalar=dcol, in1=psb[pb][:, 0:NC_],
                            op0=ALU.mult, op1=ALU.add), reads=[PS[pb], B_u, B_VT], writes=[B_yj[ys]])
                        P.op("act", lambda e, ys=ys: e.activation(out=gt_[0][:, 0:NC_], in_=yj[ys][:, 0:NC_], func=AF.Square),
                             reads=[B_yj[ys]], writes=[B_gt[0]])
                        P.op("dve", lambda e: e.tensor_scalar(out=gt_[1][:, 0:NC_], in0=gt_[0][:, 0:NC_], scalar1=0.044715,
                                                              scalar2=1.0, op0=ALU.mult, op1=ALU.add),
                             reads=[B_gt[0]], writes=[B_gt[1]])
                        P.op("dve", lambda e, ys=ys: e.tensor_tensor(out=gt_[1][:, 0:NC_], in0=gt_[1][:, 0:NC_],
                                                                     in1=yj[ys][:, 0:NC_], op=ALU.mult),
                             reads=[B_gt[1], B_yj[ys]], writes=[B_gt[1]])
                        P.op("act", lambda e: e.activation(out=gt_[2][:, 0:NC_], in_=gt_[1][:, 0:NC_], func=AF.Sigmoid,
                                                           scale=1.5957691216057308), reads=[B_gt[1]], writes=[B_gt[2]])
                        P.op("dve", lambda e, j=j, ys=ys, g3=g3: e.tensor_tensor(
                            out=g3[:, :, j], in0=yj[ys][:, 0:NC_], in1=gt_[2][:, 0:NC_], op=ALU.mult),
                            reads=[B_yj[ys], B_gt[2]], writes=[B_gst])
                    P.dma("pool", lambda e, fc=fc: e.dma_start(out=Usc[fc * 128:(fc + 1) * 128, tok0:tok0 + NTK],
                                                               in_=gst[:, 0:NTK]), d_gst, reads=[B_gst],
                          writes=[B_UF[(gi, fc)]])

            run_group(0, 1, 512, 0, 9, True)
            run_group(1, 4, 32, 4096, 5, False)
            fso = [rt.get([128, 128], F32) for _ in range(2)]
            B_fso = [Buf("fso0"), Buf("fso1")]
            d_fso = [P.dsem(f"fso0_{l}"), P.dsem(f"fso1_{l}")]
            for r_ in range(2):
                P.op("pe", lambda e, r_=r_: e.transpose(out=psb[r_][:, 0:128],
                                                        in_=FS[:, r_, :, :, :].rearrange("p d q g -> p (d q g)"),
                                                        identity=ident), reads=[B_FS, B_const], writes=[PS[r_]])
                P.op("act", lambda e, r_=r_: e.copy(out=fso[r_], in_=psb[r_][:, 0:128]), reads=[PS[r_]], writes=[B_fso[r_]])
                dst_t = nsr if r_ == 0 else nsi
                for d in range(2):
                    for q_ in range(4):
                        dstf = dst_t[q_, e_, d, :, :].rearrange("(gp g2) p -> gp (g2 p)", g2=2)
                        P.dma("sp", lambda e, r_=r_, d=d, q_=q_, dstf=dstf: e.dma_start(
                            out=dstf, in_=fso[r_][64 * d + 16 * q_:64 * d + 16 * q_ + 16, :]), d_fso[r_],
                            reads=[B_fso[r_]])
            P.barrier()

            gl = Bump(W_END)
            GW = gl.get([128, 4, 512], BF16)
            B_GW = Buf("gw")
            d_gw = P.dsem(f"gw_{l}")
            P.dma("sp", lambda e: e.dma_start(out=GW, in_=glub[e_].rearrange("(kc p) n -> p kc n", p=128)), d_gw,
                  reads=[WB[l]], writes=[B_GW])
            gtile = [gl.get([128, 4, TT], BF16) for _ in range(2)]
            B_gtile = [Buf("gti0"), Buf("gti1")]
            d_gtile = [P.dsem(f"gti0_{l}"), P.dsem(f"gti1_{l}")]
            sgt = [gl.get([128, TT], F32) for _ in range(2)]
            B_sgt = [Buf("sg0"), Buf("sg1")]
            ostg = [gl.get([128, 4, TT], BF16) for _ in range(2)]
            B_ostg = [Buf("os0"), Buf("os1")]
            d_ostg = [P.dsem(f"os0_{l}"), P.dsem(f"os1_{l}")]
            for t in range(NT):
                s_ = t % 2
                P.dma("sp", lambda e, s_=s_, t=t: e.dma_start(
                    out=gtile[s_], in_=Usc[:, t * TT:(t + 1) * TT].rearrange("(c p) t -> p c t", p=128)), d_gtile[s_],
                    writes=[B_gtile[s_]])
                for n in range(4):
                    pb = (t * 4 + n) % 4

                    def gmm(e, s_=s_, n=n, pb=pb):
                        ins = None
                        for kc in range(4):
                            ins = e.matmul(psb[pb], GW[:, kc, n * 128:(n + 1) * 128], gtile[s_][:, kc, :],
                                           start=(kc == 0), stop=(kc == 3))
                        return ins
                    P.op("pe", gmm, reads=[B_GW, B_gtile[s_]], writes=[PS[pb]])
                    ss_ = n % 2
                    bcol = VT[:, VOFF["glu_b"] + e_ * 4 + n: VOFF["glu_b"] + e_ * 4 + n + 1]
                    P.op("act", lambda e, ss_=ss_, pb=pb, bcol=bcol: e.activation(out=sgt[ss_], in_=psb[pb], func=AF.Sigmoid,
                                                                                 bias=bcol, scale=1.0),
                         reads=[PS[pb], B_VT], writes=[B_sgt[ss_]])
                    P.op("dve", lambda e, s_=s_, ss_=ss_, n=n: e.tensor_tensor(out=ostg[s_][:, n, :], in0=gtile[s_][:, n, :],
                                                                               in1=sgt[ss_], op=ALU.mult),
                         reads=[B_gtile[s_], B_sgt[ss_]], writes=[B_ostg[s_]])
                P.dma("pool", lambda e, s_=s_, t=t: e.dma_start(
                    out=MIX[0:512, t * TT:(t + 1) * TT].rearrange("(c p) t -> p c t", p=128), in_=ostg[s_]), d_ostg[s_],
                    reads=[B_ostg[s_]], writes=[MIXB[t]])
            P.barrier()

        def phase_S5_zero(l):
            ph = Bump(PERS_END)
            zt = ph.get([128, NTOK], BF16)
            bz = Buf("zt")
            dz = P.dsem(f"zt_{l}")
            P.op("pool", lambda e: e.memset(zt, 0.0), writes=[bz])
            for c in range(4):
                P.dma("sp", lambda e, c=c: e.dma_start(out=MIX[c * 128:(c + 1) * 128, :], in_=zt), dz,
                      reads=[bz], writes=MIXB)
            P.barrier()

        NEG = -30000.0

        def phase_nab(e_):
            ph = Bump(PERS_END)
            E2 = ph.get([128, 8, 15, 64], F32)
            B_E2 = Buf("E2")
            Tt = [ph.get([128, 8, 10, 64], F32) for _ in range(2)]
            B_Tt = [Buf("Tt0"), Buf("Tt1")]
            d_Tt = [P.dsem(f"Tt0_{e_}"), P.dsem(f"Tt1_{e_}")]
            d_e2 = P.dsem(f"e2_{e_}")
            P.op("pool", lambda e: e.memset(E2, NEG), writes=[B_E2])
            for p in range(128):
                col = p % 64
                c0 = min(max(col - 8, 0), 48)
                off = c0 - col + 15
                P.dma("sp", lambda e, p=p, c0=c0, off=off: e.dma_start(
                    out=E2[p:p + 1, :, :, c0:c0 + 16], in_=na_rpb[e_:e_ + 1, :, :, off:off + 16]),
                    d_e2, writes=[B_E2])
            for ti, key in enumerate(PAIR_TYPES):
                s = ti % 2
                P.op("pool", lambda e, s=s: e.memset(Tt[s], NEG), writes=[B_Tt[s]])
                for rr in range(2):
                    wr_lo, i_lo = key[rr]
                    P.op("dve", lambda e, s=s, rr=rr, wr_lo=wr_lo, i_lo=i_lo: e.tensor_copy(
                        out=Tt[s][rr * 64:(rr + 1) * 64, :, wr_lo:wr_lo + 8, :],
                        in_=E2[rr * 64:(rr + 1) * 64, :, i_lo:i_lo + 8, :]), reads=[B_E2], writes=[B_Tt[s]])
                P.dma("sp", lambda e, s=s, ti=ti: e.dma_start(
                    out=BIAS[e_, ti], in_=Tt[s].rearrange("p h w c -> p h (w c)")), d_Tt[s],
                    reads=[B_Tt[s]], writes=[B_BIAS])
            P.barrier()

        def phase_C(l):
            e_ = l // 2
            ph = Bump(PERS_END)
            qh = [ph.get([128, NTOK], BF16) for _ in range(2)]
            kh = [ph.get([128, NTOK], BF16) for _ in range(2)]
            vh = [ph.get([128, NTOK // 128, 128], BF16) for _ in range(2)]
            B_qh = [Buf("qh0"), Buf("qh1")]
            B_kh = [Buf("kh0"), Buf("kh1")]
            B_vh = [Buf("vh0"), Buf("vh1")]
            d_qh = [P.dsem(f"qh0_{l}"), P.dsem(f"qh1_{l}")]
            d_kh = [P.dsem(f"kh0_{l}"), P.dsem(f"kh1_{l}")]
            d_vh = [P.dsem(f"vh0_{l}"), P.dsem(f"vh1_{l}")]
            bh = [ph.get([128, NTYPES, 640], F32) for _ in range(2)]
            B_bh = [Buf("bh0"), Buf("bh1")]
            d_bh = [P.dsem(f"bh0_{l}"), P.dsem(f"bh1_{l}")]
            ck32 = [ph.get([128, 2, 128], F32) for _ in range(2)]
            cv32 = [ph.get([128, 2, 128], F32) for _ in range(2)]
            B_ck32 = [Buf("ck0"), Buf("ck1")]
            B_cv32 = [Buf("cv0"), Buf("cv1")]
            d_ck = [P.dsem(f"ck0_{l}"), P.dsem(f"ck1_{l}")]
            d_cv = [P.dsem(f"cv0_{l}"), P.dsem(f"cv1_{l}")]
            kcb = [ph.get([128, 256], BF16) for _ in range(2)]
            vcb = [ph.get([128, 2, 128], BF16) for _ in range(2)]
            B_kcb = [Buf("kcb0"), Buf("kcb1")]
            B_vcb = [Buf("vcb0"), Buf("vcb1")]
            at = [ph.get([128, NTOK], BF16) for _ in range(2)]
            B_at = [Buf("at0"), Buf("at1")]
            d_at = [P.dsem(f"at0_{l}"), P.dsem(f"at1_{l}")]
            NSET = 4
            bhb = [ph.get([128, NTYPES, 640], BF16) for _ in range(2)]
            B_bhb = [Buf("bhb0"), Buf("bhb1")]
            Pb = [ph.get([128, 896], BF16) for _ in range(NSET)]
            B_Pb = [Buf(f"Pb{i}") for i in range(NSET)]
            PT = [ph.get([128, 7, 128], BF16) for _ in range(NSET)]
            B_PT = [Buf(f"PT{i}") for i in range(NSET)]
            nmx = [ph.get([128, 1], F32) for _ in range(NSET)]
            B_mx = [Buf(f"mx{i}") for i in range(NSET)]
            junk = ph.get([128, 896], BF16)
            B_junk = Buf("junk")
            rsb = [ph.get([128, 128], F32) for _ in range(NSET)]
            B_rsb = [Buf(f"rsb{i}") for i in range(NSET)]
            psT = psb[6].bitcast(BF16)
            TB, OB = 6, 7

            def load_head(h):
                hp = h % 2
                P.dma("sp", lambda e: e.dma_start(out=qh[hp], in_=Qsc[h * 128:(h + 1) * 128, :]), d_qh[hp],
                      reads=B_Q, writes=[B_qh[hp]])
                P.dma("sp", lambda e: e.dma_start(out=kh[hp], in_=Ksc[h * 128:(h + 1) * 128, :]), d_kh[hp],
                      reads=B_K, writes=[B_kh[hp]])
                P.dma("sp", lambda e: e.dma_start(
                    out=vh[hp], in_=Vsc[:, h * 128:(h + 1) * 128].rearrange("(c p) d -> p c d", p=128)), d_vh[hp],
                    reads=B_V, writes=[B_vh[hp]])
                P.dma("sp", lambda e: e.dma_start(
                    out=bh[hp], in_=BIAS[e_, :, :, h, :].rearrange("t p k -> p t k")), d_bh[hp],
                    reads=[B_BIAS], writes=[B_bh[hp]])
                P.dma("sp", lambda e: e.dma_start(
                    out=ck32[hp], in_=cache_k[e_, :, h, :].rearrange("(c p) d -> p c d", p=128)), d_ck[hp],
                    writes=[B_ck32[hp]])
                P.dma("sp", lambda e: e.dma_start(
                    out=cv32[hp], in_=cache_v[e_, :, h, :].rearrange("(c p) d -> p c d", p=128)), d_cv[hp],
                    writes=[B_cv32[hp]])

            def stA(un, i):
                s3 = i % 3
                hp = un["hp"]
                bA, bB = 2 * s3, 2 * s3 + 1
                q0 = un["q0"]
                if un["kind"] == "na":
                    k0, ty = un["k0"], un["ty"]

                    def smm(e):
                        qq = qh[hp][:, q0:q0 + 128]
                        e.matmul(psb[bA], qq, kh[hp][:, k0:k0 + 512], start=True, stop=False)
                        e.matmul(psb[bA], identb, bhb[hp][:, ty, 0:512], start=False, stop=True)
                        e.matmul(psb[bB][:, 0:128], qq, kh[hp][:, k0 + 512:k0 + 640], start=True, stop=False)
                        e.matmul(psb[bB][:, 0:128], identb, bhb[hp][:, ty, 512:640], start=False, stop=True)
                        return e.matmul(psb[bB][:, 128:384], qq, kcb[hp], start=True, stop=True)
                    P.op("pe", smm, reads=[B_qh[hp], B_kh[hp], B_kcb[hp], B_bhb[hp], B_const], writes=[PS[bA], PS[bB]])
                else:
                    tok0 = un["tok0"]
                    P.op("pe", lambda e: e.matmul(psb[bA][:, 0:256], qh[hp][:, q0:q0 + 128], kh[hp][:, tok0:tok0 + 256],
                                                  start=True, stop=True), reads=[B_qh[hp], B_kh[hp]], writes=[PS[bA], PS[bB]])

            def stB(un, i):
                s3, q = i % 3, i % NSET
                NK = un["nkc"] * 128
                Sreg = psall[:, s3 * 1024:s3 * 1024 + NK]
                rd = [PS[2 * s3], PS[2 * s3 + 1]]
                P.op("dve", lambda e: e.tensor_scalar(out=junk[:, 0:NK], in0=Sreg, scalar1=-1.0, scalar2=None,
                                                      op0=ALU.mult, op1=ALU.min, accum_out=nmx[q][:, 0:1]),
                     reads=rd, writes=[B_mx[q], B_junk])
                P.op("act", lambda e: e.activation(out=Pb[q][:, 0:NK], in_=Sreg, func=AF.Exp, bias=nmx[q][:, 0:1], scale=1.0),
                     reads=rd + [B_mx[q]], writes=[B_Pb[q]])

            def stC1(un, i):
                q = i % NSET
                NKC = un["nkc"]

                def tr(e):
                    ins = None
                    for c in range(NKC):
                        ins = e.transpose(out=psT[:, c * 128:(c + 1) * 128], in_=Pb[q][:, c * 128:(c + 1) * 128],
                                          identity=identb)
                    return ins
                P.op("pe", tr, reads=[B_Pb[q], B_const], writes=[PS[TB]])
                P.op("act", lambda e: e.copy(out=PT[q][:, 0:NKC, :],
                                             in_=psT[:, 0:NKC * 128].rearrange("p (c q) -> p c q", q=128)),
                     reads=[PS[TB]], writes=[B_PT[q]])

            def stC2(un, i):
                q = i % NSET
                NKC = un["nkc"]
                hp = un["hp"]
                vch, vbufs, q0 = un["vch"], un["vbufs"], un["q0"]

                def pv(e):
                    ins = None
                    for c in range(NKC):
                        ins = e.matmul(psb[OB][:, 0:128], vch[c], PT[q][:, c, :], start=(c == 0), stop=(c == NKC - 1))
                    for c in range(NKC):
                        ins = e.matmul(psb[OB][:, 128:256], onesb, PT[q][:, c, :], start=(c == 0), stop=(c == NKC - 1))
                    return ins
                P.op("pe", pv, reads=[B_PT[q], B_const] + vbufs, writes=[PS[OB]])
                P.op("dve", lambda e: e.reciprocal(out=rsb[q], in_=psb[OB][:, 128:256]), reads=[PS[OB]], writes=[B_rsb[q]])
                P.op("dve", lambda e: e.tensor_tensor(out=at[hp][:, q0:q0 + 128], in0=psb[OB][:, 0:128], in1=rsb[q],
                                                      op=ALU.mult), reads=[PS[OB], B_rsb[q]], writes=[B_at[hp]])

            gi = [0]
            load_head(0)
            for h in range(8):
                hp = h % 2
                if h + 1 < 8:
                    load_head(h + 1)

                def trk(e, hp=hp):
                    ins = None
                    for c in range(2):
                        ins = e.transpose(out=psb[TB][:, c * 128:(c + 1) * 128], in_=ck32[hp][:, c, :], identity=ident)
                    return ins
                P.op("pe", trk, reads=[B_ck32[hp], B_const], writes=[PS[TB]])
                P.op("act", lambda e, hp=hp: e.copy(out=kcb[hp], in_=psb[TB][:, 0:256]), reads=[PS[TB]], writes=[B_kcb[hp]])
                P.op("act", lambda e, hp=hp: e.copy(out=vcb[hp], in_=cv32[hp]), reads=[B_cv32[hp]], writes=[B_vcb[hp]])
                P.op("dve", lambda e, hp=hp: e.tensor_copy(out=bhb[hp], in_=bh[hp]), reads=[B_bh[hp]], writes=[B_bhb[hp]])
                units = []
                for a in range(32):
                    ty, w0 = PAIR_MAP[a]
                    k0 = 64 * w0
                    vt0 = k0 // 128
                    units.append(dict(kind="na", hp=hp, q0=128 * a, k0=k0, ty=ty, nkc=7,
                                      vch=[vh[hp][:, vt0 + c, :] for c in range(5)] + [vcb[hp][:, c, :] for c in range(2)],
                                      vbufs=[B_vh[hp], B_vcb[hp]]))
                for sq_ in range(4):
                    tok0 = 4096 + 256 * sq_
                    for qb in range(2):
                        vt0 = tok0 // 128
                        units.append(dict(kind="ctx", hp=hp, q0=tok0 + qb * 128, tok0=tok0, nkc=2,
                                          vch=[vh[hp][:, vt0 + c, :] for c in range(2)], vbufs=[B_vh[hp]]))
                n = len(units)
                g0 = gi[0]
                for i in range(n + 3):
                    if i < n:
                        stA(units[i], g0 + i)
                    if 0 <= i - 1 < n:
                        stB(units[i - 1], g0 + i - 1)
                    if 0 <= i - 2 < n:
                        stC1(units[i - 2], g0 + i - 2)
                    if 0 <= i - 3 < n:
                        stC2(units[i - 3], g0 + i - 3)
                gi[0] += n
                P.dma("pool", lambda e, hp=hp, h=h: e.dma_start(out=MIX[512 + h * 128:512 + (h + 1) * 128, :], in_=at[hp]),
                      d_at[hp], reads=[B_at[hp]], writes=MIXB)
            P.barrier()

        if mode == "nab":
            phase_nab(0)
        elif mode == "C":
            phase_C(0)
        elif mode == "A":
            phase_A_ab(0)
        elif mode == "S5":
            phase_S5(0)
        for l in range(n_layers if big else 0):
            if l % 2 == 1:
                phase_A_conv(l)
            elif with_ab:
                if "A" not in DBG_SKIP:
                    phase_A_ab(l)
                if with_s5:
                    phase_S5(l)
                else:
                    phase_S5_zero(l)
                if "nab" not in DBG_SKIP:
                    phase_nab(l // 2)
                if "C" not in DBG_SKIP:
                    phase_C(l)
            if "D" not in DBG_SKIP:
                phase_D(l)
        P.barrier()
        P.finalize()
        for d in P.dsems:
            d.h = es.enter_context(nc.semaphore("d_" + d.name))
        with nc.Block() as block:
            @block.tensor
            def _(e):
                P.emit("pe", e, sems)

            @block.scalar
            def _(e):
                P.emit("act", e, sems)

            @block.vector
            def _(e):
                P.emit("dve", e, sems)

            @block.gpsimd
            def _(e):
                P.emit("pool", e, sems)

            @block.sync
            def _(e):
                P.emit("sp", e, sems)
    return nc


VOFF = {}
_rows = 0
for _nm, _n in (("cvec", 32), ("ada_b", 4 * 96), ("norm1_g", 64), ("norm2_g", 64), ("conv_w", 96), ("conv_b", 32),
                ("s5_d", 8), ("glu_b", 8), ("q_g", 2), ("k_g", 2)):
    VOFF[_nm] = _rows
    _rows += _n
NVROWS = _rows


def _pack_vecs(inp, core):
    rows = []
    cv = np.stack([inp["c"][core], inp["c_ctx"]], axis=0)
    rows.append(cv.reshape(32, 128))
    rows.append(inp["ada_b"].reshape(4 * 96, 128))
    rows.append(inp["norm1_g"].reshape(64, 128))
    rows.append(inp["norm2_g"].reshape(64, 128))
    rows.append(inp["conv_w"].reshape(96, 128))
    rows.append(inp["conv_b"].reshape(32, 128))
    rows.append(inp["s5_d"].reshape(8, 128))
    rows.append(inp["s5_glu_b"].reshape(8, 128))
    rows.append(inp["q_norm_g"].reshape(2, 128))
    rows.append(inp["k_norm_g"].reshape(2, 128))
    return np.ascontiguousarray(np.concatenate(rows, axis=0).astype(np.float32))


def _pair_types():
    types, tmap = [], []
    for a in range(32):
        w0 = min(max(2 * a - 4, 0), 54)
        key = []
        for rr in range(2):
            r = 2 * a + rr
            r0 = min(max(r - 4, 0), 56)
            key.append((r0 - w0, r0 - r + 7))
        key = tuple(key)
        if key not in types:
            types.append(key)
        tmap.append((types.index(key), w0))
    return types, tmap


PAIR_TYPES, PAIR_MAP = _pair_types()
NTYPES = len(PAIR_TYPES)
S5P_COLS = 16 * 3 + 4 * 16 * 32

_NC_CACHE = {}


def _get_nc(key=(DEPTH, True)):
    if key not in _NC_CACHE:
        _NC_CACHE[key] = build_program(*key)
    return _NC_CACHE[key]


def _lay_gp(a):
    return a.reshape(16, 2, 64).transpose(1, 2, 0).reshape(128, 16)


def _pack_s5(inp):
    out = np.zeros((2, 2, 128, S5P_COLS), np.float32)
    for e in range(2):
        for d in range(2):
            out[e, d, :, 0:16] = _lay_gp(inp["s5_lam_re"][e, d])
            out[e, d, :, 16:32] = _lay_gp(inp["s5_lam_im"][e, d])
            ldt = inp["s5_log_dt"][e, d].reshape(16, 2).T
            out[e, d, :, 32:48] = np.broadcast_to(ldt[:, None, :], (2, 64, 16)).reshape(128, 16)
            col = 48
            for nm, is_c in (("s5_b_re", False), ("s5_b_im", False), ("s5_c_re", True), ("s5_c_im", True)):
                a = inp[nm][e, d]
                blk = np.zeros((2, 64, 16, 2, 16), np.float32)
                for g2 in range(2):
                    if is_c:
                        blk[g2, :, :, g2, :] = a.reshape(16, 2, 16, 64)[:, g2].transpose(2, 0, 1)
                    else:
                        blk[g2, :, :, g2, :] = a.reshape(16, 2, 64, 16)[:, g2].transpose(1, 0, 2)
                out[e, d, :, col:col + 512] = blk.reshape(128, 512)
                col += 512
    return out


def make_in_maps(inp):
    ident = np.eye(128, dtype=np.float32)
    s5p = _pack_s5(inp)
    maps = []
    for i in range(8):
        x_in = np.concatenate([inp["x_sample"][i], inp["x_prompt"][4 * i:4 * i + 4].reshape(1024, D)], axis=0)
        m = {
            "x_in": np.ascontiguousarray(x_in, dtype=np.float32),
            "vecs": _pack_vecs(inp, i),
            "ident": ident,
            "ada_w": inp["ada_w"], "ab_w_in": inp["ab_w_in"], "ab_w_out": inp["ab_w_out"],
            "conv_w_in": inp["conv_w_in"], "conv_w_out": inp["conv_w_out"],
            "mlp_w1": inp["mlp_w1"], "mlp_w2": inp["mlp_w2"],
            "cache_k": np.ascontiguousarray(inp["cache_k"][i]), "cache_v": np.ascontiguousarray(inp["cache_v"][i]),
            "na_rpb": inp["na_rpb"], "glu_w": inp["s5_glu_w"], "s5p": s5p,
            "s5h0": np.ascontiguousarray(np.stack([
                np.stack([np.stack([_lay_gp(inp["state_ssm_re"][i, e, d]), _lay_gp(inp["state_ssm_im"][i, e, d])], 0)
                          for d in range(2)], 0) for e in range(2)], 0), dtype=np.float32),
        }
        maps.append(m)
    return maps


def kernel(**inp):
    inp = {k: np.asarray(v) for k, v in inp.items()}
    nc = _get_nc()
    res = run_bass_kernel_spmd(nc, make_in_maps(inp), core_ids=list(range(8)))
    rs = res.results
    y_s = np.stack([rs[i]["y"][:4096] for i in range(8)], axis=0)
    y_p = np.concatenate([rs[i]["y"][4096:].reshape(4, 256, D) for i in range(8)], axis=0)
    nk = np.concatenate([rs[i]["nck"] for i in range(8)], axis=0)
    nv = np.concatenate([rs[i]["ncv"] for i in range(8)], axis=0)
    sr = np.concatenate([rs[i]["nsr"] for i in range(8)], axis=0)
    si = np.concatenate([rs[i]["nsi"] for i in range(8)], axis=0)
    return (y_p.astype(np.float32), y_s.astype(np.float32), nk.astype(np.float32), nv.astype(np.float32),
            sr.astype(np.float32), si.astype(np.float32))
```

```python
import contextlib
import numpy as np
import concourse.bass as bass
import concourse.mybir as mybir
from concourse.bass_utils import run_bass_kernel_spmd

F32 = mybir.dt.float32
BF16 = mybir.dt.bfloat16
U8 = mybir.dt.uint8
AF = mybir.ActivationFunctionType
ALU = mybir.AluOpType

D = 2048
KC = 16
DEPTH = 4
TT = 512
NTOK = 5120
NT = NTOK // TT
NST = 8
DFF = 8192
EPS = 1e-6
ABIN = 3584
ABOUT = 1536


class Buf:
    __slots__ = ("name", "w", "r")

    def __init__(self, name):
        self.name = name
        self.w = None
        self.r = {}


class DSem:
    def __init__(self, name):
        self.name = name
        self.count = 0
        self.h = None


class Op:
    __slots__ = ("eng", "fn", "waits", "inc", "val", "dsem", "dval")

    def __init__(self, eng, fn, waits, dsem=None):
        self.eng = eng
        self.fn = fn
        self.waits = waits
        self.inc = False
        self.val = 0
        self.dsem = dsem
        self.dval = 0


ENGS = ["pe", "act", "dve", "pool", "sp"]
DBG_SKIP = set()
_I, _O = "ExternalInput", "ExternalOutput"
MODE_IO = {
    "nab": {"BIAS": _O},
    "C": {"Qsc": _I, "Ksc": _I, "Vsc": _I, "BIAS": _I, "MIX": _O},
    "A": {"XS": _I, "awib0": _I, "Usc": _O, "Qsc": _O, "Ksc": _O, "Vsc": _O},
    "S5": {"Usc": _I, "glub0": _I, "MIX": _O},
}


class Prog:
    def __init__(self):
        self.ops = {e: [] for e in ENGS}
        self.dsems = []
        self.last = {e: None for e in ENGS}

    def dsem(self, name):
        key = name.split("_")[0]
        for d in self.dsems:
            if d.name == key:
                return d
        d = DSem(key)
        self.dsems.append(d)
        return d

    def _deps(self, eng, reads, writes):
        toks = []
        for b in reads:
            if b.w is not None:
                toks.append(b.w)
        for b in writes:
            if b.w is not None:
                toks.append(b.w)
            toks.extend(b.r.values())
        out = []
        for t in toks:
            if t[0] == "E" and t[1].eng == "pe" and eng == "pe":
                continue
            out.append(t)
        return out

    def op(self, eng, fn, reads=(), writes=()):
        o = Op(eng, fn, self._deps(eng, reads, writes))
        self.ops[eng].append(o)
        self.last[eng] = o
        tok = ("E", o)
        for b in reads:
            b.r[eng] = tok
        for b in writes:
            b.w = tok
            b.r = {}
        return o

    def dma(self, eng, fn, dsem, reads=(), writes=()):
        o = Op(eng, fn, self._deps(eng, reads, writes), dsem=dsem)
        dsem.count += 1
        o.dval = 16 * dsem.count
        self.ops[eng].append(o)
        tok = ("D", dsem, o.dval)
        for b in reads:
            b.r["dma_" + dsem.name] = tok
        for b in writes:
            b.w = tok
            b.r = {}
        return o

    def barrier(self):
        toks = []
        for e in ENGS:
            if self.last[e] is not None:
                toks.append(("E", self.last[e]))
        for d in self.dsems:
            if d.count:
                toks.append(("D", d, 16 * d.count))
        for e in ENGS:
            o = Op(e, None, list(toks))
            self.ops[e].append(o)

    def finalize(self):
        for e in ENGS:
            for o in self.ops[e]:
                for t in o.waits:
                    if t[0] == "E":
                        t[1].inc = True
        for e in ENGS:
            n = 0
            for o in self.ops[e]:
                if o.inc and o.dsem is None and o.fn is not None:
                    n += 1
                    o.val = n
                elif o.inc:
                    o.val = n

    def emit(self, ename, eng, sems):
        known = {}
        for o in self.ops[ename]:
            for t in o.waits:
                if t[0] == "E":
                    p = t[1]
                    if p.dsem is not None:
                        s, v = p.dsem.h, p.dval
                        key = ("d", p.dsem.name)
                    else:
                        s, v = sems[p.eng], p.val
                        key = ("e", p.eng)
                else:
                    s, v = t[1].h, t[2]
                    key = ("d", t[1].name)
                if v <= 0:
                    continue
                if known.get(key, 0) < v:
                    eng.wait_ge(s, v)
                    known[key] = v
            if o.fn is None:
                continue
            ins = o.fn(eng)
            if o.dsem is not None:
                ins.then_inc(o.dsem.h, 16)
            elif o.inc:
                ins.then_inc(sems[ename], 1)


def build_program(n_layers=DEPTH, with_ab=True, with_s5=True, mode="full"):
    nc = bass.Bass("TRN2", target_bir_lowering=False)
    P = Prog()

    big = (mode == "full")
    io = MODE_IO.get(mode, {})

    def din(name, shape, dt=F32, heavy=False):
        if heavy and not big:
            shape = [1, 1]
        return nc.dram_tensor(name, list(shape), dt, kind="ExternalInput").ap()

    def dout(name, shape, dt=F32):
        return nc.dram_tensor(name, list(shape), dt, kind="ExternalOutput").ap()

    def dscr(name, shape, dt):
        if name in io:
            return nc.dram_tensor(name, list(shape), dt, kind=io[name]).ap()
        return nc.dram_tensor(name, list(shape), dt).ap()

    x_in = din("x_in", [NTOK, D], heavy=True)
    vecs = din("vecs", [NVROWS, 128])
    ident_in = din("ident", [128, 128])
    ada_w = din("ada_w", [DEPTH, D, 6 * D], heavy=True)
    ab_w_in = din("ab_w_in", [2, D, ABIN], heavy=True)
    ab_w_out = din("ab_w_out", [2, ABOUT, D], heavy=True)
    conv_w_in = din("conv_w_in", [2, D, 3 * D], heavy=True)
    conv_w_out = din("conv_w_out", [2, D, D], heavy=True)
    mlp_w1 = din("mlp_w1", [DEPTH, D, DFF], heavy=True)
    mlp_w2 = din("mlp_w2", [DEPTH, DFF, D], heavy=True)
    cache_k = din("cache_k", [2, 256, 8, 128])
    cache_v = din("cache_v", [2, 256, 8, 128])
    na_rpb = din("na_rpb", [2, 8, 15, 31])
    glu_w = din("glu_w", [2, 512, 512])
    s5p = din("s5p", [2, 2, 128, S5P_COLS])
    s5h0 = din("s5h0", [2, 2, 2, 128, 16])
    y_out = dout("y", [NTOK, D])
    nck = dout("nck", [4, 2, 256, 8, 128])
    ncv = dout("ncv", [4, 2, 256, 8, 128])
    nsr = dout("nsr", [4, 2, 2, 32, 64])
    nsi = dout("nsi", [4, 2, 2, 32, 64])
    XS = dscr("XS", [D, NTOK], F32)
    ZC = dscr("ZC", [D, NTOK], F32)
    GB = dscr("GB", [D, NTOK], F32)
    MIX = dscr("MIX", [ABOUT, NTOK], BF16)
    Usc = dscr("Usc", [512, NTOK], BF16)
    Qsc = dscr("Qsc", [1024, NTOK], BF16)
    Ksc = dscr("Ksc", [1024, NTOK], BF16)
    Vsc = dscr("Vsc", [NTOK, 1024], BF16)
    BIAS = dscr("BIAS", [2, NTYPES, 128, 8, 640], F32)
    glub = [dscr(f"glub{e}", [512, 512], BF16) for e in range(2)]
    w1b = [dscr(f"w1b{l}", [D, DFF], BF16) for l in range(DEPTH)]
    w2b = [dscr(f"w2b{l}", [16, 128, 64, 128], BF16) for l in range(DEPTH)]
    cwib = [dscr(f"cwib{o}", [D, 3 * D], BF16) for o in range(2)]
    cwob = [dscr(f"cwob{o}", [D, D], BF16) for o in range(2)]
    awib = [dscr(f"awib{e}", [D, ABIN], BF16) for e in range(2)]
    awob = [dscr(f"awob{e}", [ABOUT, D], BF16) for e in range(2)]

    es = contextlib.ExitStack()
    with es:
        ARENA = 206 * 1024
        arena = es.enter_context(nc.sbuf_tensor("arena", [128, ARENA], U8))
        psall = es.enter_context(nc.psum_tensor("psall", [128, 4096], F32))[:]
        psb = [psall[:, i * 512:(i + 1) * 512] for i in range(8)]
        PS = [Buf(f"ps{i}") for i in range(8)]
        sems = {e: es.enter_context(nc.semaphore("s_" + e)) for e in ENGS}

        def view(off, shape, dt):
            esz = 4 if dt == F32 else 2
            n = 1
            for s_ in shape[1:]:
                n *= s_
            a = arena[0:shape[0], off:off + n * esz].bitcast(dt)
            if len(shape) == 2:
                return a
            names = " ".join(f"d{i}" for i in range(1, len(shape)))
            kw = {f"d{i}": shape[i] for i in range(1, len(shape) - 1)}
            return a.rearrange(f"p ({names}) -> p {names}", **kw)

        class Bump:
            def __init__(self, base=0):
                self.off = base

            def get(self, shape, dt):
                esz = 4 if dt == F32 else 2
                n = 1
                for s_ in shape[1:]:
                    n *= s_
                o = self.off
                self.off = (o + n * esz + 63) // 64 * 64
                assert self.off <= ARENA, ("SBUF arena overflow", self.off)
                return view(o, shape, dt)

        pers = Bump(0)
        ident = pers.get([128, 128], F32)
        identb = pers.get([128, 128], BF16)
        onesb = pers.get([128, 128], BF16)
        epsc = pers.get([128, 1], F32)
        VT = pers.get([128, NVROWS], F32)
        ADA = pers.get([128, DEPTH, 96, 2], F32)
        G1 = pers.get([128, DEPTH, 2, KC], F32)
        G2 = pers.get([128, DEPTH, 2, KC], F32)
        QKG = pers.get([128, 4], F32)
        B_const = Buf("const")
        B_ada = Buf("ada")
        PERS_END = pers.off

        d_misc = P.dsem("misc")

        ph = Bump(PERS_END)
        vstage = ph.get([128, NVROWS // 128 + 1, 128], F32)
        B_vst = Buf("vstage")
        nvt = (NVROWS + 127) // 128
        P.dma("sp", lambda e: e.dma_start(out=ident, in_=ident_in[:, :]), d_misc, writes=[B_const])
        for i in range(nvt):
            r0 = i * 128
            r1 = min(NVROWS, r0 + 128)
            dd = P.dsem(f"vst{i}")
            bi = Buf(f"vst{i}")
            P.dma("sp", lambda e, i=i, r0=r0, r1=r1: e.dma_start(out=vstage[0:r1 - r0, i, :], in_=vecs[r0:r1, :]),
                  dd, writes=[bi])
            P.op("pe", lambda e, i=i, r0=r0, r1=r1: e.transpose(out=psb[i % 2][:, 0:r1 - r0],
                                                                  in_=vstage[0:r1 - r0, i, :],
                                                                  identity=ident[0:r1 - r0, 0:r1 - r0]),
                 reads=[bi, B_const], writes=[PS[i % 2]])
            P.op("dve", lambda e, i=i, r0=r0, r1=r1: e.tensor_copy(out=VT[:, r0:r1], in_=psb[i % 2][:, 0:r1 - r0]),
                 reads=[PS[i % 2]], writes=[B_vst])
        P.op("dve", lambda e: e.tensor_copy(out=identb, in_=ident), reads=[B_const], writes=[B_const])
        P.op("dve", lambda e: e.memset(onesb, 1.0), writes=[B_const])
        P.op("dve", lambda e: e.memset(epsc, EPS), writes=[B_const])
        B_VT = B_vst

        WB = {}

        def cast2d(dst, src, rows, cols, dsm):
            piece = cols if cols <= 2048 else (2048 if cols % 2048 == 0 else 1792)
            nrp = 1024
            for r0 in range(0, rows, nrp):
                r1 = min(rows, r0 + nrp)
                s_ = src[r0:r1, :].rearrange("k (a n) -> k a n", n=piece)
                d_ = dst[r0:r1, :].rearrange("k (a n) -> k a n", n=piece)
                P.dma("pool", lambda e, s_=s_, d_=d_: e.dma_start(out=d_, in_=s_), dsm)

        cast_sems = []
        if not big:
            for l in range(n_layers):
                WB[l] = Buf(f"wb{l}")
            P.op("dve", lambda e: e.memset(ADA, 0.0), writes=[Buf("x")])
            P.op("dve", lambda e: e.memset(G1, 1.0), writes=[Buf("x")])
            P.op("dve", lambda e: e.memset(G2, 1.0), writes=[Buf("x")])
        for l in range(n_layers if big else 0):
            dsm = P.dsem(f"cast{l}")
            cast_sems.append(dsm)
            if l % 2 == 0:
                e_ = l // 2
                if with_ab:
                    cast2d(awib[e_], ab_w_in[e_], D, ABIN, dsm)
                    cast2d(awob[e_], ab_w_out[e_], ABOUT, D, dsm)
                    cast2d(glub[e_], glu_w[e_], 512, 512, dsm)
            else:
                o_ = l // 2
                cast2d(cwib[o_], conv_w_in[o_], D, 3 * D, dsm)
                cast2d(cwob[o_], conv_w_out[o_], D, D, dsm)
            cast2d(w1b[l], mlp_w1[l], D, DFF, dsm)
            for n in range(16):
                s_ = mlp_w2[l][:, n * 128:(n + 1) * 128].rearrange("(kc p) m -> p kc m", p=128)
                P.dma("pool", lambda e, s_=s_, n=n, l=l: e.dma_start(out=w2b[l][n], in_=s_), dsm)
            b = Buf(f"wb{l}")
            b.w = ("D", dsm, 16 * dsm.count)
            WB[l] = b

        sc = ph.get([128, 2, KC], F32)
        sc2 = ph.get([128, KC, 2], F32)
        B_sc = Buf("sc")
        P.op("act", lambda e: e.activation(out=sc, in_=VT[:, VOFF["cvec"]:VOFF["cvec"] + 32].rearrange(
            "p (v k) -> p v k", v=2), func=AF.Silu), reads=[B_VT], writes=[B_sc])
        P.op("dve", lambda e: e.tensor_copy(out=sc2, in_=sc.rearrange("p v k -> p k v")), reads=[B_sc], writes=[B_sc])
        NCOL = 512
        awbuf = [ph.get([128, KC, NCOL], F32) for _ in range(2)]
        B_aw = [Buf("aw0"), Buf("aw1")]
        d_aw = [P.dsem("aw0"), P.dsem("aw1")]
        nblk = 6 * D // NCOL
        it = 0
        for l in range(n_layers if big else 0):
            for b in range(nblk):
                s = it % 2
                src = ada_w[l][:, b * NCOL:(b + 1) * NCOL].rearrange("(kc p) n -> p kc n", p=128)
                P.dma("sp", lambda e, s=s, src=src: e.dma_start(out=awbuf[s], in_=src), d_aw[s], writes=[B_aw[s]])
                pb = 2 + (it % 2)

                def mm(e, s=s, b=b, pb=pb):
                    ins = None
                    for jl in range(NCOL // 128):
                        for kc in range(KC):
                            ins = e.matmul(psb[pb][:, jl * 2:jl * 2 + 2], awbuf[s][:, kc, jl * 128:(jl + 1) * 128],
                                           sc2[:, kc, :], start=(kc == 0), stop=(kc == KC - 1))
                    return ins
                P.op("pe", mm, reads=[B_aw[s], B_sc], writes=[PS[pb]])
                j0 = b * (NCOL // 128)
                for cvi in range(2):
                    P.op("dve", lambda e, l=l, j0=j0, pb=pb, cvi=cvi: e.tensor_tensor(
                        out=ADA[:, l, j0:j0 + 4, cvi], in0=psb[pb][:, 0:8].rearrange("p (j v) -> p j v", v=2)[:, :, cvi],
                        in1=VT[:, VOFF["ada_b"] + l * 96 + j0: VOFF["ada_b"] + l * 96 + j0 + 4], op=ALU.add),
                        reads=[PS[pb], B_VT], writes=[B_ada])
                it += 1
        for l in range(n_layers if big else 0):
            for cv in range(2):
                for (G, sidx, nm) in ((G1, 1, "norm1_g"), (G2, 4, "norm2_g")):
                    P.op("dve", lambda e, l=l, cv=cv, G=G, sidx=sidx, nm=nm: e.scalar_tensor_tensor(
                        out=G[:, l, cv, :], in0=ADA[:, l, sidx * 16:(sidx + 1) * 16, cv], scalar=1.0,
                        in1=VT[:, VOFF[nm] + l * 16: VOFF[nm] + (l + 1) * 16], op0=ALU.add, op1=ALU.mult),
                        reads=[B_ada, B_VT], writes=[B_ada])

        def ada_ap(l, sidx, cv, kc):
            return ADA[:, l, sidx * 16 + kc, cv:cv + 1]

        XSB = [Buf(f"xs{t}") for t in range(NT)]
        ZCB = [Buf(f"zc{t}") for t in range(NT)]
        GBB = [Buf(f"gb{t}") for t in range(NT)]
        MIXB = [Buf(f"mixd{t}") for t in range(NT)]
        xin_t = [ph.get([128, D], F32) for _ in range(2)]
        xst = [ph.get([128, KC, 128], F32) for _ in range(2)]
        B_xin = [Buf("xin0"), Buf("xin1")]
        B_xst = [Buf("xst0"), Buf("xst1")]
        d_xin = [P.dsem("xin0"), P.dsem("xin1")]
        d_xst = [P.dsem("xst0"), P.dsem("xst1")]
        for tb in range(NTOK // 128 if big else 0):
            s = tb % 2
            P.dma("sp", lambda e, s=s, tb=tb: e.dma_start(out=xin_t[s], in_=x_in[tb * 128:(tb + 1) * 128, :]),
                  d_xin[s], writes=[B_xin[s]])
            for q in range(4):
                pb = 4 + (tb * 4 + q) % 4

                def tr(e, s=s, q=q, pb=pb):
                    ins = None
                    for i in range(4):
                        kc = q * 4 + i
                        ins = e.transpose(out=psb[pb][:, i * 128:(i + 1) * 128], in_=xin_t[s][:, kc * 128:(kc + 1) * 128],
                                          identity=ident)
                    return ins
                P.op("pe", tr, reads=[B_xin[s], B_const], writes=[PS[pb]])
                eng = "act" if q % 2 == 0 else "dve"
                if eng == "act":
                    P.op("act", lambda e, s=s, q=q, pb=pb: e.copy(
                        out=xst[s][:, q * 4:(q + 1) * 4, :], in_=psb[pb].rearrange("p (a b) -> p a b", a=4)),
                        reads=[PS[pb]], writes=[B_xst[s]])
                else:
                    P.op("dve", lambda e, s=s, q=q, pb=pb: e.tensor_copy(
                        out=xst[s][:, q * 4:(q + 1) * 4, :], in_=psb[pb].rearrange("p (a b) -> p a b", a=4)),
                        reads=[PS[pb]], writes=[B_xst[s]])
            dst = XS[:, tb * 128:(tb + 1) * 128].rearrange("(kc p) t -> p kc t", p=128)
            P.dma("sp", lambda e, s=s, dst=dst: e.dma_start(out=dst, in_=xst[s]), d_xst[s],
                  reads=[B_xst[s]], writes=[XSB[tb // 4]])
        P.barrier()

        def tile_cv(t):
            return 0 if t < NST else 1

        def norm_modulate(l, which, t, xt, B_x, hbuf, B_h, sqtmp, B_sq, rstd, B_r, tmp, B_tmp, psn):
            cv = tile_cv(t)
            G = G1 if which == 1 else G2
            sidx = 0 if which == 1 else 3
            P.op("act", lambda e: e.activation(out=hbuf, in_=xt, func=AF.Square), reads=B_x, writes=[B_h])

            def mm(e):
                ins = None
                for kc in range(KC):
                    ins = e.matmul(psb[psn], onesb, hbuf[:, kc, :], start=(kc == 0), stop=(kc == KC - 1))
                return ins
            P.op("pe", mm, reads=[B_h, B_const], writes=[PS[psn]])
            P.op("act", lambda e: e.activation(out=sqtmp, in_=psb[psn], func=AF.Sqrt, bias=epsc[:, 0:1], scale=1.0 / D),
                 reads=[PS[psn], B_const], writes=[B_sq])
            P.op("dve", lambda e: e.reciprocal(out=rstd, in_=sqtmp), reads=[B_sq], writes=[B_r])
            for kc in range(KC):
                s = kc % 2
                P.op("dve", lambda e, kc=kc, s=s: e.scalar_tensor_tensor(
                    out=tmp[s], in0=xt[:, kc, :], scalar=G[:, l, cv, kc:kc + 1], in1=rstd,
                    op0=ALU.mult, op1=ALU.mult), reads=B_x + [B_r, B_ada], writes=[B_tmp[s]])
                P.op("act", lambda e, kc=kc, s=s: e.activation(
                    out=hbuf[:, kc, :], in_=tmp[s], func=AF.Identity, bias=ada_ap(l, sidx, cv, kc), scale=1.0),
                    reads=[B_tmp[s], B_ada], writes=[B_h])

        def phase_A_conv(l):
            if True:
                ph = Bump(PERS_END)
                xt2 = [ph.get([128, KC, TT], F32) for _ in range(2)]
                B_x2 = [Buf("xA0"), Buf("xA1")]
                d_x2 = [P.dsem(f"xA0_{l}"), P.dsem(f"xA1_{l}")]
                hbuf = ph.get([128, KC, TT], BF16)
                B_h = Buf("hA")
                sqtmp = ph.get([128, TT], F32)
                rstd = ph.get([128, TT], F32)
                tmp = [ph.get([128, TT], F32) for _ in range(2)]
                B_sq, B_r, B_tmp = Buf("sq"), Buf("rstd"), [Buf("tmp0"), Buf("tmp1")]
                wblk = [ph.get([128, KC, 512], BF16) for _ in range(4)]
                B_w = [Buf(f"wA{i}") for i in range(4)]
                d_w = [P.dsem(f"wA{i}_{l}") for i in range(4)]
                tA = [ph.get([128, TT], F32) for _ in range(2)]
                B_tA = [Buf("tA0"), Buf("tA1")]
                zst = [ph.get([128, 4, TT], F32) for _ in range(2)]
                B_zst = [Buf("zst0"), Buf("zst1")]
                d_zst = [P.dsem(f"zst0_{l}"), P.dsem(f"zst1_{l}")]
                wi = 0
                zi = 0
                pbi = 0
                wsrc = cwib[l // 2]

                def load_x(t):
                    s = t % 2
                    src = XS[:, t * TT:(t + 1) * TT].rearrange("(kc p) t -> p kc t", p=128)
                    P.dma("sp", lambda e, s=s, src=src: e.dma_start(out=xt2[s], in_=src), d_x2[s],
                          reads=[XSB[t]], writes=[B_x2[s]])
                load_x(0)
                for t in range(NT):
                    s = t % 2
                    if t + 1 < NT:
                        load_x(t + 1)
                    norm_modulate(l, 1, t, xt2[s], [B_x2[s]], hbuf, B_h, sqtmp, B_sq, rstd, B_r, tmp, B_tmp, 0)
                    for b in range(4):
                        ws = wi % 4
                        wi += 1
                        src = wsrc[:, b * 512:(b + 1) * 512].rearrange("(kc p) n -> p kc n", p=128)
                        P.dma("sp", lambda e, ws=ws, src=src: e.dma_start(out=wblk[ws], in_=src), d_w[ws],
                              reads=[WB[l]], writes=[B_w[ws]])
                        zs = zi % 2
                        zi += 1
                        for jl in range(4):
                            pb = 1 + pbi % 6
                            pbi += 1

                            def mm(e, ws=ws, jl=jl, pb=pb):
                                ins = None
                                for kc in range(KC):
                                    ins = e.matmul(psb[pb], wblk[ws][:, kc, jl * 128:(jl + 1) * 128], hbuf[:, kc, :],
                                                   start=(kc == 0), stop=(kc == KC - 1))
                                return ins
                            P.op("pe", mm, reads=[B_w[ws], B_h], writes=[PS[pb]])
                            P.op("act", lambda e, zs=zs, jl=jl, pb=pb: e.copy(out=zst[zs][:, jl, :], in_=psb[pb]),
                                 reads=[PS[pb]], writes=[B_zst[zs]])
                        dst = GB[b * 512:(b + 1) * 512, t * TT:(t + 1) * TT].rearrange("(c p) t -> p c t", p=128)
                        P.dma("pool", lambda e, zs=zs, dst=dst: e.dma_start(out=dst, in_=zst[zs]), d_zst[zs],
                              reads=[B_zst[zs]], writes=[GBB[t]])
                    for b in range(4):
                        wsa = wi % 4
                        wsb = (wi + 1) % 4
                        wi += 2
                        srca = wsrc[:, 2048 + b * 512:2048 + (b + 1) * 512].rearrange("(kc p) n -> p kc n", p=128)
                        srcb = wsrc[:, 4096 + b * 512:4096 + (b + 1) * 512].rearrange("(kc p) n -> p kc n", p=128)
                        P.dma("sp", lambda e, ws=wsa, src=srca: e.dma_start(out=wblk[ws], in_=src), d_w[wsa],
                              reads=[WB[l]], writes=[B_w[wsa]])
                        P.dma("sp", lambda e, ws=wsb, src=srcb: e.dma_start(out=wblk[ws], in_=src), d_w[wsb],
                              reads=[WB[l]], writes=[B_w[wsb]])
                        zs = zi % 2
                        zi += 1
                        for jl in range(4):
                            pa = 1 + pbi % 6
                            pbb = 1 + (pbi + 1) % 6
                            pbi += 2

                            def mma(e, ws=wsa, jl=jl, pb=pa):
                                ins = None
                                for kc in range(KC):
                                    ins = e.matmul(psb[pb], wblk[ws][:, kc, jl * 128:(jl + 1) * 128], hbuf[:, kc, :],
                                                   start=(kc == 0), stop=(kc == KC - 1))
                                return ins
                            P.op("pe", mma, reads=[B_w[wsa], B_h], writes=[PS[pa]])
                            P.op("pe", lambda e, ws=wsb, jl=jl, pb=pbb: mma(e, ws, jl, pb), reads=[B_w[wsb], B_h],
                                 writes=[PS[pbb]])
                            ts = jl % 2
                            P.op("act", lambda e, ts=ts, pb=pa: e.copy(out=tA[ts], in_=psb[pb]),
                                 reads=[PS[pa]], writes=[B_tA[ts]])
                            P.op("dve", lambda e, ts=ts, zs=zs, jl=jl, pb=pbb: e.tensor_tensor(
                                out=zst[zs][:, jl, :], in0=tA[ts], in1=psb[pb], op=ALU.mult),
                                reads=[B_tA[ts], PS[pbb]], writes=[B_zst[zs]])
                        dst = ZC[b * 512:(b + 1) * 512, t * TT:(t + 1) * TT].rearrange("(c p) t -> p c t", p=128)
                        P.dma("pool", lambda e, zs=zs, dst=dst: e.dma_start(out=dst, in_=zst[zs]), d_zst[zs],
                              reads=[B_zst[zs]], writes=[ZCB[t]])
                P.barrier()

        def phase_D(l):
            is_ab = (l % 2 == 0)
            e_ = l // 2
            o_ = l // 2
            last = (l == n_layers - 1)
            ph = Bump(PERS_END)
            xts = [ph.get([128, KC, TT], F32) for _ in range(2)]
            B_xks = [[Buf(f"xk{s_}_{k}") for k in range(KC)] for s_ in range(2)]
            d_xs = [P.dsem(f"xD0_{l}"), P.dsem(f"xD1_{l}")]
            hbuf = ph.get([128, KC, TT], BF16)
            B_h = Buf("hD")
            d_mix = P.dsem(f"mix_{l}")
            a_off = ph.off
            abuf = ph.get([128, 64, TT], BF16)
            B_a = [Buf(f"a{k}") for k in range(64)]
            sqtmp = ph.get([128, TT], F32)
            rstd = ph.get([128, TT], F32)
            tmp = [ph.get([128, TT], F32) for _ in range(2)]
            B_sq, B_r, B_tmp = Buf("sq"), Buf("rstd"), [Buf("tmp0"), Buf("tmp1")]
            rl, B_rl = tmp, B_tmp
            wblk = [ph.get([128, KC, 512], BF16) for _ in range(2)]
            B_w = [Buf("wD0"), Buf("wD1")]
            d_w = [P.dsem(f"wD0_{l}"), P.dsem(f"wD1_{l}")]
            have_mix = (not is_ab) or with_ab
            KM = 12 if is_ab else 16
            wsrc = awob[e_] if is_ab else cwob[o_]
            if not is_ab:
                zcb = [ph.get([128, TT + 2], F32) for _ in range(2)]
                gbb = [ph.get([128, TT], F32) for _ in range(2)]
                B_zc = [Buf("zcb0"), Buf("zcb1")]
                B_gb = [Buf("gbb0"), Buf("gbb1")]
                d_zc = [P.dsem(f"zcb0_{l}"), P.dsem(f"zcb1_{l}")]
                d_gb = [P.dsem(f"gbb0_{l}"), P.dsem(f"gbb1_{l}")]
                cv1 = [ph.get([128, TT], F32) for _ in range(2)]
                B_cv = [Buf("cv0"), Buf("cv1")]
                cwo = VOFF["conv_w"] + o_ * 48
                cbo = VOFF["conv_b"] + o_ * 16
            if last:
                ost = [view(a_off + i * 8192, [128, D], F32) for i in range(2)]
                B_ost = [B_a[0:8], B_a[8:16]]
                d_ost = [P.dsem("ost0"), P.dsem("ost1")]
            cnt = {"w": 0, "pb": 0, "o": 0}

            def load_x(t):
                s_ = t % 2
                src = XS[:, t * TT:(t + 1) * TT].rearrange("(kc p) t -> p kc t", p=128)
                P.dma("sp", lambda e: e.dma_start(out=xts[s_], in_=src), d_xs[s_], reads=[XSB[t]], writes=B_xks[s_])

            def prep_mix_ab(t):
                if have_mix:
                    srcm = MIX[:, t * TT:(t + 1) * TT].rearrange("(c p) t -> p c t", p=128)
                    P.dma("sp", lambda e: e.dma_start(out=hbuf[:, 0:12, :], in_=srcm), d_mix,
                          reads=[MIXB[t]], writes=[B_h])

            def prep_mix_conv_chunk(t, j):
                s = j % 2
                segs = [(0, TT)] if t < NST else [(0, 256), (256, 256)]
                t0 = t * TT
                lo_ok = (t < NST and t > 0)
                hi_ok = (t < NST - 1)
                c0 = t0 - (1 if lo_ok else 0)
                c1 = t0 + TT + (1 if hi_ok else 0)
                srcz = ZC[j * 128:(j + 1) * 128, c0:c1]
                o0 = 0 if lo_ok else 1
                rd = [ZCB[t]]
                if lo_ok:
                    rd.append(ZCB[t - 1])
                if hi_ok:
                    rd.append(ZCB[t + 1])
                if not lo_ok:
                    P.op("pool", lambda e: e.memset(zcb[s][:, 0:1], 0.0), writes=[B_zc[s]])
                if not hi_ok:
                    P.op("pool", lambda e: e.memset(zcb[s][:, TT + 1:TT + 2], 0.0), writes=[B_zc[s]])
                n_ = c1 - c0
                P.dma("sp", lambda e: e.dma_start(out=zcb[s][:, o0:o0 + n_], in_=srcz), d_zc[s], reads=rd, writes=[B_zc[s]])
                srcg = GB[j * 128:(j + 1) * 128, t0:t0 + TT]
                P.dma("sp", lambda e: e.dma_start(out=gbb[s], in_=srcg), d_gb[s], reads=[GBB[t]], writes=[B_gb[s]])
                w0 = VT[:, cwo + 0 * 16 + j: cwo + 0 * 16 + j + 1]
                w1 = VT[:, cwo + 1 * 16 + j: cwo + 1 * 16 + j + 1]
                w2 = VT[:, cwo + 2 * 16 + j: cwo + 2 * 16 + j + 1]
                cb = VT[:, cbo + j: cbo + j + 1]
                P.op("dve", lambda e: e.tensor_scalar(out=cv1[s], in0=zcb[s][:, 1:TT + 1], scalar1=w1, scalar2=cb,
                                                      op0=ALU.mult, op1=ALU.add), reads=[B_zc[s], B_VT], writes=[B_cv[s]])
                for (a0, ln) in segs:
                    lskip = 1 if (a0 > 0) else 0
                    P.op("dve", lambda e, a0=a0, ln=ln, lskip=lskip: e.scalar_tensor_tensor(
                        out=cv1[s][:, a0 + lskip:a0 + ln], in0=zcb[s][:, a0 + lskip:a0 + ln], scalar=w0,
                        in1=cv1[s][:, a0 + lskip:a0 + ln], op0=ALU.mult, op1=ALU.add),
                        reads=[B_zc[s], B_VT, B_cv[s]], writes=[B_cv[s]])
                    rskip = 1 if (a0 + ln < TT) else 0
                    P.op("dve", lambda e, a0=a0, ln=ln, rskip=rskip: e.scalar_tensor_tensor(
                        out=cv1[s][:, a0:a0 + ln - rskip], in0=zcb[s][:, a0 + 2:a0 + 2 + ln - rskip], scalar=w2,
                        in1=cv1[s][:, a0:a0 + ln - rskip], op0=ALU.mult, op1=ALU.add),
                        reads=[B_zc[s], B_VT, B_cv[s]], writes=[B_cv[s]])
                P.op("pool", lambda e: e.tensor_tensor(out=hbuf[:, j, :], in0=cv1[s], in1=gbb[s], op=ALU.mult),
                     reads=[B_cv[s], B_gb[s]], writes=[B_h])

            def load_wblk(src_ap, kdim):
                ws = cnt["w"] % 2
                cnt["w"] += 1
                P.dma("sp", lambda e: e.dma_start(out=wblk[ws][:, 0:kdim, :], in_=src_ap), d_w[ws],
                      reads=[WB[l]], writes=[B_w[ws]])
                return ws

            def next_pb():
                pb = 1 + cnt["pb"] % 6
                cnt["pb"] += 1
                return pb

            load_x(0)
            if is_ab:
                prep_mix_ab(0)
            else:
                for j in range(KC):
                    prep_mix_conv_chunk(0, j)
            for t in range(NT):
                cv = tile_cv(t)
                xt = xts[t % 2]
                B_xk = B_xks[t % 2]
                if have_mix:
                    for b in range(4):
                        ws = load_wblk(wsrc[:, b * 512:(b + 1) * 512].rearrange("(kc p) n -> p kc n", p=128), KM)
                        for jl in range(4):
                            j = b * 4 + jl
                            pb = next_pb()

                            def mm(e, ws=ws, jl=jl, pb=pb):
                                ins = None
                                for kc in range(KM):
                                    ins = e.matmul(psb[pb], wblk[ws][:, kc, jl * 128:(jl + 1) * 128], hbuf[:, kc, :],
                                                   start=(kc == 0), stop=(kc == KM - 1))
                                return ins
                            P.op("pe", mm, reads=[B_w[ws], B_h], writes=[PS[pb]])
                            P.op("dve", lambda e, j=j, pb=pb, xt=xt, cv=cv: e.scalar_tensor_tensor(
                                out=xt[:, j, :], in0=psb[pb], scalar=ada_ap(l, 2, cv, j), in1=xt[:, j, :],
                                op0=ALU.mult, op1=ALU.add), reads=[PS[pb], B_ada, B_xk[j]], writes=[B_xk[j]])
                norm_modulate(l, 2, t, xt, B_xk, hbuf, B_h, sqtmp, B_sq, rstd, B_r, tmp, B_tmp, 0)
                for b in range(16):
                    ws = load_wblk(w1b[l][:, b * 512:(b + 1) * 512].rearrange("(kc p) n -> p kc n", p=128), KC)
                    for jl in range(4):
                        j = b * 4 + jl
                        pb = next_pb()

                        def mm(e, ws=ws, jl=jl, pb=pb):
                            ins = None
                            for kc in range(KC):
                                ins = e.matmul(psb[pb], wblk[ws][:, kc, jl * 128:(jl + 1) * 128], hbuf[:, kc, :],
                                               start=(kc == 0), stop=(kc == KC - 1))
                            return ins
                        P.op("pe", mm, reads=[B_w[ws], B_h], writes=[PS[pb]])
                        rs = j % 2
                        P.op("act", lambda e, rs=rs, pb=pb: e.activation(out=rl[rs], in_=psb[pb], func=AF.Relu),
                             reads=[PS[pb]], writes=[B_rl[rs]])
                        P.op("pool", lambda e, rs=rs, j=j: e.tensor_tensor(out=abuf[:, j, :], in0=rl[rs], in1=rl[rs],
                                                                           op=ALU.mult),
                             reads=[B_rl[rs]], writes=[B_a[j]])
                for n in range(16):
                    ws = cnt["w"] % 2
                    cnt["w"] += 1
                    P.dma("sp", lambda e, ws=ws, n=n: e.dma_start(
                        out=wblk[ws].rearrange("p a b -> p (a b)").rearrange("p (k m) -> p k m", m=128),
                        in_=w2b[l][n]), d_w[ws], reads=[WB[l]], writes=[B_w[ws]])
                    if t + 1 < NT:
                        if n == 0:
                            load_x(t + 1)
                            if is_ab:
                                prep_mix_ab(t + 1)
                        if not is_ab:
                            prep_mix_conv_chunk(t + 1, n)
                    pb = next_pb()

                    def mm(e, ws=ws, pb=pb):
                        w = wblk[ws].rearrange("p a b -> p (a b)").rearrange("p (k m) -> p k m", m=128)
                        ins = None
                        for kc in range(64):
                            ins = e.matmul(psb[pb], w[:, kc, :], abuf[:, kc, :], start=(kc == 0), stop=(kc == 63))
                        return ins
                    P.op("pe", mm, reads=[B_w[ws]] + B_a, writes=[PS[pb]])
                    P.op("dve", lambda e, n=n, pb=pb, xt=xt, cv=cv: e.scalar_tensor_tensor(
                        out=xt[:, n, :], in0=psb[pb], scalar=ada_ap(l, 5, cv, n), in1=xt[:, n, :],
                        op0=ALU.mult, op1=ALU.add), reads=[PS[pb], B_ada, B_xk[n]], writes=[B_xk[n]])
                if not last:
                    dst = XS[:, t * TT:(t + 1) * TT].rearrange("(kc p) t -> p kc t", p=128)
                    P.dma("pool", lambda e, dst=dst, xt=xt: e.dma_start(out=dst, in_=xt), d_xs[t % 2], reads=B_xk,
                          writes=[XSB[t]])
                else:
                    for tb in range(4):
                        os_ = cnt["o"] % 2
                        cnt["o"] += 1
                        for q in range(4):
                            pb = next_pb()

                            def tr(e, tb=tb, q=q, pb=pb, xt=xt):
                                ins = None
                                for i in range(4):
                                    kc = q * 4 + i
                                    ins = e.transpose(out=psb[pb][:, i * 128:(i + 1) * 128],
                                                      in_=xt[:, kc, tb * 128:(tb + 1) * 128], identity=ident)
                                return ins
                            P.op("pe", tr, reads=B_xk[q * 4:q * 4 + 4] + [B_const], writes=[PS[pb]])
                            if q % 2 == 0:
                                P.op("act", lambda e, os_=os_, q=q, pb=pb: e.copy(
                                    out=ost[os_][:, q * 512:(q + 1) * 512], in_=psb[pb]),
                                    reads=[PS[pb]], writes=B_ost[os_])
                            else:
                                P.op("dve", lambda e, os_=os_, q=q, pb=pb: e.tensor_copy(
                                    out=ost[os_][:, q * 512:(q + 1) * 512], in_=psb[pb]),
                                    reads=[PS[pb]], writes=B_ost[os_])
                        r0 = t * TT + tb * 128
                        P.dma("pool", lambda e, os_=os_, r0=r0: e.dma_start(out=y_out[r0:r0 + 128, :], in_=ost[os_]),
                              d_ost[os_], reads=B_ost[os_])
            P.barrier()

        B_U = [Buf(f"U{t}") for t in range(NT)]
        B_Q = [Buf(f"Q{t}") for t in range(NT)]
        B_K = [Buf(f"K{t}") for t in range(NT)]
        B_V = [Buf(f"V{t}") for t in range(NT)]
        B_BIAS = Buf("BIAS")
        P.op("dve", lambda e: e.tensor_scalar(out=QKG[:, 0:2], in0=VT[:, VOFF["q_g"]:VOFF["q_g"] + 2],
                                              scalar1=float(128 ** -0.5), scalar2=None, op0=ALU.mult),
             reads=[B_VT], writes=[B_ada])
        P.op("dve", lambda e: e.tensor_copy(out=QKG[:, 2:4], in_=VT[:, VOFF["k_g"]:VOFF["k_g"] + 2]),
             reads=[B_VT], writes=[B_ada])

        def phase_A_ab(l):
            e_ = l // 2
            ph = Bump(PERS_END)
            xt2 = [ph.get([128, KC, TT], F32) for _ in range(2)]
            B_x2 = [Buf("xA0"), Buf("xA1")]
            d_x2 = [P.dsem(f"xB0_{l}"), P.dsem(f"xB1_{l}")]
            hbuf = ph.get([128, KC, TT], BF16)
            B_h = Buf("hA")
            sqtmp = ph.get([128, TT], F32)
            rstd = ph.get([128, TT], F32)
            tmp = [ph.get([128, TT], F32) for _ in range(2)]
            B_sq, B_r, B_tmp = Buf("sq"), Buf("rstd"), [Buf("tmp0"), Buf("tmp1")]
            wblk = [ph.get([128, KC, 512], BF16) for _ in range(2)]
            B_w = [Buf("wB0"), Buf("wB1")]
            d_w = [P.dsem(f"wB0_{l}"), P.dsem(f"wB1_{l}")]
            stg = [ph.get([128, 4, TT], BF16) for _ in range(2)]
            B_stg = [Buf("stg0"), Buf("stg1")]
            d_stg = [P.dsem(f"stg0_{l}"), P.dsem(f"stg1_{l}")]
            hsq = [ph.get([128, TT], BF16) for _ in range(2)]
            B_hsq = [Buf("hsq0"), Buf("hsq1")]
            t1 = [ph.get([128, TT], F32) for _ in range(2)]
            B_t1 = [Buf("t10"), Buf("t11")]
            r1 = [ph.get([128, TT], F32) for _ in range(2)]
            B_r1 = [Buf("r10"), Buf("r11")]
            kf = [ph.get([128, TT], F32) for _ in range(2)]
            B_kf = [Buf("kf0"), Buf("kf1")]
            ktok = [ph.get([128, 4, 128], F32) for _ in range(2)]
            B_ktok = [Buf("ktok0"), Buf("ktok1")]
            d_ktok = [P.dsem(f"ktok0_{l}"), P.dsem(f"ktok1_{l}")]
            vf = [ph.get([128, TT], F32) for _ in range(2)]
            B_vf = [Buf("vf0"), Buf("vf1")]
            d_vf = [P.dsem(f"vf0_{l}"), P.dsem(f"vf1_{l}")]
            cnt = {"w": 0, "pb": 0, "st": 0, "hn": 0, "kt": 0, "vf": 0, "pn": 0}

            def load_x(t):
                s = t % 2
                src = XS[:, t * TT:(t + 1) * TT].rearrange("(kc p) t -> p kc t", p=128)
                P.dma("sp", lambda e, s=s, src=src: e.dma_start(out=xt2[s], in_=src), d_x2[s],
                      reads=[XSB[t]], writes=[B_x2[s]])

            def load_w(b):
                ws = cnt["w"] % 2
                cnt["w"] += 1
                src = awib[e_][:, b * 512:(b + 1) * 512].rearrange("(kc p) n -> p kc n", p=128)
                P.dma("sp", lambda e, ws=ws, src=src: e.dma_start(out=wblk[ws], in_=src), d_w[ws],
                      reads=[WB[l]], writes=[B_w[ws]])
                return ws

            def proj_fm(ws, jl):
                pb = 1 + cnt["pb"] % 4
                cnt["pb"] += 1

                def mm(e, ws=ws, jl=jl, pb=pb):
                    ins = None
                    for kc in range(KC):
                        ins = e.matmul(psb[pb], wblk[ws][:, kc, jl * 128:(jl + 1) * 128], hbuf[:, kc, :],
                                       start=(kc == 0), stop=(kc == KC - 1))
                    return ins
                P.op("pe", mm, reads=[B_w[ws], B_h], writes=[PS[pb]])
                return pb

            load_x(0)
            for t in range(NT):
                s = t % 2
                is_prompt = t >= NST
                if t + 1 < NT:
                    load_x(t + 1)
                norm_modulate(l, 1, t, xt2[s], [B_x2[s]], hbuf, B_h, sqtmp, B_sq, rstd, B_r, tmp, B_tmp, 0)
                ws = load_w(0)
                ss = cnt["st"] % 2
                cnt["st"] += 1
                for jl in range(4):
                    pb = proj_fm(ws, jl)
                    P.op("act", lambda e, ss=ss, jl=jl, pb=pb: e.copy(out=stg[ss][:, jl, :], in_=psb[pb]),
                         reads=[PS[pb]], writes=[B_stg[ss]])
                dst = Usc[:, t * TT:(t + 1) * TT].rearrange("(c p) t -> p c t", p=128)
                P.dma("pool", lambda e, ss=ss, dst=dst: e.dma_start(out=dst, in_=stg[ss]), d_stg[ss],
                      reads=[B_stg[ss]], writes=[B_U[t]])
                pend = []

                def flush(keep):
                    while len(pend) > keep:
                        pend.pop(0)()
                for which in range(0 if "Aqk" not in DBG_SKIP else 2, 2):
                    for hb in range(2):
                        ws = load_w(1 + which * 2 + hb)
                        ss = cnt["st"] % 2
                        cnt["st"] += 1
                        for jl in range(4):
                            h = hb * 4 + jl
                            pb = proj_fm(ws, jl)
                            hs = cnt["hn"] % 2
                            cnt["hn"] += 1
                            pn = 5 + cnt["pn"] % 2
                            cnt["pn"] += 1
                            P.op("act", lambda e, hs=hs, pb=pb: e.activation(out=hsq[hs], in_=psb[pb], func=AF.Square),
                                 reads=[PS[pb]], writes=[B_hsq[hs]])

                            def s2(which=which, hb=hb, jl=jl, h=h, pb=pb, hs=hs, pn=pn, ss=ss):
                                P.op("pe", lambda e: e.matmul(psb[pn], onesb, hsq[hs], start=True, stop=True),
                                     reads=[B_hsq[hs], B_const], writes=[PS[pn]])
                                P.op("act", lambda e: e.activation(out=t1[hs], in_=psb[pn], func=AF.Sqrt,
                                                                   bias=epsc[:, 0:1], scale=1.0 / 128),
                                     reads=[PS[pn], B_const], writes=[B_t1[hs]])
                                P.op("dve", lambda e: e.reciprocal(out=r1[hs], in_=t1[hs]),
                                     reads=[B_t1[hs]], writes=[B_r1[hs]])
                                gcol = which * 2 + e_
                                if which == 1 and is_prompt and "Akt" not in DBG_SKIP:
                                    P.op("dve", lambda e: e.scalar_tensor_tensor(
                                        out=kf[hs], in0=psb[pb], scalar=QKG[:, gcol:gcol + 1], in1=r1[hs],
                                        op0=ALU.mult, op1=ALU.mult), reads=[PS[pb], B_r1[hs], B_ada], writes=[B_kf[hs]])
                                    P.op("act", lambda e: e.copy(out=stg[ss][:, jl, :], in_=kf[hs]),
                                         reads=[B_kf[hs]], writes=[B_stg[ss]])
                                    ks = cnt["kt"] % 2
                                    cnt["kt"] += 1

                                    def tr(e):
                                        ins = None
                                        for tb in range(4):
                                            ins = e.transpose(out=psb[7][:, tb * 128:(tb + 1) * 128],
                                                              in_=kf[hs][:, tb * 128:(tb + 1) * 128], identity=ident)
                                        return ins
                                    P.op("pe", tr, reads=[B_kf[hs], B_const], writes=[PS[7]])
                                    P.op("dve", lambda e: e.tensor_copy(
                                        out=ktok[ks], in_=psb[7].rearrange("p (a b) -> p a b", a=4)),
                                        reads=[PS[7]], writes=[B_ktok[ks]])
                                    s0 = (t - NST) * 2
                                    for sq2 in range(2):
                                        dstk = nck[s0 + sq2, e_, :, h, :].rearrange("(tb p) d -> p tb d", p=128)
                                        P.dma("pool", lambda e, dstk=dstk, sq2=sq2: e.dma_start(
                                            out=dstk, in_=ktok[ks][:, 2 * sq2:2 * sq2 + 2, :]), d_ktok[ks],
                                            reads=[B_ktok[ks]])
                                else:
                                    P.op("dve", lambda e: e.scalar_tensor_tensor(
                                        out=stg[ss][:, jl, :], in0=psb[pb], scalar=QKG[:, gcol:gcol + 1], in1=r1[hs],
                                        op0=ALU.mult, op1=ALU.mult), reads=[PS[pb], B_r1[hs], B_ada], writes=[B_stg[ss]])
                                if jl == 3:
                                    dsc = Qsc if which == 0 else Ksc
                                    dst = dsc[hb * 512:(hb + 1) * 512, t * TT:(t + 1) * TT].rearrange(
                                        "(c p) t -> p c t", p=128)
                                    P.dma("pool", lambda e: e.dma_start(out=dst, in_=stg[ss]), d_stg[ss],
                                          reads=[B_stg[ss]], writes=[(B_Q if which == 0 else B_K)[t]])
                            pend.append(s2)
                            flush(1)
                flush(0)
                for ch in range(2 if "Av" not in DBG_SKIP else 0):
                    ws = load_w(5 + ch)
                    ss = cnt["st"] % 2
                    cnt["st"] += 1
                    for tb in range(4):
                        pb = 1 + cnt["pb"] % 4
                        cnt["pb"] += 1

                        def mmv(e, ws=ws, tb=tb, pb=pb):
                            ins = None
                            for kc in range(KC):
                                ins = e.matmul(psb[pb], hbuf[:, kc, tb * 128:(tb + 1) * 128], wblk[ws][:, kc, :],
                                               start=(kc == 0), stop=(kc == KC - 1))
                            return ins
                        P.op("pe", mmv, reads=[B_w[ws], B_h], writes=[PS[pb]])
                        if is_prompt and "Avf" not in DBG_SKIP:
                            vs = cnt["vf"] % 2
                            cnt["vf"] += 1
                            P.op("act", lambda e, vs=vs, pb=pb: e.copy(out=vf[vs], in_=psb[pb]),
                                 reads=[PS[pb]], writes=[B_vf[vs]])
                            P.op("dve", lambda e, vs=vs, ss=ss, tb=tb: e.tensor_copy(out=stg[ss][:, tb, :], in_=vf[vs]),
                                 reads=[B_vf[vs]], writes=[B_stg[ss]])
                            sq_ = (t - NST) * 2 + tb // 2
                            dstv = ncv[sq_, e_, (tb % 2) * 128:(tb % 2) * 128 + 128, ch * 4:(ch + 1) * 4, :].rearrange(
                                "p h d -> p (h d)")
                            P.dma("pool", lambda e, vs=vs, dstv=dstv: e.dma_start(out=dstv, in_=vf[vs]), d_vf[vs],
                                  reads=[B_vf[vs]])
                        else:
                            P.op("act", lambda e, ss=ss, tb=tb, pb=pb: e.copy(out=stg[ss][:, tb, :], in_=psb[pb]),
                                 reads=[PS[pb]], writes=[B_stg[ss]])
                    dst = Vsc[t * TT:(t + 1) * TT, ch * 512:(ch + 1) * 512].rearrange("(tb p) c -> p tb c", p=128)
                    P.dma("pool", lambda e, ss=ss, dst=dst: e.dma_start(out=dst, in_=stg[ss]), d_stg[ss],
                          reads=[B_stg[ss]], writes=[B_V[t]])
            P.barrier()


        POWS = [0, 1, 2, 3, 4, 5, 6, 7, 8, 16, 32, 64, 128, 256, 512, 1024, 2048]
        NPW = len(POWS)
        TWO_PI = 6.283185307179586
        PI_ = 3.141592653589793
        B_UF = {(g, fc): Buf(f"uf{g}{fc}") for g in range(2) for fc in range(4)}

        def phase_S5(l):
            e_ = l // 2
            ph = Bump(PERS_END)
            WA = ph.get([128, 2, 2, 4, 8, 128], BF16)
            WC = ph.get([128, 2, 2, 8, 4, 160], BF16)
            WK = ph.get([128, 2, 8, 4, 128], BF16)
            PR = ph.get([128, 2, NPW, 16], F32)
            PIm = ph.get([128, 2, NPW, 16], F32)
            NPI = ph.get([128, 2, NPW, 16], F32)
            H0 = ph.get([128, 2, 2, 16], F32)
            ZER = ph.get([128, 8], F32)
            HPI = ph.get([128, 1], F32)
            W_END = ph.off
            B_W = Buf("s5w")
            B_pow = Buf("s5pow")
            SPt = ph.get([128, 2, S5P_COLS], F32)
            B_SP = Buf("s5sp")
            d_sp = P.dsem(f"s5sp_{l}")
            for d in range(2):
                P.dma("sp", lambda e, d=d: e.dma_start(out=SPt[:, d, :], in_=s5p[e_, d]), d_sp, writes=[B_SP])
                P.dma("sp", lambda e, d=d: e.dma_start(out=H0[:, d, :, :], in_=s5h0[e_, d].rearrange("r p g -> p r g")),
                      d_sp, writes=[B_SP])
            P.op("pool", lambda e: e.memset(ZER, 0.0), writes=[B_pow])
            P.op("pool", lambda e: e.memset(HPI, PI_ / 2), writes=[B_pow])
            P.op("pool", lambda e: e.memset(WC, 0.0), writes=[B_W])
            lamr, lami, ldt = SPt[:, :, 0:16], SPt[:, :, 16:32], SPt[:, :, 32:48]
            sm = [ph.get([128, 2, 16], F32) for _ in range(12)]
            dt_, ar_, ai_, t1_, t2_, den_, nr_, fre, fim, a1_, a2_, _x = sm
            big_ = [ph.get([128, 2, NPW, 16], F32) for _ in range(5)]
            ANG, MGL, TF, Rr, M1 = big_
            TI = ph.get([128, 2, NPW, 16], F32).bitcast(mybir.dt.int32)

            def f2(a):
                return a.rearrange("p d n g -> p (d n g)")
            Bp = B_pow
            P.op("act", lambda e: e.activation(out=dt_, in_=ldt, func=AF.Exp), reads=[B_SP], writes=[Bp])
            P.op("dve", lambda e: e.tensor_tensor(out=ar_, in0=lamr, in1=dt_, op=ALU.mult), reads=[B_SP, Bp], writes=[Bp])
            P.op("dve", lambda e: e.tensor_tensor(out=ai_, in0=lami, in1=dt_, op=ALU.mult), reads=[B_SP, Bp], writes=[Bp])
            for i, n in enumerate(POWS):
                P.op("dve", lambda e, i=i, n=n: e.tensor_scalar(out=ANG[:, :, i, :], in0=ai_, scalar1=float(n), scalar2=None,
                                                                op0=ALU.mult), reads=[Bp], writes=[Bp])
                P.op("dve", lambda e, i=i, n=n: e.tensor_scalar(out=MGL[:, :, i, :], in0=ar_, scalar1=float(n), scalar2=None,
                                                                op0=ALU.mult), reads=[Bp], writes=[Bp])
            P.op("dve", lambda e: e.tensor_scalar(out=f2(TF), in0=f2(ANG), scalar1=1.0 / TWO_PI, scalar2=None, op0=ALU.mult),
                 reads=[Bp], writes=[Bp])
            P.op("dve", lambda e: e.tensor_copy(out=f2(TI), in_=f2(TF)), reads=[Bp], writes=[Bp])
            P.op("dve", lambda e: e.tensor_copy(out=f2(TF), in_=f2(TI)), reads=[Bp], writes=[Bp])
            P.op("dve", lambda e: e.scalar_tensor_tensor(out=f2(Rr), in0=f2(TF), scalar=-TWO_PI, in1=f2(ANG), op0=ALU.mult,
                                                         op1=ALU.add), reads=[Bp], writes=[Bp])
            P.op("dve", lambda e: e.tensor_scalar(out=f2(M1), in0=f2(Rr), scalar1=PI_, scalar2=-TWO_PI, op0=ALU.is_gt,
                                                  op1=ALU.mult), reads=[Bp], writes=[Bp])
            P.op("dve", lambda e: e.tensor_tensor(out=f2(Rr), in0=f2(Rr), in1=f2(M1), op=ALU.add), reads=[Bp], writes=[Bp])
            P.op("dve", lambda e: e.tensor_scalar(out=f2(M1), in0=f2(Rr), scalar1=-PI_, scalar2=TWO_PI, op0=ALU.is_lt,
                                                  op1=ALU.mult), reads=[Bp], writes=[Bp])
            P.op("dve", lambda e: e.tensor_tensor(out=f2(Rr), in0=f2(Rr), in1=f2(M1), op=ALU.add), reads=[Bp], writes=[Bp])
            P.op("act", lambda e: e.activation(out=f2(TF), in_=f2(Rr), func=AF.Sin), reads=[Bp], writes=[Bp])
            P.op("dve", lambda e: e.tensor_scalar(out=f2(M1), in0=f2(Rr), scalar1=-1.0, scalar2=None, op0=ALU.mult),
                 reads=[Bp], writes=[Bp])
            P.op("dve", lambda e: e.tensor_tensor(out=f2(M1), in0=f2(M1), in1=f2(Rr), op=ALU.max), reads=[Bp], writes=[Bp])
            P.op("act", lambda e: e.activation(out=f2(ANG), in_=f2(M1), func=AF.Sin, bias=HPI[:, 0:1], scale=-1.0),
                 reads=[Bp], writes=[Bp])
            P.op("act", lambda e: e.activation(out=f2(MGL), in_=f2(MGL), func=AF.Exp), reads=[Bp], writes=[Bp])
            P.op("dve", lambda e: e.tensor_tensor(out=f2(PR), in0=f2(MGL), in1=f2(ANG), op=ALU.mult), reads=[Bp], writes=[Bp])
            P.op("dve", lambda e: e.tensor_tensor(out=f2(PIm), in0=f2(MGL), in1=f2(TF), op=ALU.mult), reads=[Bp], writes=[Bp])
            P.op("dve", lambda e: e.tensor_scalar(out=f2(NPI), in0=f2(PIm), scalar1=-1.0, scalar2=None, op0=ALU.mult),
                 reads=[Bp], writes=[Bp])
            pr1, pi1 = PR[:, :, 1, :], PIm[:, :, 1, :]

            def tt(out, a, b, op):
                P.op("dve", lambda e: e.tensor_tensor(out=out, in0=a, in1=b, op=op), reads=[Bp, B_SP], writes=[Bp])
            tt(t1_, lamr, lamr, ALU.mult)
            tt(t2_, lami, lami, ALU.mult)
            tt(den_, t1_, t2_, ALU.add)
            P.op("dve", lambda e: e.reciprocal(out=den_, in_=den_), reads=[Bp], writes=[Bp])
            P.op("dve", lambda e: e.tensor_scalar(out=nr_, in0=pr1, scalar1=-1.0, scalar2=None, op0=ALU.add),
                 reads=[Bp], writes=[Bp])
            tt(a1_, nr_, lamr, ALU.mult)
            tt(a2_, pi1, lami, ALU.mult)
            tt(fre, a1_, a2_, ALU.add)
            tt(fre, fre, den_, ALU.mult)
            tt(a1_, pi1, lamr, ALU.mult)
            tt(a2_, nr_, lami, ALU.mult)
            tt(fim, a1_, a2_, ALU.subtract)
            tt(fim, fim, den_, ALU.mult)
            ER = ph.get([128, 2, 8, 16], F32)
            EI = ph.get([128, 2, 8, 16], F32)
            e1 = ph.get([128, 2, 8, 16], F32)
            e2 = ph.get([128, 2, 8, 16], F32)
            freb = fre.unsqueeze(2).broadcast_to([128, 2, 8, 16])
            fimb = fim.unsqueeze(2).broadcast_to([128, 2, 8, 16])
            tt(e1, PR[:, :, 0:8, :], freb, ALU.mult)
            tt(e2, PIm[:, :, 0:8, :], fimb, ALU.mult)
            tt(ER, e1, e2, ALU.subtract)
            tt(e1, PR[:, :, 0:8, :], fimb, ALU.mult)
            tt(e2, PIm[:, :, 0:8, :], freb, ALU.mult)
            tt(EI, e1, e2, ALU.add)
            NAT = [[ph.get([128, 16, 32], F32) for _ in range(2)] for _ in range(2)]
            B_NAT = [Buf("nat0"), Buf("nat1")]
            q1 = [ph.get([128, 16, 32], F32) for _ in range(2)]
            BB = [[ph.get([128, 16, 32], F32) for _ in range(2)] for _ in range(2)]
            B_BB = Buf("bb")
            B_q = Buf("q1")

            def bc(a):
                return a.unsqueeze(2).broadcast_to([128, 16, 32])
            it = 0
            for d in range(2):
                Br = SPt[:, d, 48:560].rearrange("p (g c) -> p g c", c=32)
                Bi = SPt[:, d, 560:1072].rearrange("p (g c) -> p g c", c=32)
                for n in range(8):
                    st_ = it % 2
                    it += 1
                    er, ei = bc(ER[:, d, n, :]), bc(EI[:, d, n, :])
                    Nr, Ni = NAT[st_]
                    rd = [Bp, B_SP, B_q]

                    def o(eng, out, a, b, op, rds, wrs):
                        P.op(eng, lambda e: e.tensor_tensor(out=out, in0=a, in1=b, op=op), reads=rds, writes=wrs)
                    o("dve", q1[0], Br, er, ALU.mult, rd, [B_q])
                    o("dve", q1[1], Bi, ei, ALU.mult, rd, [B_q])
                    o("dve", Nr, q1[0], q1[1], ALU.subtract, [B_q], [B_NAT[st_]])
                    o("dve", q1[0], Bi, er, ALU.mult, rd, [B_q])
                    o("dve", q1[1], Br, ei, ALU.mult, rd, [B_q])
                    o("dve", Ni, q1[0], q1[1], ALU.add, [B_q], [B_NAT[st_]])
                    if n == 0:
                        for r_ in range(2):
                            P.op("pool", lambda e, d=d, r_=r_, st_=st_: e.tensor_copy(out=BB[d][r_], in_=NAT[st_][r_]),
                                 reads=[B_NAT[st_]], writes=[B_BB])
                    s_idx = (7 - n) if d == 0 else n
                    for r_ in range(2):
                        pb = (it * 2 + r_) % 4

                        def tr(e, st_=st_, r_=r_, pb=pb):
                            ins = None
                            for fc in range(4):
                                ins = e.transpose(out=psb[pb][:, fc * 128:(fc + 1) * 128],
                                                  in_=NAT[st_][r_][:, 4 * fc:4 * fc + 4, :].rearrange("p a b -> p (a b)"),
                                                  identity=ident)
                            return ins
                        P.op("pe", tr, reads=[B_NAT[st_], B_const], writes=[PS[pb]])
                        P.op("act", lambda e, d=d, r_=r_, s_idx=s_idx, pb=pb: e.copy(
                            out=WA[:, d, r_, :, s_idx, :], in_=psb[pb].rearrange("p (a b) -> p a b", a=4)),
                            reads=[PS[pb]], writes=[B_W])
            Bpad = ph.get([128, 2, 16, 128], F32)
            B_Bpad = Buf("bpad")
            Cc = [[ph.get([128, 16, 32], F32) for _ in range(2)] for _ in range(2)]
            B_Cc = [Buf("cc0"), Buf("cc1")]
            _ccp = [ph.get([128, 16, 128], F32) for _ in range(2)]
            CcP = [_ccp, _ccp]
            _bccp = Buf("ccp")
            B_CcP = [_bccp, _bccp]
            P.op("pool", lambda e: e.memset(Bpad, 0.0), writes=[B_Bpad])
            for r_ in range(2):
                P.op("pool", lambda e, r_=r_: e.memset(CcP[0][r_], 0.0), writes=[B_CcP[0]])
            P.op("pool", lambda e: e.memset(WK, 0.0), writes=[B_W])
            it = 0
            for d in range(2):
                for r_ in range(2):
                    for g4 in range(4):
                        P.op("pool", lambda e, d=d, r_=r_, g4=g4: e.tensor_copy(
                            out=Bpad[:, r_, :, :].rearrange("p (fc g) c -> p fc g c", g=4)[:, :, g4, 32 * g4:32 * g4 + 32],
                            in_=BB[d][r_].rearrange("p (fc g) c -> p fc g c", g=4)[:, :, g4, :]),
                            reads=[B_BB], writes=[B_Bpad])
                CTr = SPt[:, d, 1072:1584].rearrange("p (g c) -> p g c", c=32)
                CTi = SPt[:, d, 1584:2096].rearrange("p (g c) -> p g c", c=32)
                for n in range(9):
                    st_ = it % 2
                    it += 1
                    prb, pib, npb = bc(PR[:, d, n, :]), bc(PIm[:, d, n, :]), bc(NPI[:, d, n, :])
                    CR, CN = Cc[st_]
                    rd = [Bp, B_SP, B_q]
                    o("dve", q1[0], CTr, prb, ALU.mult, rd, [B_q])
                    o("dve", q1[1], CTi, pib, ALU.mult, rd, [B_q])
                    o("dve", CR, q1[0], q1[1], ALU.subtract, [B_q], [B_Cc[st_]])
                    o("dve", q1[0], CTr, npb, ALU.mult, rd, [B_q])
                    o("dve", q1[1], CTi, prb, ALU.mult, rd, [B_q])
                    o("dve", CN, q1[0], q1[1], ALU.subtract, [B_q], [B_Cc[st_]])
                    if n >= 1:
                        for r_ in range(2):
                            src4 = Cc[st_][r_].rearrange("p (fc g) c -> p fc g c", g=4)
                            P.op("act", lambda e, d=d, r_=r_, n=n, src4=src4: e.copy(
                                out=WC[:, d, r_, n - 1, :, 0:96].rearrange("p fc (g c) -> p fc g c", c=32),
                                in_=src4[:, :, 0:3, :]), reads=[B_Cc[st_]], writes=[B_W])
                            P.op("act", lambda e, d=d, r_=r_, n=n, src4=src4: e.copy(
                                out=WC[:, d, r_, n - 1, :, 128:160], in_=src4[:, :, 3, :]), reads=[B_Cc[st_]], writes=[B_W])
                    if n <= 7:
                        for r_ in range(2):
                            for g4 in range(4):
                                P.op("pool", lambda e, st_=st_, r_=r_, g4=g4: e.tensor_copy(
                                    out=CcP[st_][r_].rearrange("p (fc g) c -> p fc g c", g=4)[:, :, g4, 32 * g4:32 * g4 + 32],
                                    in_=Cc[st_][r_].rearrange("p (fc g) c -> p fc g c", g=4)[:, :, g4, :]),
                                    reads=[B_Cc[st_]], writes=[B_CcP[st_]])
                        pb = 4 + it % 4

                        def kmm(e, st_=st_, pb=pb):
                            ins = None
                            for fc in range(4):
                                k = 0
                                for g4 in range(4):
                                    for r_ in range(2):
                                        ins = e.matmul(psb[pb][:, fc * 128:(fc + 1) * 128], Bpad[:, r_, 4 * fc + g4, :],
                                                       CcP[st_][r_][:, 4 * fc + g4, :], start=(k == 0), stop=(k == 7))
                                        k += 1
                            return ins
                        P.op("pe", kmm, reads=[B_Bpad, B_CcP[st_]], writes=[PS[pb]])
                        P.op("act", lambda e, d=d, n=n, pb=pb: e.copy(
                            out=WK[:, d, n, :, :], in_=psb[pb].rearrange("p (a b) -> p a b", a=4)),
                            reads=[PS[pb]], writes=[B_W])
            P.barrier()

            rt = Bump(W_END)
            ubuf = rt.get([128, 4096], BF16)
            B_u = Buf("ubuf")
            d_u = P.dsem(f"s5u_{l}")
            um = [rt.get([128, 4096], BF16) for _ in range(4)]
            B_um = [Buf(f"um{i}") for i in range(4)]
            d_um = [P.dsem(f"s5um{i}_{l}") for i in range(4)]
            XS_ = [[rt.get([128, 2, 512], F32) for _ in range(2)] for _ in range(2)]
            B_XS = [[[Buf(f"xs{s_}{i}m"), Buf(f"xs{s_}{i}e")] for i in range(2)] for s_ in range(2)]
            XP = rt.get([128, 16, 512], BF16)
            B_XP = [Buf(f"xp{i}") for i in range(16)]
            yj = [rt.get([128, 512], F32) for _ in range(2)]
            B_yj = [Buf("yj0"), Buf("yj1")]
            gt_ = [rt.get([128, 512], F32) for _ in range(3)]
            B_gt = [Buf("gt0"), Buf("gt1"), Buf("gt2")]
            gst = rt.get([128, 4096], BF16)
            B_gst = Buf("gst")
            d_gst = P.dsem(f"s5g_{l}")
            FS = rt.get([128, 2, 2, 4, 16], F32)
            B_FS = Buf("fs")
            for i in range(4):
                P.op("pool", lambda e, i=i: e.memset(um[i], 0.0), writes=[B_um[i]])
            cnt = {"px": 0, "py": 0, "ch": 0, "yj": 0}

            ptmp = rt.get([128, 512], F32)
            B_ptmp = Buf("ptmp")

            def stt_any(eng, out, in0, scal, in1, rds, wrs, tmpv):
                if eng == "dve":
                    P.op("dve", lambda e: e.scalar_tensor_tensor(out=out, in0=in0, scalar=scal, in1=in1,
                                                                 op0=ALU.mult, op1=ALU.add), reads=rds, writes=wrs)
                else:
                    P.op("pool", lambda e: e.tensor_scalar(out=tmpv, in0=in0, scalar1=scal, scalar2=None, op0=ALU.mult),
                         reads=rds, writes=[B_ptmp])
                    P.op("pool", lambda e: e.tensor_tensor(out=out, in0=tmpv, in1=in1, op=ALU.add),
                         reads=rds + [B_ptmp], writes=wrs)

            def run_group(gi, nseq, Cs, tok0, npass, is_sample):
                NC_ = nseq * Cs
                NTK = NC_ * 8

                def v3(ap2):
                    return ap2[:, 0:NC_].rearrange("p (q c) -> p q c", c=Cs)
                for fc in range(4):
                    P.dma("sp", lambda e, fc=fc: e.dma_start(out=ubuf[:, 0:NTK], in_=Usc[fc * 128:(fc + 1) * 128,
                                                                                         tok0:tok0 + NTK]),
                          d_u, reads=[B_UF[(gi, fc)]] + (B_U if gi == 0 and fc == 0 else []), writes=[B_u])
                    u4 = ubuf[:, 0:NTK].rearrange("p (c s) -> p c s", s=8)
                    for g4 in range(4):
                        gp = 4 * fc + g4
                        P.dma("sp", lambda e, fc=fc, g4=g4: e.dma_start(
                            out=um[g4][32 * g4:32 * g4 + 32, 0:NTK],
                            in_=Usc[fc * 128 + 32 * g4:fc * 128 + 32 * g4 + 32, tok0:tok0 + NTK]), d_um[g4],
                            reads=[B_UF[(gi, fc)]], writes=[B_um[g4]])
                        um4 = um[g4][:, 0:NTK].rearrange("p (c s) -> p c s", s=8)
                        for d in range(2):
                            eng = "dve" if d == 0 else "pool"
                            set_ = cnt["ch"] % 2
                            cnt["ch"] += 1
                            X = XS_[set_]
                            BX = B_XS[set_]
                            for r_ in range(2):
                                pb = cnt["px"] % 4
                                cnt["px"] += 1

                                def amm(e, d=d, r_=r_, fc=fc, pb=pb, um4=um4):
                                    ins = None
                                    for s_ in range(8):
                                        ins = e.matmul(psb[pb][:, 0:NC_], WA[:, d, r_, fc, s_, :], um4[:, :, s_],
                                                       start=(s_ == 0), stop=(s_ == 7))
                                    return ins
                                P.op("pe", amm, reads=[B_W, B_um[g4]], writes=[PS[pb]])
                                P.op("act", lambda e, r_=r_, pb=pb, X=X: e.copy(out=X[0][:, r_, 0:NC_], in_=psb[pb][:, 0:NC_]),
                                     reads=[PS[pb]], writes=BX[0])
                            pr8, pi8, npi8 = (PR[:, d, 8, gp:gp + 1], PIm[:, d, 8, gp:gp + 1], NPI[:, d, 8, gp:gp + 1])
                            if is_sample:
                                col = 0 if d == 0 else NC_ - 1
                                xc = X[0][:, :, col]
                                P.op("dve", lambda e, xc=xc, d=d, gp=gp, pr8=pr8: e.scalar_tensor_tensor(
                                    out=xc, in0=H0[:, d, :, gp], scalar=pr8, in1=xc, op0=ALU.mult, op1=ALU.add),
                                    reads=BX[0] + [B_pow, B_SP], writes=BX[0])
                                P.op("dve", lambda e, X=X, col=col, d=d, gp=gp, npi8=npi8: e.scalar_tensor_tensor(
                                    out=X[0][:, 0, col:col + 1], in0=H0[:, d, 1, gp:gp + 1], scalar=npi8,
                                    in1=X[0][:, 0, col:col + 1], op0=ALU.mult, op1=ALU.add),
                                    reads=BX[0] + [B_pow, B_SP], writes=BX[0])
                                P.op("dve", lambda e, X=X, col=col, d=d, gp=gp, pi8=pi8: e.scalar_tensor_tensor(
                                    out=X[0][:, 1, col:col + 1], in0=H0[:, d, 0, gp:gp + 1], scalar=pi8,
                                    in1=X[0][:, 1, col:col + 1], op0=ALU.mult, op1=ALU.add),
                                    reads=BX[0] + [B_pow, B_SP], writes=BX[0])

                            def v4(t):
                                return t[:, :, 0:NC_].rearrange("p r (q c) -> p r q c", c=Cs)
                            cur = 0
                            for k in range(npass):
                                sh = 1 << k
                                pr_, pi_, npi_ = (PR[:, d, 8 + k, gp:gp + 1], PIm[:, d, 8 + k, gp:gp + 1],
                                                  NPI[:, d, 8 + k, gp:gp + 1])
                                src, dst = v4(X[cur]), v4(X[1 - cur])
                                Bs, Bd = BX[cur], BX[1 - cur]
                                if d == 0:
                                    a_, b_, c_ = slice(sh, Cs), slice(0, Cs - sh), slice(0, sh)
                                else:
                                    a_, b_, c_ = slice(0, Cs - sh), slice(sh, Cs), slice(Cs - sh, Cs)
                                if nseq == 1:
                                    P.op("dve", lambda e, src=src, dst=dst, a_=a_, b_=b_, pr_=pr_: e.scalar_tensor_tensor(
                                        out=dst[:, :, 0, a_], in0=src[:, :, 0, b_], scalar=pr_, in1=src[:, :, 0, a_],
                                        op0=ALU.mult, op1=ALU.add), reads=Bs + [B_pow], writes=[Bd[0]])
                                else:
                                    for r2 in range(2):
                                        P.op("dve", lambda e, src=src, dst=dst, a_=a_, b_=b_, pr_=pr_, r2=r2: e.scalar_tensor_tensor(
                                            out=dst[:, r2, :, a_], in0=src[:, r2, :, b_], scalar=pr_, in1=src[:, r2, :, a_],
                                            op0=ALU.mult, op1=ALU.add), reads=Bs + [B_pow], writes=[Bd[0]])
                                P.op("dve", lambda e, src=src, dst=dst, a_=a_, b_=b_, npi_=npi_: e.scalar_tensor_tensor(
                                    out=dst[:, 0, :, a_], in0=src[:, 1, :, b_], scalar=npi_, in1=dst[:, 0, :, a_],
                                    op0=ALU.mult, op1=ALU.add), reads=Bs + [Bd[0], B_pow], writes=[Bd[0]])
                                P.op("dve", lambda e, src=src, dst=dst, a_=a_, b_=b_, pi_=pi_: e.scalar_tensor_tensor(
                                    out=dst[:, 1, :, a_], in0=src[:, 0, :, b_], scalar=pi_, in1=dst[:, 1, :, a_],
                                    op0=ALU.mult, op1=ALU.add), reads=Bs + [Bd[0], B_pow], writes=[Bd[0]])
                                if nseq == 1:
                                    P.op("act", lambda e, src=src, dst=dst, c_=c_: e.copy(out=dst[:, :, 0, c_], in_=src[:, :, 0, c_]),
                                         reads=Bs, writes=[Bd[1]])
                                else:
                                    for r2 in range(2):
                                        P.op("act", lambda e, src=src, dst=dst, c_=c_, r2=r2: e.copy(
                                            out=dst[:, r2, :, c_], in_=src[:, r2, :, c_]), reads=Bs, writes=[Bd[1]])
                                cur = 1 - cur
                            for r_ in range(2):
                                Xf = v4(X[cur])[:, r_, :, :]
                                BXf = BX[cur]
                                slot = (g4 * 2 + d) * 2 + r_
                                xp3 = v3(XP[:, slot, :])
                                if not is_sample:
                                    ecol = Cs - 1 if d == 0 else 0
                                    P.op("act", lambda e, d=d, r_=r_, gp=gp, Xf=Xf, ecol=ecol: e.copy(
                                        out=FS[:, r_, d, :, gp], in_=Xf[:, :, ecol]), reads=BXf, writes=[B_FS])
                                if d == 0:
                                    P.op("act", lambda e, xp3=xp3, Xf=Xf: e.copy(out=xp3[:, :, 1:Cs], in_=Xf[:, :, 0:Cs - 1]),
                                         reads=BXf, writes=[B_XP[slot]])
                                    ec = 0
                                else:
                                    P.op("act", lambda e, xp3=xp3, Xf=Xf: e.copy(out=xp3[:, :, 0:Cs - 1], in_=Xf[:, :, 1:Cs]),
                                         reads=BXf, writes=[B_XP[slot]])
                                    ec = Cs - 1
                                if is_sample:
                                    hsrc = H0[:, d, r_, gp:gp + 1]
                                    P.op("act", lambda e, xp3=xp3, ec=ec, hsrc=hsrc: e.copy(out=xp3[:, 0, ec:ec + 1], in_=hsrc),
                                         reads=[B_SP], writes=[B_XP[slot]])
                                else:
                                    P.op("act", lambda e, xp3=xp3, ec=ec: e.copy(out=xp3[:, :, ec], in_=ZER[:, 0:nseq]),
                                         reads=[B_pow], writes=[B_XP[slot]])
                    g3 = gst[:, 0:NTK].rearrange("p (c s) -> p c s", s=8)
                    dcol = VT[:, VOFF["s5_d"] + e_ * 4 + fc: VOFF["s5_d"] + e_ * 4 + fc + 1]
                    for j in range(8):
                        pb = 4 + cnt["py"] % 4
                        cnt["py"] += 1

                        def ymm(e, j=j, fc=fc, pb=pb, u4=u4):
                            first = True
                            for s_ in range(0, j + 1):
                                e.matmul(psb[pb][:, 0:NC_], WK[:, 0, j - s_, fc, :], u4[:, :, s_], start=first, stop=False)
                                first = False
                            for s_ in range(j, 8):
                                e.matmul(psb[pb][:, 0:NC_], WK[:, 1, s_ - j, fc, :], u4[:, :, s_], start=False, stop=False)
                            ins = None
                            for g4 in range(4):
                                for d in range(2):
                                    nidx = j if d == 0 else 7 - j
                                    for r_ in range(2):
                                        slot = (g4 * 2 + d) * 2 + r_
                                        last = (g4 == 3 and d == 1 and r_ == 1)
                                        if g4 < 3:
                                            ins = e.matmul(psb[pb][32 * g4:32 * g4 + 32, 0:NC_],
                                                           WC[:, d, r_, nidx, fc, 32 * g4:32 * g4 + 32], XP[:, slot, 0:NC_],
                                                           start=False, stop=last)
                                        else:
                                            ins = e.matmul(psb[pb][64:128, 0:NC_], WC[:, d, r_, nidx, fc, 96:160],
                                                           XP[:, slot, 0:NC_], start=False, stop=last)
                            return ins
                        P.op("pe", ymm, reads=[B_W, B_u] + B_XP, writes=[PS[pb]])
                        ys = cnt["yj"] % 2
                        cnt["yj"] += 1
                        P.op("dve", lambda e, j=j, pb=pb, ys=ys, u4=u4, dcol=dcol: e.scalar_tensor_tensor(
                            out=yj[ys][:, 0:NC_], in0=u4[:, :, j], scalar=dcol, in1=psb[pb][:, 0:NC_],
                            op0=ALU.mult, op1=ALU.add), reads=[PS[pb], B_u, B_VT], writes=[B_yj[ys]])
                        P.op("act", lambda e, ys=ys: e.activation(out=gt_[0][:, 0:NC_], in_=yj[ys][:, 0:NC_], func=AF.Square),
                             reads=[B_yj[ys]], writes=[B_gt[0]])
                        P.op("dve", lambda e: e.tensor_scalar(out=gt_[1][:, 0:NC_], in0=gt_[0][:, 0:NC_], scalar1=0.044715,
                                                              scalar2=1.0, op0=ALU.mult, op1=ALU.add),
                             reads=[B_gt[0]], writes=[B_gt[1]])
                        P.op("dve", lambda e, ys=ys: e.tensor_tensor(out=gt_[1][:, 0:NC_], in0=gt_[1][:, 0:NC_],
                                                                     in1=yj[ys][:, 0:NC_], op=ALU.mult),
                             reads=[B_gt[1], B_yj[ys]], writes=[B_gt[1]])
                        P.op("act", lambda e: e.activation(out=gt_[2][:, 0:NC_], in_=gt_[1][:, 0:NC_], func=AF.Sigmoid,
                                                           scale=1.5957691216057308), reads=[B_gt[1]], writes=[B_gt[2]])
                        P.op("dve", lambda e, j=j, ys=ys, g3=g3: e.tensor_tensor(
                            out=g3[:, :, j], in0=yj[ys][:, 0:NC_], in1=gt_[2][:, 0:NC_], op=ALU.mult),
                            reads=[B_yj[ys], B_gt[2]], writes=[B_gst])
                    P.dma("pool", lambda e, fc=fc: e.dma_start(out=Usc[fc * 128:(fc + 1) * 128, tok0:tok0 + NTK],
                                                               in_=gst[:, 0:NTK]), d_gst, reads=[B_gst],
                          writes=[B_UF[(gi, fc)]])

            run_group(0, 1, 512, 0, 9, True)
            run_group(1, 4, 32, 4096, 5, False)
            fso = [rt.get([128, 128], F32) for _ in range(2)]
            B_fso = [Buf("fso0"), Buf("fso1")]
            d_fso = [P.dsem(f"fso0_{l}"), P.dsem(f"fso1_{l}")]
            for r_ in range(2):
                P.op("pe", lambda e, r_=r_: e.transpose(out=psb[r_][:, 0:128],
                                                        in_=FS[:, r_, :, :, :].rearrange("p d q g -> p (d q g)"),
                                                        identity=ident), reads=[B_FS, B_const], writes=[PS[r_]])
                P.op("act", lambda e, r_=r_: e.copy(out=fso[r_], in_=psb[r_][:, 0:128]), reads=[PS[r_]], writes=[B_fso[r_]])
                dst_t = nsr if r_ == 0 else nsi
                for d in range(2):
                    for q_ in range(4):
                        dstf = dst_t[q_, e_, d, :, :].rearrange("(gp g2) p -> gp (g2 p)", g2=2)
                        P.dma("sp", lambda e, r_=r_, d=d, q_=q_, dstf=dstf: e.dma_start(
                            out=dstf, in_=fso[r_][64 * d + 16 * q_:64 * d + 16 * q_ + 16, :]), d_fso[r_],
                            reads=[B_fso[r_]])
            P.barrier()

            gl = Bump(W_END)
            GW = gl.get([128, 4, 512], BF16)
            B_GW = Buf("gw")
            d_gw = P.dsem(f"gw_{l}")
            P.dma("sp", lambda e: e.dma_start(out=GW, in_=glub[e_].rearrange("(kc p) n -> p kc n", p=128)), d_gw,
                  reads=[WB[l]], writes=[B_GW])
            gtile = [gl.get([128, 4, TT], BF16) for _ in range(2)]
            B_gtile = [Buf("gti0"), Buf("gti1")]
            d_gtile = [P.dsem(f"gti0_{l}"), P.dsem(f"gti1_{l}")]
            sgt = [gl.get([128, TT], F32) for _ in range(2)]
            B_sgt = [Buf("sg0"), Buf("sg1")]
            ostg = [gl.get([128, 4, TT], BF16) for _ in range(2)]
            B_ostg = [Buf("os0"), Buf("os1")]
            d_ostg = [P.dsem(f"os0_{l}"), P.dsem(f"os1_{l}")]
            for t in range(NT):
                s_ = t % 2
                P.dma("sp", lambda e, s_=s_, t=t: e.dma_start(
                    out=gtile[s_], in_=Usc[:, t * TT:(t + 1) * TT].rearrange("(c p) t -> p c t", p=128)), d_gtile[s_],
                    writes=[B_gtile[s_]])
                for n in range(4):
                    pb = (t * 4 + n) % 4

                    def gmm(e, s_=s_, n=n, pb=pb):
                        ins = None
                        for kc in range(4):
                            ins = e.matmul(psb[pb], GW[:, kc, n * 128:(n + 1) * 128], gtile[s_][:, kc, :],
                                           start=(kc == 0), stop=(kc == 3))
                        return ins
                    P.op("pe", gmm, reads=[B_GW, B_gtile[s_]], writes=[PS[pb]])
                    ss_ = n % 2
                    bcol = VT[:, VOFF["glu_b"] + e_ * 4 + n: VOFF["glu_b"] + e_ * 4 + n + 1]
                    P.op("act", lambda e, ss_=ss_, pb=pb, bcol=bcol: e.activation(out=sgt[ss_], in_=psb[pb], func=AF.Sigmoid,
                                                                                 bias=bcol, scale=1.0),
                         reads=[PS[pb], B_VT], writes=[B_sgt[ss_]])
                    P.op("dve", lambda e, s_=s_, ss_=ss_, n=n: e.tensor_tensor(out=ostg[s_][:, n, :], in0=gtile[s_][:, n, :],
                                                                               in1=sgt[ss_], op=ALU.mult),
                         reads=[B_gtile[s_], B_sgt[ss_]], writes=[B_ostg[s_]])
                P.dma("pool", lambda e, s_=s_, t=t: e.dma_start(
                    out=MIX[0:512, t * TT:(t + 1) * TT].rearrange("(c p) t -> p c t", p=128), in_=ostg[s_]), d_ostg[s_],
                    reads=[B_ostg[s_]], writes=[MIXB[t]])
            P.barrier()

        def phase_S5_zero(l):
            ph = Bump(PERS_END)
            zt = ph.get([128, NTOK], BF16)
            bz = Buf("zt")
            dz = P.dsem(f"zt_{l}")
            P.op("pool", lambda e: e.memset(zt, 0.0), writes=[bz])
            for c in range(4):
                P.dma("sp", lambda e, c=c: e.dma_start(out=MIX[c * 128:(c + 1) * 128, :], in_=zt), dz,
                      reads=[bz], writes=MIXB)
            P.barrier()

        NEG = -30000.0

        def phase_nab(e_):
            ph = Bump(PERS_END)
            E2 = ph.get([128, 8, 15, 64], F32)
            B_E2 = Buf("E2")
            Tt = [ph.get([128, 8, 10, 64], F32) for _ in range(2)]
            B_Tt = [Buf("Tt0"), Buf("Tt1")]
            d_Tt = [P.dsem(f"Tt0_{e_}"), P.dsem(f"Tt1_{e_}")]
            d_e2 = P.dsem(f"e2_{e_}")
            P.op("pool", lambda e: e.memset(E2, NEG), writes=[B_E2])
            for p in range(128):
                col = p % 64
                c0 = min(max(col - 8, 0), 48)
                off = c0 - col + 15
                P.dma("sp", lambda e, p=p, c0=c0, off=off: e.dma_start(
                    out=E2[p:p + 1, :, :, c0:c0 + 16], in_=na_rpb[e_:e_ + 1, :, :, off:off + 16]),
                    d_e2, writes=[B_E2])
            for ti, key in enumerate(PAIR_TYPES):
                s = ti % 2
                P.op("pool", lambda e, s=s: e.memset(Tt[s], NEG), writes=[B_Tt[s]])
                for rr in range(2):
                    wr_lo, i_lo = key[rr]
                    P.op("dve", lambda e, s=s, rr=rr, wr_lo=wr_lo, i_lo=i_lo: e.tensor_copy(
                        out=Tt[s][rr * 64:(rr + 1) * 64, :, wr_lo:wr_lo + 8, :],
                        in_=E2[rr * 64:(rr + 1) * 64, :, i_lo:i_lo + 8, :]), reads=[B_E2], writes=[B_Tt[s]])
                P.dma("sp", lambda e, s=s, ti=ti: e.dma_start(
                    out=BIAS[e_, ti], in_=Tt[s].rearrange("p h w c -> p h (w c)")), d_Tt[s],
                    reads=[B_Tt[s]], writes=[B_BIAS])
            P.barrier()

        def phase_C(l):
            e_ = l // 2
            ph = Bump(PERS_END)
            qh = [ph.get([128, NTOK], BF16) for _ in range(2)]
            kh = [ph.get([128, NTOK], BF16) for _ in range(2)]
            vh = [ph.get([128, NTOK // 128, 128], BF16) for _ in range(2)]
            B_qh = [Buf("qh0"), Buf("qh1")]
            B_kh = [Buf("kh0"), Buf("kh1")]
            B_vh = [Buf("vh0"), Buf("vh1")]
            d_qh = [P.dsem(f"qh0_{l}"), P.dsem(f"qh1_{l}")]
            d_kh = [P.dsem(f"kh0_{l}"), P.dsem(f"kh1_{l}")]
            d_vh = [P.dsem(f"vh0_{l}"), P.dsem(f"vh1_{l}")]
            bh = [ph.get([128, NTYPES, 640], F32) for _ in range(2)]
            B_bh = [Buf("bh0"), Buf("bh1")]
            d_bh = [P.dsem(f"bh0_{l}"), P.dsem(f"bh1_{l}")]
            ck32 = [ph.get([128, 2, 128], F32) for _ in range(2)]
            cv32 = [ph.get([128, 2, 128], F32) for _ in range(2)]
            B_ck32 = [Buf("ck0"), Buf("ck1")]
            B_cv32 = [Buf("cv0"), Buf("cv1")]
            d_ck = [P.dsem(f"ck0_{l}"), P.dsem(f"ck1_{l}")]
            d_cv = [P.dsem(f"cv0_{l}"), P.dsem(f"cv1_{l}")]
            kcb = [ph.get([128, 256], BF16) for _ in range(2)]
            vcb = [ph.get([128, 2, 128], BF16) for _ in range(2)]
            B_kcb = [Buf("kcb0"), Buf("kcb1")]
            B_vcb = [Buf("vcb0"), Buf("vcb1")]
            at = [ph.get([128, NTOK], BF16) for _ in range(2)]
            B_at = [Buf("at0"), Buf("at1")]
            d_at = [P.dsem(f"at0_{l}"), P.dsem(f"at1_{l}")]
            NSET = 4
            bhb = [ph.get([128, NTYPES, 640], BF16) for _ in range(2)]
            B_bhb = [Buf("bhb0"), Buf("bhb1")]
            Pb = [ph.get([128, 896], BF16) for _ in range(NSET)]
            B_Pb = [Buf(f"Pb{i}") for i in range(NSET)]
            PT = [ph.get([128, 7, 128], BF16) for _ in range(NSET)]
            B_PT = [Buf(f"PT{i}") for i in range(NSET)]
            nmx = [ph.get([128, 1], F32) for _ in range(NSET)]
            B_mx = [Buf(f"mx{i}") for i in range(NSET)]
            junk = ph.get([128, 896], BF16)
            B_junk = Buf("junk")
            rsb = [ph.get([128, 128], F32) for _ in range(NSET)]
            B_rsb = [Buf(f"rsb{i}") for i in range(NSET)]
            psT = psb[6].bitcast(BF16)
            TB, OB = 6, 7

            def load_head(h):
                hp = h % 2
                P.dma("sp", lambda e: e.dma_start(out=qh[hp], in_=Qsc[h * 128:(h + 1) * 128, :]), d_qh[hp],
                      reads=B_Q, writes=[B_qh[hp]])
                P.dma("sp", lambda e: e.dma_start(out=kh[hp], in_=Ksc[h * 128:(h + 1) * 128, :]), d_kh[hp],
                      reads=B_K, writes=[B_kh[hp]])
                P.dma("sp", lambda e: e.dma_start(
                    out=vh[hp], in_=Vsc[:, h * 128:(h + 1) * 128].rearrange("(c p) d -> p c d", p=128)), d_vh[hp],
                    reads=B_V, writes=[B_vh[hp]])
                P.dma("sp", lambda e: e.dma_start(
                    out=bh[hp], in_=BIAS[e_, :, :, h, :].rearrange("t p k -> p t k")), d_bh[hp],
                    reads=[B_BIAS], writes=[B_bh[hp]])
                P.dma("sp", lambda e: e.dma_start(
                    out=ck32[hp], in_=cache_k[e_, :, h, :].rearrange("(c p) d -> p c d", p=128)), d_ck[hp],
                    writes=[B_ck32[hp]])
                P.dma("sp", lambda e: e.dma_start(
                    out=cv32[hp], in_=cache_v[e_, :, h, :].rearrange("(c p) d -> p c d", p=128)), d_cv[hp],
                    writes=[B_cv32[hp]])

            def stA(un, i):
                s3 = i % 3
                hp = un["hp"]
                bA, bB = 2 * s3, 2 * s3 + 1
                q0 = un["q0"]
                if un["kind"] == "na":
                    k0, ty = un["k0"], un["ty"]

                    def smm(e):
                        qq = qh[hp][:, q0:q0 + 128]
                        e.matmul(psb[bA], qq, kh[hp][:, k0:k0 + 512], start=True, stop=False)
                        e.matmul(psb[bA], identb, bhb[hp][:, ty, 0:512], start=False, stop=True)
                        e.matmul(psb[bB][:, 0:128], qq, kh[hp][:, k0 + 512:k0 + 640], start=True, stop=False)
                        e.matmul(psb[bB][:, 0:128], identb, bhb[hp][:, ty, 512:640], start=False, stop=True)
                        return e.matmul(psb[bB][:, 128:384], qq, kcb[hp], start=True, stop=True)
                    P.op("pe", smm, reads=[B_qh[hp], B_kh[hp], B_kcb[hp], B_bhb[hp], B_const], writes=[PS[bA], PS[bB]])
                else:
                    tok0 = un["tok0"]
                    P.op("pe", lambda e: e.matmul(psb[bA][:, 0:256], qh[hp][:, q0:q0 + 128], kh[hp][:, tok0:tok0 + 256],
                                                  start=True, stop=True), reads=[B_qh[hp], B_kh[hp]], writes=[PS[bA], PS[bB]])

            def stB(un, i):
                s3, q = i % 3, i % NSET
                NK = un["nkc"] * 128
                Sreg = psall[:, s3 * 1024:s3 * 1024 + NK]
                rd = [PS[2 * s3], PS[2 * s3 + 1]]
                P.op("dve", lambda e: e.tensor_scalar(out=junk[:, 0:NK], in0=Sreg, scalar1=-1.0, scalar2=None,
                                                      op0=ALU.mult, op1=ALU.min, accum_out=nmx[q][:, 0:1]),
                     reads=rd, writes=[B_mx[q], B_junk])
                P.op("act", lambda e: e.activation(out=Pb[q][:, 0:NK], in_=Sreg, func=AF.Exp, bias=nmx[q][:, 0:1], scale=1.0),
                     reads=rd + [B_mx[q]], writes=[B_Pb[q]])

            def stC1(un, i):
                q = i % NSET
                NKC = un["nkc"]

                def tr(e):
                    ins = None
                    for c in range(NKC):
                        ins = e.transpose(out=psT[:, c * 128:(c + 1) * 128], in_=Pb[q][:, c * 128:(c + 1) * 128],
                                          identity=identb)
                    return ins
                P.op("pe", tr, reads=[B_Pb[q], B_const], writes=[PS[TB]])
                P.op("act", lambda e: e.copy(out=PT[q][:, 0:NKC, :],
                                             in_=psT[:, 0:NKC * 128].rearrange("p (c q) -> p c q", q=128)),
                     reads=[PS[TB]], writes=[B_PT[q]])

            def stC2(un, i):
                q = i % NSET
                NKC = un["nkc"]
                hp = un["hp"]
                vch, vbufs, q0 = un["vch"], un["vbufs"], un["q0"]

                def pv(e):
                    ins = None
                    for c in range(NKC):
                        ins = e.matmul(psb[OB][:, 0:128], vch[c], PT[q][:, c, :], start=(c == 0), stop=(c == NKC - 1))
                    for c in range(NKC):
                        ins = e.matmul(psb[OB][:, 128:256], onesb, PT[q][:, c, :], start=(c == 0), stop=(c == NKC - 1))
                    return ins
                P.op("pe", pv, reads=[B_PT[q], B_const] + vbufs, writes=[PS[OB]])
                P.op("dve", lambda e: e.reciprocal(out=rsb[q], in_=psb[OB][:, 128:256]), reads=[PS[OB]], writes=[B_rsb[q]])
                P.op("dve", lambda e: e.tensor_tensor(out=at[hp][:, q0:q0 + 128], in0=psb[OB][:, 0:128], in1=rsb[q],
                                                      op=ALU.mult), reads=[PS[OB], B_rsb[q]], writes=[B_at[hp]])

            gi = [0]
            load_head(0)
            for h in range(8):
                hp = h % 2
                if h + 1 < 8:
                    load_head(h + 1)

                def trk(e, hp=hp):
                    ins = None
                    for c in range(2):
                        ins = e.transpose(out=psb[TB][:, c * 128:(c + 1) * 128], in_=ck32[hp][:, c, :], identity=ident)
                    return ins
                P.op("pe", trk, reads=[B_ck32[hp], B_const], writes=[PS[TB]])
                P.op("act", lambda e, hp=hp: e.copy(out=kcb[hp], in_=psb[TB][:, 0:256]), reads=[PS[TB]], writes=[B_kcb[hp]])
                P.op("act", lambda e, hp=hp: e.copy(out=vcb[hp], in_=cv32[hp]), reads=[B_cv32[hp]], writes=[B_vcb[hp]])
                P.op("dve", lambda e, hp=hp: e.tensor_copy(out=bhb[hp], in_=bh[hp]), reads=[B_bh[hp]], writes=[B_bhb[hp]])
                units = []
                for a in range(32):
                    ty, w0 = PAIR_MAP[a]
                    k0 = 64 * w0
                    vt0 = k0 // 128
                    units.append(dict(kind="na", hp=hp, q0=128 * a, k0=k0, ty=ty, nkc=7,
                                      vch=[vh[hp][:, vt0 + c, :] for c in range(5)] + [vcb[hp][:, c, :] for c in range(2)],
                                      vbufs=[B_vh[hp], B_vcb[hp]]))
                for sq_ in range(4):
                    tok0 = 4096 + 256 * sq_
                    for qb in range(2):
                        vt0 = tok0 // 128
                        units.append(dict(kind="ctx", hp=hp, q0=tok0 + qb * 128, tok0=tok0, nkc=2,
                                          vch=[vh[hp][:, vt0 + c, :] for c in range(2)], vbufs=[B_vh[hp]]))
                n = len(units)
                g0 = gi[0]
                for i in range(n + 3):
                    if i < n:
                        stA(units[i], g0 + i)
                    if 0 <= i - 1 < n:
                        stB(units[i - 1], g0 + i - 1)
                    if 0 <= i - 2 < n:
                        stC1(units[i - 2], g0 + i - 2)
                    if 0 <= i - 3 < n:
                        stC2(units[i - 3], g0 + i - 3)
                gi[0] += n
                P.dma("pool", lambda e, hp=hp, h=h: e.dma_start(out=MIX[512 + h * 128:512 + (h + 1) * 128, :], in_=at[hp]),
                      d_at[hp], reads=[B_at[hp]], writes=MIXB)
            P.barrier()

        if mode == "nab":
            phase_nab(0)
        elif mode == "C":
            phase_C(0)
        elif mode == "A":
            phase_A_ab(0)
        elif mode == "S5":
            phase_S5(0)
        for l in range(n_layers if big else 0):
            if l % 2 == 1:
                phase_A_conv(l)
            elif with_ab:
                if "A" not in DBG_SKIP:
                    phase_A_ab(l)
                if with_s5:
                    phase_S5(l)
                else:
                    phase_S5_zero(l)
                if "nab" not in DBG_SKIP:
                    phase_nab(l // 2)
                if "C" not in DBG_SKIP:
                    phase_C(l)
            if "D" not in DBG_SKIP:
                phase_D(l)
        P.barrier()
        P.finalize()
        for d in P.dsems:
            d.h = es.enter_context(nc.semaphore("d_" + d.name))
        with nc.Block() as block:
            @block.tensor
            def _(e):
                P.emit("pe", e, sems)

            @block.scalar
            def _(e):
                P.emit("act", e, sems)

            @block.vector
            def _(e):
                P.emit("dve", e, sems)

            @block.gpsimd
            def _(e):
                P.emit("pool", e, sems)

            @block.sync
            def _(e):
                P.emit("sp", e, sems)
    return nc


VOFF = {}
_rows = 0
for _nm, _n in (("cvec", 32), ("ada_b", 4 * 96), ("norm1_g", 64), ("norm2_g", 64), ("conv_w", 96), ("conv_b", 32),
                ("s5_d", 8), ("glu_b", 8), ("q_g", 2), ("k_g", 2)):
    VOFF[_nm] = _rows
    _rows += _n
NVROWS = _rows


def _pack_vecs(inp, core):
    rows = []
    cv = np.stack([inp["c"][core], inp["c_ctx"]], axis=0)
    rows.append(cv.reshape(32, 128))
    rows.append(inp["ada_b"].reshape(4 * 96, 128))
    rows.append(inp["norm1_g"].reshape(64, 128))
    rows.append(inp["norm2_g"].reshape(64, 128))
    rows.append(inp["conv_w"].reshape(96, 128))
    rows.append(inp["conv_b"].reshape(32, 128))
    rows.append(inp["s5_d"].reshape(8, 128))
    rows.append(inp["s5_glu_b"].reshape(8, 128))
    rows.append(inp["q_norm_g"].reshape(2, 128))
    rows.append(inp["k_norm_g"].reshape(2, 128))
    return np.ascontiguousarray(np.concatenate(rows, axis=0).astype(np.float32))


def _pair_types():
    types, tmap = [], []
    for a in range(32):
        w0 = min(max(2 * a - 4, 0), 54)
        key = []
        for rr in range(2):
            r = 2 * a + rr
            r0 = min(max(r - 4, 0), 56)
            key.append((r0 - w0, r0 - r + 7))
        key = tuple(key)
        if key not in types:
            types.append(key)
        tmap.append((types.index(key), w0))
    return types, tmap


PAIR_TYPES, PAIR_MAP = _pair_types()
NTYPES = len(PAIR_TYPES)
S5P_COLS = 16 * 3 + 4 * 16 * 32

_NC_CACHE = {}


def _get_nc(key=(DEPTH, True)):
    if key not in _NC_CACHE:
        _NC_CACHE[key] = build_program(*key)
    return _NC_CACHE[key]


def _lay_gp(a):
    return a.reshape(16, 2, 64).transpose(1, 2, 0).reshape(128, 16)


def _pack_s5(inp):
    out = np.zeros((2, 2, 128, S5P_COLS), np.float32)
    for e in range(2):
        for d in range(2):
            out[e, d, :, 0:16] = _lay_gp(inp["s5_lam_re"][e, d])
            out[e, d, :, 16:32] = _lay_gp(inp["s5_lam_im"][e, d])
            ldt = inp["s5_log_dt"][e, d].reshape(16, 2).T
            out[e, d, :, 32:48] = np.broadcast_to(ldt[:, None, :], (2, 64, 16)).reshape(128, 16)
            col = 48
            for nm, is_c in (("s5_b_re", False), ("s5_b_im", False), ("s5_c_re", True), ("s5_c_im", True)):
                a = inp[nm][e, d]
                blk = np.zeros((2, 64, 16, 2, 16), np.float32)
                for g2 in range(2):
                    if is_c:
                        blk[g2, :, :, g2, :] = a.reshape(16, 2, 16, 64)[:, g2].transpose(2, 0, 1)
                    else:
                        blk[g2, :, :, g2, :] = a.reshape(16, 2, 64, 16)[:, g2].transpose(1, 0, 2)
                out[e, d, :, col:col + 512] = blk.reshape(128, 512)
                col += 512
    return out


def make_in_maps(inp):
    ident = np.eye(128, dtype=np.float32)
    s5p = _pack_s5(inp)
    maps = []
    for i in range(8):
        x_in = np.concatenate([inp["x_sample"][i], inp["x_prompt"][4 * i:4 * i + 4].reshape(1024, D)], axis=0)
        m = {
            "x_in": np.ascontiguousarray(x_in, dtype=np.float32),
            "vecs": _pack_vecs(inp, i),
            "ident": ident,
            "ada_w": inp["ada_w"], "ab_w_in": inp["ab_w_in"], "ab_w_out": inp["ab_w_out"],
            "conv_w_in": inp["conv_w_in"], "conv_w_out": inp["conv_w_out"],
            "mlp_w1": inp["mlp_w1"], "mlp_w2": inp["mlp_w2"],
            "cache_k": np.ascontiguousarray(inp["cache_k"][i]), "cache_v": np.ascontiguousarray(inp["cache_v"][i]),
            "na_rpb": inp["na_rpb"], "glu_w": inp["s5_glu_w"], "s5p": s5p,
            "s5h0": np.ascontiguousarray(np.stack([
                np.stack([np.stack([_lay_gp(inp["state_ssm_re"][i, e, d]), _lay_gp(inp["state_ssm_im"][i, e, d])], 0)
                          for d in range(2)], 0) for e in range(2)], 0), dtype=np.float32),
        }
        maps.append(m)
    return maps


def kernel(**inp):
    inp = {k: np.asarray(v) for k, v in inp.items()}
    nc = _get_nc()
    res = run_bass_kernel_spmd(nc, make_in_maps(inp), core_ids=list(range(8)))
    rs = res.results
    y_s = np.stack([rs[i]["y"][:4096] for i in range(8)], axis=0)
    y_p = np.concatenate([rs[i]["y"][4096:].reshape(4, 256, D) for i in range(8)], axis=0)
    nk = np.concatenate([rs[i]["nck"] for i in range(8)], axis=0)
    nv = np.concatenate([rs[i]["ncv"] for i in range(8)], axis=0)
    sr = np.concatenate([rs[i]["nsr"] for i in range(8)], axis=0)
    si = np.concatenate([rs[i]["nsi"] for i in range(8)], axis=0)
    return (y_p.astype(np.float32), y_s.astype(np.float32), nk.astype(np.float32), nv.astype(np.float32),
            sr.astype(np.float32), si.astype(np.float32))
```
